# Optimizing a Trainium2 kernel written in Bass

```python
import jax, jax.numpy as jnp
from jax import lax
import numpy as np

D_MODEL = 2048
BATCH = 2
SEQ = 8192
DEPTH = 1

PLE_DIM = 256
A_HEADS = 16
A_KV_GROUPS = 4
A_REP = A_HEADS // A_KV_GROUPS
HEAD_DIM = 64
A_WIDTH = A_HEADS * HEAD_DIM
A_KV_WIDTH = A_KV_GROUPS * HEAD_DIM
CMP_LEN = 32
CMP_STRIDE = 16
CMP_RATIO = CMP_LEN // CMP_STRIDE
SEL_BLOCK = 64
SEL_TOP_N = 16
WINDOW = 512
Q_BLOCK = 128
FORCE_BONUS = 1000.0
B_WIDTH = D_MODEL // 2
B_GROUPS = 8
B_GROUP_DIM = B_WIDTH // B_GROUPS
B_CHUNK = 128
N_BRANCHES = 2
NEG = -1e30
EPS = 1e-6

IN_SPLITS = [A_WIDTH,
             6 * A_KV_WIDTH,
             3 * A_HEADS,
             A_WIDTH,
             2 * B_WIDTH,
             B_WIDTH,
             N_BRANCHES * D_MODEL]
IN_WIDTH = int(sum(IN_SPLITS))
SPLIT_IDX = [int(c) for c in np.cumsum(IN_SPLITS)[:-1]]

kernel_name = "hybrid_nsa_gmlp_gated_block"


def rms_norm(x, g):
    xf = x.astype(jnp.float32)
    y = xf * lax.rsqrt(jnp.mean(xf * xf, axis=-1, keepdims=True) + EPS)
    return (y * g.astype(jnp.float32)).astype(x.dtype)


def layer_norm(x, g, b):
    xf = x.astype(jnp.float32)
    mu = jnp.mean(xf, axis=-1, keepdims=True)
    xc = xf - mu
    y = xc * lax.rsqrt(jnp.mean(xc * xc, axis=-1, keepdims=True) + EPS)
    return (y * g.astype(jnp.float32) + b.astype(jnp.float32)).astype(x.dtype)


def alibi_slopes():
    h = np.arange(1, A_HEADS + 1, dtype=np.float32)
    s = np.power(np.float32(2.0), -8.0 * h / A_HEADS).astype(np.float32)
    return jnp.asarray(s.reshape(A_KV_GROUPS, A_REP))


def compress(kv, pe, w1, w2):
    b, s, g, d = kv.shape
    ch = kv.reshape(b, s // CMP_STRIDE, CMP_STRIDE, g, d)
    nc = s // CMP_STRIDE - CMP_RATIO + 1
    blocks = jnp.concatenate([ch[:, r:nc + r] for r in range(CMP_RATIO)], axis=2)
    blocks = blocks + pe[None, None, :, None, :]
    hid = jax.nn.silu(jnp.einsum('bnlgd,lde->bnge', blocks, w1))
    return jnp.einsum('bnge,ef->bngf', hid, w2)


def nsa_attention(q, k_c, v_c, k_s, v_s, k_w, v_w, gate_logits,
                  pe_k, w1_k, w2_k, pe_v, w1_v, w2_v):
    b, s = q.shape[:2]
    q = q.reshape(b, s, A_KV_GROUPS, A_REP, HEAD_DIM)
    kc = compress(k_c, pe_k, w1_k, w2_k)
    vc = compress(v_c, pe_v, w1_v, w2_v)
    nc = kc.shape[1]
    c_end = jnp.arange(nc, dtype=jnp.int32) * CMP_STRIDE + (CMP_LEN - 1)
    ns = s // SEL_BLOCK
    n_top = min(SEL_TOP_N, ns)
    ci = np.arange(nc)[:, None]
    sj = np.arange(ns)[None, :]
    overlap = jnp.asarray(((CMP_STRIDE * ci < SEL_BLOCK * sj + SEL_BLOCK) &
                           (CMP_STRIDE * ci + CMP_LEN > SEL_BLOCK * sj)).astype(np.float32))
    kb = k_s.reshape(b, ns, SEL_BLOCK, A_KV_GROUPS, HEAD_DIM).transpose(0, 3, 1, 2, 4)
    vb = v_s.reshape(b, ns, SEL_BLOCK, A_KV_GROUPS, HEAD_DIM).transpose(0, 3, 1, 2, 4)
    kw = jnp.pad(k_w, ((0, 0), (WINDOW, 0), (0, 0), (0, 0)))
    vw = jnp.pad(v_w, ((0, 0), (WINDOW, 0), (0, 0), (0, 0)))
    slopes = alibi_slopes()
    gates = jax.nn.sigmoid(gate_logits.astype(jnp.float32)).reshape(b, s, 3, A_KV_GROUPS, A_REP)
    scale = HEAD_DIM ** -0.5
    gather = jax.vmap(jax.vmap(lambda blk, ix: blk[ix]))
    sel_off = jnp.arange(SEL_BLOCK, dtype=jnp.int32)
    win_off = jnp.arange(WINDOW + Q_BLOCK, dtype=jnp.int32)
    blk_ids = jnp.arange(ns, dtype=jnp.int32)

    def block_fn(i):
        q0 = i * Q_BLOCK
        qb = lax.dynamic_slice_in_dim(q, q0, Q_BLOCK, axis=1)
        gb = lax.dynamic_slice_in_dim(gates, q0, Q_BLOCK, axis=1)
        t = q0 + jnp.arange(Q_BLOCK, dtype=jnp.int32)
        dc = t[:, None] - c_end[None, :]
        vc_mask = dc >= 0
        sc = jnp.einsum('bqgrd,bngd->bgrqn', qb, kc).astype(jnp.float32) * scale
        sc = jnp.where(vc_mask, sc - slopes[:, :, None, None] * dc.astype(jnp.float32), NEG)
        p_c = jax.nn.softmax(sc, axis=-1) * vc_mask
        o_c = jnp.einsum('bgrqn,bngd->bqgrd', p_c.astype(vc.dtype), vc)
        imp = jnp.einsum('bgrqn,nj->bgqj', p_c, overlap)
        cur = t // SEL_BLOCK
        forced = ((blk_ids[None, :] == 0) | (blk_ids[None, :] == cur[:, None]) |
                  (blk_ids[None, :] == cur[:, None] - 1))
        causal = blk_ids[None, :] * SEL_BLOCK <= t[:, None]
        score = jnp.where(causal, imp + jnp.where(forced, FORCE_BONUS, 0.0), -1.0)
        _, idx = lax.top_k(score, n_top)
        ks = gather(kb, idx)
        vs = gather(vb, idx)
        pos = idx[..., None] * SEL_BLOCK + sel_off
        ds = (t[None, None, :, None, None] - pos)[:, :, None]
        ss = jnp.einsum('bqgrd,bgqnkd->bgrqnk', qb, ks).astype(jnp.float32) * scale
        ss = jnp.where(ds >= 0, ss - slopes[None, :, :, None, None, None] * ds.astype(jnp.float32), NEG)
        p_s = jax.nn.softmax(ss.reshape(b, A_KV_GROUPS, A_REP, Q_BLOCK, n_top * SEL_BLOCK), axis=-1)
        p_s = p_s.reshape(b, A_KV_GROUPS, A_REP, Q_BLOCK, n_top, SEL_BLOCK)
        o_s = jnp.einsum('bgrqnk,bgqnkd->bqgrd', p_s.astype(vs.dtype), vs)
        kwb = lax.dynamic_slice_in_dim(kw, q0, WINDOW + Q_BLOCK, axis=1)
        vwb = lax.dynamic_slice_in_dim(vw, q0, WINDOW + Q_BLOCK, axis=1)
        kp = q0 - WINDOW + win_off
        dw = t[:, None] - kp[None, :]
        vw_mask = (dw >= 0) & (dw < WINDOW) & (kp[None, :] >= 0)
        sw = jnp.einsum('bqgrd,bkgd->bgrqk', qb, kwb).astype(jnp.float32) * scale
        sw = jnp.where(vw_mask, sw - slopes[:, :, None, None] * dw.astype(jnp.float32), NEG)
        p_w = jax.nn.softmax(sw, axis=-1)
        o_w = jnp.einsum('bgrqk,bkgd->bqgrd', p_w.astype(vwb.dtype), vwb)
        o = (gb[:, :, 0, :, :, None] * o_c + gb[:, :, 1, :, :, None] * o_s +
             gb[:, :, 2, :, :, None] * o_w)
        return o.astype(q.dtype)

    out = lax.map(block_fn, jnp.arange(s // Q_BLOCK, dtype=jnp.int32))
    return out.transpose(1, 0, 2, 3, 4, 5).reshape(b, s, A_WIDTH)


def spatial_gating(uv, ln_g, ln_b, w_s, b_s):
    uv = jax.nn.gelu(uv)
    u, v = jnp.split(uv, 2, axis=-1)
    v = layer_norm(v, ln_g, ln_b)
    b, s, _ = v.shape
    v = v.reshape(b, s // B_CHUNK, B_CHUNK, B_GROUPS, B_GROUP_DIM)
    tril = jnp.tril(jnp.ones((B_CHUNK, B_CHUNK), dtype=w_s.dtype))
    sv = jnp.einsum('gts,bcsgd->bctgd', w_s * tril, v) + b_s.T[None, None, :, :, None]
    return u * sv.reshape(b, s, B_WIDTH)


def setup_inputs(seed: int = 0) -> dict:
    key = jax.random.key(seed)
    ks = jax.random.split(key, 24)
    f32 = jnp.float32
    nrm = lambda k, shape, sc: jax.random.normal(k, shape, f32) * sc
    L = DEPTH
    return {
        "x": nrm(ks[0], (BATCH, SEQ, D_MODEL), 1.0),
        "p": nrm(ks[1], (DEPTH, BATCH, SEQ, PLE_DIM), 1.0),
        "norm_g": 1.0 + nrm(ks[2], (L, D_MODEL), 0.05),
        "w_in": nrm(ks[3], (L, D_MODEL, IN_WIDTH), D_MODEL ** -0.5),
        "cmp_pe_k": nrm(ks[4], (L, CMP_LEN, HEAD_DIM), 0.1),
        "cmp_w1_k": nrm(ks[5], (L, CMP_LEN, HEAD_DIM, HEAD_DIM), (CMP_LEN * HEAD_DIM) ** -0.5),
        "cmp_w2_k": nrm(ks[6], (L, HEAD_DIM, HEAD_DIM), HEAD_DIM ** -0.5),
        "cmp_pe_v": nrm(ks[7], (L, CMP_LEN, HEAD_DIM), 0.1),
        "cmp_w1_v": nrm(ks[8], (L, CMP_LEN, HEAD_DIM, HEAD_DIM), (CMP_LEN * HEAD_DIM) ** -0.5),
        "cmp_w2_v": nrm(ks[9], (L, HEAD_DIM, HEAD_DIM), HEAD_DIM ** -0.5),
        "ln_v_g": 1.0 + nrm(ks[10], (L, B_WIDTH), 0.05),
        "ln_v_b": nrm(ks[11], (L, B_WIDTH), 0.02),
        "sgu_w": nrm(ks[12], (L, B_GROUPS, B_CHUNK, B_CHUNK), B_CHUNK ** -0.5),
        "sgu_b": 1.0 + nrm(ks[13], (L, B_GROUPS, B_CHUNK), 0.05),
        "w_up_a": nrm(ks[14], (L, A_WIDTH, D_MODEL), A_WIDTH ** -0.5),
        "w_up_b": nrm(ks[15], (L, B_WIDTH, D_MODEL), B_WIDTH ** -0.5),
        "w_out": nrm(ks[16], (L, D_MODEL, D_MODEL), D_MODEL ** -0.5),
        "w_ple": nrm(ks[17], (L, PLE_DIM, D_MODEL), PLE_DIM ** -0.5),
        "w_ple_gate": nrm(ks[18], (L, D_MODEL, D_MODEL), D_MODEL ** -0.5),
        "final_g": 1.0 + nrm(ks[19], (D_MODEL,), 0.05),
    }


def reference(x, p, norm_g, w_in, cmp_pe_k, cmp_w1_k, cmp_w2_k, cmp_pe_v, cmp_w1_v, cmp_w2_v,
              ln_v_g, ln_v_b, sgu_w, sgu_b, w_up_a, w_up_b, w_out, w_ple, w_ple_gate, final_g):
    b, s, _ = x.shape
    for l in range(DEPTH):
        h = rms_norm(x, norm_g[l])
        proj = h @ w_in[l]
        q, kv, g_logits, z_a, uv, z_b, merge_logits = jnp.split(proj, SPLIT_IDX, axis=-1)
        k_c, v_c, k_s, v_s, k_w, v_w = [t.reshape(b, s, A_KV_GROUPS, HEAD_DIM)
                                        for t in jnp.split(kv, 6, axis=-1)]
        o_a = nsa_attention(q, k_c, v_c, k_s, v_s, k_w, v_w, g_logits,
                            cmp_pe_k[l], cmp_w1_k[l], cmp_w2_k[l],
                            cmp_pe_v[l], cmp_w1_v[l], cmp_w2_v[l]) * jax.nn.silu(z_a)
        o_b = spatial_gating(uv, ln_v_g[l], ln_v_b[l], sgu_w[l], sgu_b[l]) * jax.nn.silu(z_b)
        g_a, g_b = jnp.split(jax.nn.sigmoid(merge_logits), 2, axis=-1)
        merged = g_a * (o_a @ w_up_a[l]) + g_b * (o_b @ w_up_b[l])
        x = x + merged @ w_out[l]
        x = x + jax.nn.sigmoid(x @ w_ple_gate[l]) * (p[l] @ w_ple[l])
    return rms_norm(x, final_g)
```

```python
import os
import numpy as np
from contextlib import ExitStack
import ml_dtypes
import concourse.bass as bass
import concourse.mybir as mybir
from concourse.bass_utils import run_bass_kernel_spmd

F32 = mybir.dt.float32
BF16 = mybir.dt.bfloat16
AF = mybir.ActivationFunctionType
ALU = mybir.AluOpType

S = 8192
D = 2048
NT = 64
NOWN = 16
NEGM = -30000.0
EPS = 1e-6
CW = 512
SKIP = os.environ.get('KDBG_SKIP', '')


class Res:
    __slots__ = ("name", "w", "r")

    def __init__(self, name):
        self.name = name
        self.w = None
        self.r = []


class Op:
    __slots__ = ("eng", "fn", "deps", "inc", "sem", "val", "dma")


class Prog:
    ENGS = ["pe", "act", "dve", "pool", "sp"]

    def __init__(self, nc, stack):
        self.nc = nc
        self.stack = stack
        self.ops = {e: [] for e in self.ENGS}
        self.dma_sems = {}
        self.nsem = 0
        self.final = []

    def newsem(self, name):
        self.nsem += 1
        return self.stack.enter_context(self.nc.semaphore(f"{name}{self.nsem}"))

    def _add(self, eng, fn, reads, writes, dma_key=None):
        op = Op()
        op.eng = eng
        op.fn = fn
        op.inc = False
        op.sem = None
        op.val = 0
        op.dma = dma_key
        writes = list(writes) + [r for r in reads if r.name.startswith("bank") and r not in writes]
        deps = []
        for r in reads:
            if r.w is not None:
                deps.append(r.w)
        for w in writes:
            if w.w is not None:
                deps.append(w.w)
            deps.extend(w.r)
        op.deps = [d for d in deps
                   if not (d.eng == "pe" and eng == "pe" and d.dma is None and dma_key is None)]
        for r in reads:
            r.r.append(op)
        for w in writes:
            w.w = op
            w.r = []
        self.ops[eng].append(op)
        return op

    def g(self, eng, name, reads, writes, *args, **kw):
        return self._add(eng, lambda e: getattr(e, name)(*args, **kw), reads, writes)

    def mm(self, out, lhsT, rhs, start, stop, reads, writes):
        return self._add("pe", lambda e: e.matmul(out, lhsT=lhsT, rhs=rhs, start=start, stop=stop),
                         reads, writes)

    def tr(self, out, in_, ident, reads, writes):
        return self._add("pe", lambda e: e.transpose(out=out, in_=in_, identity=ident), reads, writes)

    def act(self, out, in_, func, reads, writes, eng="act", **kw):
        return self._add(eng, lambda e: e.activation(out=out, in_=in_, func=func, **kw), reads, writes)

    def dma(self, eng, out, in_, reads=(), writes=(), key=None):
        if key is None:
            key = (writes[0].name if writes else reads[0].name)
        return self._add(eng, lambda e: e.dma_start(out=out, in_=in_), reads, writes, dma_key=key)

    def barrier(self):
        deps = []
        for e in self.ENGS:
            for o in reversed(self.ops[e]):
                if o.fn is not None and o.dma is None:
                    deps.append(o)
                    break
        deps += getattr(self, "dma_pending", [])
        self.dma_pending = []
        for e in self.ENGS:
            op = Op()
            op.eng = e
            op.fn = None
            op.inc = False
            op.sem = None
            op.val = 0
            op.dma = None
            op.deps = list(deps)
            self.ops[e].append(op)

    def finalize(self):
        allops = [o for e in self.ENGS for o in self.ops[e]]
        for o in allops:
            for d in o.deps:
                d.inc = True
        fin = [r.w for r in self.final if r.w is not None]
        for o in fin:
            o.inc = True
        MAXV = 30000
        order = {}
        for e in self.ENGS:
            sem = None
            cnt = 0
            for o in self.ops[e]:
                if o.dma is not None:
                    continue
                if o.inc:
                    if sem is None or cnt >= MAXV:
                        sem = self.newsem("c" + e)
                        cnt = 0
                    cnt += 1
                    o.sem, o.val = sem, cnt
        for o in self.dma_order:
            if o.dma not in self.dma_sems:
                self.dma_sems[o.dma] = [self.newsem("d"), 0]
            ent = self.dma_sems[o.dma]
            if ent[1] + 16 > MAXV:
                ent[0] = self.newsem("d")
                ent[1] = 0
            ent[1] += 16
            o.sem, o.val = ent[0], ent[1]
            o.inc = True
        self.fin_ops = fin

    def emit(self):
        nc = self.nc

        def run(ename, eng):
            known = {}
            for o in self.ops[ename]:
                for d in o.deps:
                    k = id(d.sem)
                    if known.get(k, 0) < d.val:
                        eng.wait_ge(d.sem, d.val)
                        known[k] = d.val
                if o.fn is None:
                    continue
                ins = o.fn(eng)
                if o.inc:
                    ins.then_inc(o.sem, 16 if o.dma is not None else 1)
            if ename == "sp":
                for d in self.fin_ops:
                    k = id(d.sem)
                    if known.get(k, 0) < d.val:
                        eng.wait_ge(d.sem, d.val)
                        known[k] = d.val

        with nc.Block() as block:
            @block.tensor
            def _(e):
                run("pe", e)

            @block.scalar
            def _(e):
                run("act", e)

            @block.vector
            def _(e):
                run("dve", e)

            @block.gpsimd
            def _(e):
                run("pool", e)

            @block.sync
            def _(e):
                run("sp", e)


_orig_add = Prog._add


def _add_wrapped(self, eng, fn, reads, writes, dma_key=None):
    op = _orig_add(self, eng, fn, reads, writes, dma_key)
    if dma_key is not None:
        if not hasattr(self, "dma_order"):
            self.dma_order = []
        self.dma_order.append(op)
        if not hasattr(self, "dma_pending"):
            self.dma_pending = []
        self.dma_pending.append(op)
    return op


Prog._add = _add_wrapped


def _split3(a):
    a = np.asarray(a, np.float64)
    hi = a.astype(np.float32).astype(ml_dtypes.bfloat16)
    r1 = a - hi.astype(np.float64)
    mid = r1.astype(np.float32).astype(ml_dtypes.bfloat16)
    r2 = r1 - mid.astype(np.float64)
    lo = r2.astype(np.float32).astype(ml_dtypes.bfloat16)
    return hi, mid, lo


def _common_tables():
    bf = ml_dtypes.bfloat16
    t = {}
    pos = np.arange(S)
    pk = np.zeros((9, S), np.float32)
    pk[0:3] = 1.0
    pk[3:6] = 128.0 * (pos // 128)
    pk[6:9] = pos % 128
    t["posk_s"] = pk.astype(bf)
    n = np.arange(512)
    ce = 16 * n + 31
    pc = np.zeros((9, 512), np.float32)
    pc[0:3] = 1.0
    pc[3:6] = 128.0 * (ce // 128)
    pc[6:9] = ce % 128
    t["posk_c"] = pc.astype(bf)
    ng = (np.arange(4)[None, :, None] * 128 + np.arange(128)[:, None, None])
    jj = np.arange(128)[None, None, :]
    t["ov"] = ((ng >= 4 * jj - 1) & (ng <= 4 * jj + 3)).astype(np.float32).astype(bf)
    t["ind"] = ((np.arange(128)[:, None] % 64) == (np.arange(4096)[None, :] // 64)).astype(np.float32).astype(bf)
    t["identb"] = np.eye(128, dtype=np.float32).astype(bf)
    t["identf"] = np.eye(128, dtype=np.float32)
    t["trilT"] = (np.arange(128)[:, None] <= np.arange(128)[None, :]).astype(np.float32)
    return t


def _core_tables(c):
    bf = ml_dtypes.bfloat16
    t = {}
    h = np.arange(16)
    slopes = np.power(2.0, -8.0 * (h + 1) / 16.0)
    slopes = np.power(np.float32(2.0), (-8.0 * (h + 1).astype(np.float32) / 16)).astype(np.float32).astype(np.float64)
    s_hi, s_mid, s_lo = _split3(slopes)
    qb = np.zeros((NOWN, 9, 16, 128), bf)
    tq = np.arange(128)
    for i in range(NOWN):
        j = 4 * i + c
        tt = 128 * j + tq
        A = -slopes[:, None] * tt[None, :].astype(np.float64)
        a_hi, a_mid, a_lo = _split3(A)
        qb[i, 0], qb[i, 1], qb[i, 2] = a_hi, a_mid, a_lo
        for k, sv in enumerate((s_hi, s_mid, s_lo)):
            qb[i, 3 + k] = np.broadcast_to(sv[:, None], (16, 128))
            qb[i, 6 + k] = np.broadcast_to(sv[:, None], (16, 128))
    t["qb"] = qb.reshape(NOWN, 9, 2048)
    cm = np.zeros((NOWN, 128, 4, 128), np.float32)
    fb2 = np.zeros((NOWN, 128, 128), np.float32)
    caus = np.zeros((NOWN, 128, 128), np.float32)
    blk = np.arange(128)
    for i in range(NOWN):
        j = 4 * i + c
        tt = 128 * j + tq
        ng = np.arange(4)[None, :, None] * 128 + np.arange(128)[:, None, None]
        ok = (16 * ng + 31 <= tt[None, None, :]) & (ng <= 510)
        cm[i] = np.where(ok, 0.0, NEGM)
        cur = tt // 64
        forced = (blk[None, :] == 0) | (blk[None, :] == cur[:, None]) | (blk[None, :] == cur[:, None] - 1)
        cz = blk[None, :] * 64 <= tt[:, None]
        fb2[i] = np.where(cz, np.where(forced, 1000.0, 0.0), -1.0)
        caus[i] = cz.astype(np.float32)
    t["cm"] = cm.astype(bf)
    t["fb2"] = fb2
    t["caus"] = caus
    k = np.arange(128)
    tri = np.zeros((128, 4, 128), np.float32)
    for r in range(4):
        d = 128 * (c - r) + tq[None, :] - k[:, None]
        tri[:, r, :] = np.where(d >= 0, 0.0, NEGM)
    t["tri"] = tri.astype(bf)
    wm = np.zeros((128, 8, 128), np.float32)
    for r in range(8):
        d = 128 * (c + 4 - r) + tq[None, :] - k[:, None]
        wm[:, r, :] = np.where((d >= 0) & (d < 512), 0.0, NEGM)
    t["wm"] = wm.astype(bf)
    return t


def build(debug=()):
    nc = bass.Bass("TRN2", target_bir_lowering=False)
    bf = BF16

    def din(name, shape, dt=F32):
        return nc.dram_tensor(name, list(shape), dt, kind="ExternalInput").ap()

    def dscr(name, shape, dt):
        kind = "ExternalOutput" if name in debug else "Internal"
        return nc.dram_tensor(name, list(shape), dt, kind=kind).ap()

    xb = din("xb", [S, D])
    xo = din("xo", [2048, D])
    po = din("po", [2048, 256])
    norm_g = din("norm_g", [128, D])
    final_g = din("final_g", [128, D])
    w_in = din("w_in", [D, 10800])
    w_own = din("w_own", [D, 9264])
    pe_k = din("pe_k", [64, 32]); w1_k = din("w1_k", [64, 32, 64]); w2_k = din("w2_k", [64, 64])
    pe_v = din("pe_v", [64, 32]); w1_v = din("w1_v", [64, 32, 64]); w2_v = din("w2_v", [64, 64])
    ln_g = din("ln_g", [128, 1024]); ln_b = din("ln_b", [128, 1024])
    sgu_wT = din("sgu_wT", [128, 8, 128])
    sgu_bT = din("sgu_bT", [128, 8])
    w_up_a = din("w_up_a", [1024, D]); w_up_b = din("w_up_b", [1024, D])
    w_out = din("w_out", [D, D]); w_ple = din("w_ple", [256, D]); w_pg = din("w_pg", [D, D])
    t_posk_s = din("posk_s", [9, S], bf); t_posk_c = din("posk_c", [9, 512], bf)
    t_ov = din("ov", [128, 4, 128], bf); t_ind = din("ind", [128, 4096], bf)
    t_identb = din("identb", [128, 128], bf); t_identf = din("identf", [128, 128])
    t_trilT = din("trilT", [128, 128])
    t_qb = din("qb", [NOWN, 9, 2048], bf); t_cm = din("cm", [NOWN, 128, 4, 128], bf)
    t_fb2 = din("fb2", [NOWN, 128, 128]); t_caus = din("caus", [NOWN, 128, 128])
    t_tri = din("tri", [128, 4, 128], bf); t_wm = din("wm", [128, 8, 128], bf)

    out = nc.dram_tensor("out", [2048, D], F32, kind="ExternalOutput").ap()

    KT = dscr("KT", [8, 128, S + 128], bf)
    VS = dscr("VS", [2, 4, 128, NT, 65], bf)
    rKT = Res("KT"); rVS = Res("VS")
    rOUT = Res("OUT")

    with ExitStack() as top:
        P = Prog(nc, top)
        P.final.append(rOUT)
        for nm in debug:
            pass

        def sbt(st, name, shape, dt):
            return st.enter_context(nc.sbuf_tensor("s_" + name, list(shape), dt))

        banks = [top.enter_context(nc.psum_tensor(f"bank{k}", [128, 512], F32)) for k in range(8)]
        rbank = [Res(f"bank{k}") for k in range(8)]
        bstate = {"k": 0}

        def nb():
            k = bstate["k"]
            bstate["k"] = (k + 1) % 5
            return k

        def bankbf(k):
            return banks[k][:].bitcast(bf)

        identb = sbt(top, "identb", [128, 128], bf); r_identb = Res("identb")
        identf = sbt(top, "identf", [128, 128], F32); r_identf = Res("identf")
        P.dma("sp", identb[:], t_identb, writes=[r_identb])
        P.dma("sp", identf[:], t_identf, writes=[r_identf])
        kcT = sbt(top, "kcA", [128, 4, 512], bf); r_kcT = Res("kcA")
        vcA = sbt(top, "vcA", [128, 4, 4, 65], bf); r_vcA = Res("vcA")

        evac_rr = {"k": 0}

        def evac(out_ap, in_ap, reads, writes, eng=None, func=AF.Copy, **kw):
            if eng is None:
                eng = "act" if (evac_rr["k"] % 2 == 0) else "dve"
                evac_rr["k"] += 1
            if eng == "act" or func != AF.Copy or kw:
                return P.act(out_ap, in_ap, func, reads, writes, **kw)
            return P.g("dve", "tensor_copy", reads, writes, out=out_ap, in_=in_ap)

        def make_norm(st, gsrc, tag, nx=2, junk_=None, xn_=None):
            gt = sbt(st, "gt" + tag, [128, D], F32); r_gt = Res("gt" + tag)
            P.dma("sp", gt[:], gsrc, writes=[r_gt])
            xts = [sbt(st, f"xt{tag}{k}", [128, D], F32) for k in range(nx)]
            r_xts = [Res(f"xt{tag}{k}") for k in range(nx)]
            junk, r_junk = junk_ if junk_ else (sbt(st, "junk" + tag, [128, D], bf), Res("junk" + tag))
            xn, r_xn = xn_ if xn_ else (sbt(st, "xn" + tag, [128, D], bf), Res("xn" + tag))
            stat = sbt(st, "stat" + tag, [128, 4], F32); r_stat = Res("stat" + tag)
            return dict(gt=gt, r_gt=r_gt, xts=xts, r_xts=r_xts, junk=junk, r_junk=r_junk, xn=xn,
                        r_xn=r_xn, stat=stat, r_stat=r_stat)

        def norm_load(N, k, src):
            P.dma("sp", N["xts"][k][:], src, writes=[N["r_xts"][k]])

        def norm_compute(N, k, dst_fn, r_dst):
            xt = N["xts"][k]; rxt = N["r_xts"][k]
            stat = N["stat"]; rs = N["r_stat"]
            P.act(N["junk"][:], xt[:], AF.Square, [rxt], [N["r_junk"], rs], accum_out=stat[:, 0:1])
            P.act(stat[:, 1:2], stat[:, 0:1], AF.Ln, [rs], [rs], scale=1.0 / D, bias=EPS)
            P.act(stat[:, 2:3], stat[:, 1:2], AF.Exp, [rs], [rs], scale=-0.5)
            P.g("dve", "scalar_tensor_tensor", [rxt, rs, N["r_gt"]], [N["r_xn"]], out=N["xn"][:], in0=xt[:],
                scalar=stat[:, 2:3], in1=N["gt"][:], op0=ALU.mult, op1=ALU.mult)
            for half in range(2):
                b = nb()
                bb = bankbf(b)
                for cc in range(8):
                    ck = half * 8 + cc
                    P.tr(bb[:, cc * 128:(cc + 1) * 128], N["xn"][:, ck * 128:(ck + 1) * 128], identb[:],
                         [N["r_xn"], r_identb], [rbank[b]])
                evac(dst_fn(half), bb.rearrange("p (c t) -> p c t", c=8), [rbank[b]], [r_dst])

        WT = {}
        r_WT = Res("WTscr")
        wcol = {"q": 0, "gl": 1024, "za": 1072, "uv": 2096, "zb": 4144, "mg": 5168}
        wlist = [("upa", w_up_a, 0, 2048, 8), ("upb", w_up_b, 0, 2048, 8), ("out", w_out, 0, 2048, 16),
                 ("pg", w_pg, 0, 2048, 16), ("ple", w_ple, 0, 2048, 2)]
        def conv_units(st):
            cst = [sbt(st, f"cst{k}", [128, 2048], F32) for k in range(4)]; r_cst = [Res(f"cst{k}") for k in range(4)]
            cbf = [sbt(st, f"cbf{k}", [128, 2048], bf) for k in range(4)]; r_cbf = [Res(f"cbf{k}") for k in range(4)]
            cc_ = 0
            for (nm, src, col0, ncol, KC) in wlist:
                scr = WT[nm]
                for kc in range(KC):
                    for c0 in range(0, ncol, 2048):
                        w_ = min(2048, ncol - c0)
                        k_ = cc_ % 4
                        cc_ += 1
                        P.dma("sp", cst[k_][:, 0:w_], src[kc * 128:(kc + 1) * 128, col0 + c0:col0 + c0 + w_], writes=[r_cst[k_]])
                        if cc_ % 2 == 0:
                            P.act(cbf[k_][:, 0:w_], cst[k_][:, 0:w_], AF.Copy, [r_cst[k_]], [r_cbf[k_]])
                        else:
                            P.g("dve", "tensor_copy", [r_cst[k_]], [r_cbf[k_]], out=cbf[k_][:, 0:w_], in_=cst[k_][:, 0:w_])

                        def store(scr=scr, c0=c0, w_=w_, kc=kc, k_=k_):
                            if w_ % CW == 0:
                                P.dma("sp", scr[c0 // CW:(c0 + w_) // CW, :, kc, :].rearrange("c p w -> p c w"),
                                      cbf[k_][:, 0:w_].rearrange("p (c w) -> p c w", w=CW), reads=[r_cbf[k_]], writes=[r_WT], key="WTw")
                            else:
                                P.dma("sp", scr[c0 // CW, :, kc, 0:w_], cbf[k_][:, 0:w_], reads=[r_cbf[k_]], writes=[r_WT], key="WTw")
                        yield store

        for (nm, src, col0, ncol, KC) in wlist:
            WT[nm] = dscr("wb_" + nm, [(ncol + CW - 1) // CW, 128, KC, CW], bf)
        with ExitStack() as st:
            N = make_norm(st, norm_g, "a")
            wkv = sbt(st, "wkv", [128, 16, 1536], bf); r_wkv = Res("wkv")
            wst = [sbt(st, f"wsta{k}", [128, 1536], F32) for k in range(2)]
            r_wst = [Res(f"wsta{k}") for k in range(2)]
            for ck in range(16):
                P.dma("sp", wst[ck % 2][:], w_in[ck * 128:(ck + 1) * 128, 1024:2560], writes=[r_wst[ck % 2]])
                P.g("pool", "tensor_copy", [r_wst[ck % 2]], [r_wkv], out=wkv[:, ck, :], in_=wst[ck % 2][:])
            hT = [sbt(st, f"hTa{k}", [128, 16, 128], bf) for k in range(2)]
            r_hT = [Res(f"hTa{k}") for k in range(2)]
            kvtok = sbt(st, "kvtok", [128, 1024], bf); r_kvtok = Res("kvtok")
            KTst = [sbt(st, f"KTst{k}", [128, 8, 512], bf) for k in range(2)]
            r_KTst = [Res(f"KTst{k}") for k in range(2)]
            Vst = [sbt(st, f"Vst{k}", [128, 2, 4, 4, 65], bf) for k in range(2)]
            r_Vst = [Res(f"Vst{k}") for k in range(2)]
            for k in range(2):
                P.g("pool", "memset", [], [r_Vst[k]], Vst[k][:], 1.0)
            cgen = conv_units(st)
            cpend = []

            def conv_step(n):
                while cpend:
                    cpend.pop(0)()
                for _ in range(n):
                    try:
                        cpend.append(next(cgen))
                    except StopIteration:
                        break
            norm_load(N, 0, xb[0:128, :])
            for T in range(int(os.environ.get('KDBG_NT', NT))):
                conv_step(1)
                grp, tt = T // 4, T % 4
                sk = grp % 2
                if T + 1 < NT:
                    norm_load(N, (T + 1) % 2, xb[(T + 1) * 128:(T + 2) * 128, :])
                hk = T % 2
                if 'n' not in SKIP:
                  norm_compute(N, T % 2, lambda half, hk=hk: hT[hk][:, half * 8:(half + 1) * 8, :], r_hT[hk])
                if 'k' in SKIP:
                    continue
                bks = [nb() for _ in range(3)]
                for cb in range(3):
                    for ck in range(16):
                        if 'm' in SKIP:
                            break
                        P.mm(banks[bks[cb]][:, :], hT[hk][:, ck, :], wkv[:, ck, cb * 512:(cb + 1) * 512],
                             ck == 0, ck == 15, [r_hT[hk], r_wkv], [rbank[bks[cb]]])
                if 'e' in SKIP:
                    continue
                EV = os.environ.get('KDBG_EV', '12345')
                if '1' in EV:
                    evac(kvtok[:, 0:512], banks[bks[0]][:, :], [rbank[bks[0]]], [r_kvtok], eng="act")
                if '2' in EV:
                    evac(kvtok[:, 512:768], banks[bks[1]][:, 0:256], [rbank[bks[1]]], [r_kvtok], eng="dve")
                if '3' in EV:
                    evac(Vst[sk][:, 0, tt, :, 0:64], banks[bks[1]][:, 256:512].rearrange("p (g e) -> p g e", g=4),
                         [rbank[bks[1]]], [r_Vst[sk]], eng="dve")
                if '4' in EV:
                    evac(kvtok[:, 768:1024], banks[bks[2]][:, 0:256], [rbank[bks[2]]], [r_kvtok], eng="act")
                if '5' in EV:
                    evac(Vst[sk][:, 1, tt, :, 0:64], banks[bks[2]][:, 256:512].rearrange("p (g e) -> p g e", g=4),
                         [rbank[bks[2]]], [r_Vst[sk]], eng="dve")
                if 't' in SKIP:
                    continue
                b = nb()
                bb = bankbf(b)
                for m in range(8):
                    P.tr(bb[:, m * 128:(m + 1) * 128], kvtok[:, m * 128:(m + 1) * 128], identb[:],
                         [r_kvtok, r_identb], [rbank[b]])
                evac(KTst[sk][:, :, tt * 128:(tt + 1) * 128], bb.rearrange("p (m t) -> p m t", m=8),
                     [rbank[b]], [r_KTst[sk]])
                if tt == 3 and 's' not in SKIP:
                    P.dma("sp", KT.rearrange("m p t -> p m t")[:, :, grp * 512:(grp + 1) * 512], KTst[sk][:],
                          reads=[r_KTst[sk]], writes=[rKT], key="KTw")
                    for y in range(2):
                        for gq in range(4):
                            P.dma("sp", VS[y, gq][:, grp * 4:(grp + 1) * 4, :],
                                  Vst[sk][:, y, :, gq, :], reads=[r_Vst[sk]], writes=[rVS], key="VSw")

            for _ in range(400):
                conv_step(3)
            conv_step(0)

        P.barrier()
        with ExitStack() as st:
          if not os.environ.get('KDBG_NOCMP'):
              w1b = [sbt(st, f"w1b{y}", [128, 32, 128], bf) for y in range(2)]
              w2b = [sbt(st, f"w2b{y}", [128, 128], bf) for y in range(2)]
              peT = [sbt(st, f"peT{y}", [128, 32], bf) for y in range(2)]
              r_cw = Res("cmpw")
              for y, (w1, w2, pe) in enumerate(((w1_k, w2_k, pe_k), (w1_v, w2_v, pe_v))):
                  P.g("pool", "memset", [], [r_cw], w1b[y][:], 0.0)
                  P.g("pool", "memset", [], [r_cw], w2b[y][:], 0.0)
                  for hf in range(2):
                      ps_ = slice(hf * 64, hf * 64 + 64)
                      P.dma("pool", w1b[y][ps_, :, hf * 64:hf * 64 + 64], w1,
                            writes=[r_cw], key="cmpw")
                      P.dma("pool", w2b[y][ps_, hf * 64:hf * 64 + 64], w2, writes=[r_cw], key="cmpw")
                      P.dma("pool", peT[y][ps_, :], pe, writes=[r_cw], key="cmpw")
              cbias = sbt(st, "cbias", [128, 2], F32); r_cb = Res("cbias")
              for y in range(2):
                  b = nb()
                  for l in range(32):
                      P.mm(banks[b][:, 0:1], w1b[y][:, l, :], peT[y][:, l:l + 1], l == 0, l == 31, [r_cw], [rbank[b]])
                  evac(cbias[:, y:y + 1], banks[b][:, 0:1], [rbank[b]], [r_cb], eng="dve")
              P.g("pool", "memset", [], [r_kcT], kcT[:], 0.0)
              for g_ in range(4):
                  pp = slice(64, 73) if g_ % 2 == 0 else slice(0, 9)
                  P.dma("sp", kcT[pp, g_, :], t_posk_c, writes=[r_kcT], key="kcApos")
              P.g("pool", "memset", [], [r_vcA], vcA[:], 0.0)
              P.g("pool", "memset", [], [r_vcA], vcA[:, :, :, 64:65], 1.0)
              kin = [sbt(st, f"kin{k}", [128, S + 128], bf) for k in range(2)]
              r_kin = [Res(f"kin{k}") for k in range(2)]
              hid = sbt(st, "hid", [128, 128], bf); r_hid = Res("hid")
              cnt = 0
              for y in range(2):
                  for pr in range(2):
                      kb = cnt % 2
                      cnt += 1
                      P.dma("sp", kin[kb][:, 0:S], KT[y * 2 + pr][:, 0:S], reads=[rKT], writes=[r_kin[kb]])
                      for nt in range(4):
                          nn = 128 if nt < 3 else 127
                          b = nb()
                          for l in range(32):
                              base = 16 * 128 * nt + l
                              rhs = kin[kb][:, base:base + 16 * (nn - 1) + 1:16]
                              P.mm(banks[b][:, 0:nn], w1b[y][:, l, :], rhs, l == 0, l == 31,
                                   [r_cw, r_kin[kb]], [rbank[b]])
                          if nn < 128:
                              P.g("dve", "memset", [], [r_hid], hid[:], 0.0)
                          P.act(hid[:, 0:nn], banks[b][:, 0:nn], AF.Silu, [rbank[b], r_cb], [r_hid],
                                bias=cbias[:, y:y + 1])
                          b2 = nb()
                          if y == 0:
                              P.mm(banks[b2][:, 0:128], w2b[0][:], hid[:], True, True, [r_cw, r_hid], [rbank[b2]])
                              evac(kcT[0:64, 2 * pr, nt * 128:(nt + 1) * 128], banks[b2][0:64, 0:128], [rbank[b2]], [r_kcT],
                                   eng="dve")
                              evac(kcT[64:128, 2 * pr + 1, nt * 128:(nt + 1) * 128], banks[b2][64:128, 0:128], [rbank[b2]], [r_kcT],
                                   eng="dve")
                          else:
                              P.mm(banks[b2][:, 0:128], hid[:], w2b[1][:], True, True, [r_cw, r_hid], [rbank[b2]])
                              evac(vcA[:, nt, pr * 2:pr * 2 + 2, 0:64],
                                   banks[b2][:, 0:128].rearrange("p (g e) -> p g e", g=2), [rbank[b2]], [r_vcA],
                                   eng="dve")
              if "dbg_kc" in debug:
                  dk = nc.dram_tensor("dbg_kc", [128, 4, 512], bf, kind="ExternalOutput").ap()
                  dv = nc.dram_tensor("dbg_vc", [128, 4, 4, 65], bf, kind="ExternalOutput").ap()
                  rd = Res("dbgkc")
                  P.dma("sp", dk, kcT[:], reads=[r_kcT], writes=[rd], key="dbg")
                  P.dma("sp", dv, vcA[:], reads=[r_vcA], writes=[rd], key="dbg")
                  P.final.append(rd)

        if "stop1" in debug:
            P.final.extend([rKT, rVS])
            P.finalize()
            P.emit()
            return nc


        P.barrier()
        ACTS = dscr("ACTS", [2048, 9216], bf)
        GATES = dscr("GATES", [2048, 48], F32)
        rACTS = Res("ACTS"); rGATES = Res("GATES")
        acol = {"q": 0, "za": 1024, "uv": 2048, "zb": 4096, "mg": 5120}
        nblk = int(os.environ.get("KDBG_NB", NOWN))
        with ExitStack() as st:
            N = make_norm(st, norm_g, "o", nx=2)
            hTo = sbt(st, "hTo", [128, 16, 2048], bf); r_hTo = Res("hTo")
            norm_load(N, 0, xo[0:128, :])
            for i in range(nblk):
                if i + 1 < nblk:
                    norm_load(N, (i + 1) % 2, xo[(i + 1) * 128:(i + 2) * 128, :])
                norm_compute(N, i % 2, lambda half, i=i: hTo[:, half * 8:(half + 1) * 8, i * 128:(i + 1) * 128], r_hTo)
            wbufa = [sbt(st, f"wbufa{k}", [128, 16, CW], bf) for k in range(2)]; r_wbufa = [Res(f"wbufa{k}") for k in range(2)]
            wf32 = [sbt(st, f"wf32{k}", [128, 16, CW], F32) for k in range(2)]; r_wf32 = [Res(f"wf32{k}") for k in range(2)]
            stg = [sbt(st, f"stg{k}", [128, CW], bf) for k in range(4)]; r_stg = [Res(f"stg{k}") for k in range(4)]
            gstg = [sbt(st, f"gstg{k}", [128, 48], F32) for k in range(2)]; r_gstg = [Res(f"gstg{k}") for k in range(2)]
            ca = {"w": 0, "s": 0}
            funcs = {"q": AF.Copy, "gl": AF.Sigmoid, "za": AF.Silu, "uv": AF.Gelu, "zb": AF.Silu, "mg": AF.Sigmoid}
            for (nm, wd) in (("gl", 48), ("q", 1024), ("za", 1024), ("uv", 2048), ("zb", 1024), ("mg", 4096)):
                for cc in range(0, wd, CW):
                    w_ = min(CW, wd - cc)
                    wk = ca["w"] % 2
                    ca["w"] += 1
                    c_ = wcol[nm] + cc
                    for q4 in range(4):
                        P.dma("sp", wf32[wk][:, q4 * 4:(q4 + 1) * 4, 0:w_],
                              w_own[q4 * 512:(q4 + 1) * 512, c_:c_ + w_].rearrange("(c p) n -> p c n", p=128), writes=[r_wf32[wk]])
                    for h2 in range(2):
                        P.g("dve", "tensor_copy", [r_wf32[wk]], [r_wbufa[wk]], out=wbufa[wk][:, h2 * 8:(h2 + 1) * 8, 0:w_],
                            in_=wf32[wk][:, h2 * 8:(h2 + 1) * 8, 0:w_])
                    for i in range(nblk):
                        b = nb()
                        for ck in range(16):
                            P.mm(banks[b][:, 0:w_], hTo[:, ck, i * 128:(i + 1) * 128], wbufa[wk][:, ck, 0:w_], ck == 0, ck == 15,
                                 [r_hTo, r_wbufa[wk]], [rbank[b]])
                        if nm == "gl":
                            k_ = i % 2
                            P.act(gstg[k_][:], banks[b][:, 0:48], AF.Sigmoid, [rbank[b]], [r_gstg[k_]])
                            P.dma("sp", GATES[i * 128:(i + 1) * 128, :], gstg[k_][:], reads=[r_gstg[k_]], writes=[rGATES], key="GATESw")
                        else:
                            k_ = ca["s"] % 4
                            ca["s"] += 1
                            if nm == "q":
                                P.act(stg[k_][:, 0:w_], banks[b][:, 0:w_], AF.Copy, [rbank[b]], [r_stg[k_]], scale=0.125)
                            else:
                                P.act(stg[k_][:, 0:w_], banks[b][:, 0:w_], funcs[nm], [rbank[b]], [r_stg[k_]])
                            P.dma("sp", ACTS[i * 128:(i + 1) * 128, acol[nm] + cc:acol[nm] + cc + w_], stg[k_][:, 0:w_],
                                  reads=[r_stg[k_]], writes=[rACTS], key="ACTSw")

        OAB = dscr("OAB", [2048, 2048], bf); rOAB = Res("OAB")
        ssum = sbt(top, "ssum", [128, 64], F32); r_ssum = Res("ssum")
        P.barrier()
        with ExitStack() as st:
            b1 = sbt(st, "b1", [128, 2048], bf); r_b1 = Res("b1")
            oab = sbt(st, "oab", [128, 2048], bf); r_oab = Res("oab")
            N = None
            lng = sbt(st, "lng", [128, 1024], F32); lnb = sbt(st, "lnb", [128, 1024], F32); r_ln = Res("ln")
            P.dma("sp", lng[:], ln_g, writes=[r_ln], key="lnc"); P.dma("sp", lnb[:], ln_b, writes=[r_ln], key="lnc")
            ind = sbt(st, "ind", [128, 4096], bf)
            ov = sbt(st, "ov", [128, 4, 128], bf); tri = sbt(st, "tri", [128, 4, 128], bf); wm = sbt(st, "wm", [128, 8, 128], bf)
            r_tab = Res("tab")
            for dst, src in ((ind, t_ind), (ov, t_ov), (tri, t_tri), (wm, t_wm)):
                P.dma("sp", dst[:], src, writes=[r_tab], key="tab")
            f1 = sbt(st, "f1", [128, 2048], F32); r_f1 = Res("f1")
            wsf = f1[:, 0:1024].rearrange("p (a b) -> p a b", a=8); trl = sbt(st, "trl", [128, 128], F32)
            wsb = sbt(st, "wsb", [128, 8, 128], bf); bsT = sbt(st, "bsT", [128, 8], F32); r_sg = Res("sgw")
            P.dma("sp", wsf, sgu_wT, writes=[r_sg, r_f1], key="sgw"); P.dma("sp", trl[:], t_trilT, writes=[r_sg], key="sgw")
            P.dma("sp", bsT[:], sgu_bT, writes=[r_sg], key="sgw")
            P.g("dve", "tensor_tensor", [r_sg, r_f1], [r_sg], out=wsb[:], in0=wsf,
                in1=trl[:].unsqueeze(1).broadcast_to([128, 8, 128]), op=ALU.mult)

            hTb = sbt(st, "hTb", [128, 16, 128], bf); r_hTb = Res("hTb")
            wbuf = None; r_wbuf = None
            wctr = {"c": 0}

            def dense(lhsT_fn, r_l, KC, wname, c0, width, epi):
                for s0 in range(0, width, CW):
                    dense1(lhsT_fn, r_l, KC, wname, c0 + s0, min(CW, width - s0), lambda b, w, s0=s0: epi(b, w, s0))

            def dense1(lhsT_fn, r_l, KC, wname, c0, width, epi):
                wk = wctr["c"] % 2
                wctr["c"] += 1
                P.dma("sp", wbuf[wk][:, 0:KC, 0:width], WT[wname][c0 // CW][:, :, 0:width], reads=[r_WT], writes=[r_wbuf[wk]])
                b = nb()
                for ck in range(KC):
                    P.mm(banks[b][:, 0:width], lhsT_fn(ck), wbuf[wk][:, ck, 0:width], ck == 0, ck == KC - 1,
                         [r_l, r_wbuf[wk]], [rbank[b]])
                epi(b, width)

            ablk2 = [sbt(st, f"ablk{k}", [128, 5120], bf) for k in range(2)]; r_ablk2 = [Res(f"ablk{k}") for k in range(2)]
            gates2 = [sbt(st, f"gates{k}", [128, 48], F32) for k in range(2)]; r_gates2 = [Res(f"gates{k}") for k in range(2)]
            cur = {}
            QA = sbt(st, "QA", [128, 4, 512], bf); r_QA = Res("QA")
            P.g("pool", "memset", [], [r_QA], QA[:], 0.0)
            cmt = sbt(st, "cmt", [128, 4, 128], bf); r_cmt = Res("cmt")
            fb2 = sbt(st, "fb2", [128, 128], F32); caus = sbt(st, "caus", [128, 128], F32); r_fb = Res("fb")
            PT = [sbt(st, f"PT{k}", [128, 512], bf) for k in range(4)]; r_PT = [Res(f"PT{k}") for k in range(4)]
            PTc = sbt(st, "PTc", [128, 4, 512], bf); r_PTc = [Res(f"PTc{k}") for k in range(4)]
            Osb = [sbt(st, f"Osb{k}", [65, 512], F32) for k in range(2)]; r_Osb = [Res(f"Osb{k}") for k in range(2)]
            oatt = sbt(st, "oatt", [128, 1024], F32); r_oatt = Res("oatt")
            otmp = sbt(st, "otmp", [128, 256], F32); r_otmp = Res("otmp")
            sm = sbt(st, "sm", [128, 32], F32); r_sm = Res("sm")
            imp = sbt(st, "imp", [128, 128], F32); r_imp = Res("imp")
            sc2 = sbt(st, "sc2", [128, 128], F32); r_sc2 = Res("sc2")
            m8 = sbt(st, "m8", [128, 16], F32); r_m8 = Res("m8")
            mbq = sbt(st, "mbq", [128, 128], bf); r_mbq = Res("mbq")
            MBT = sbt(st, "MBT", [128, 2, 128], bf); r_MBT = Res("MBT")
            P.g("pool", "memset", [], [r_MBT], MBT[:], 0.0)
            Kbuf = [sbt(st, f"Kbuf{k}", [128, S], bf) for k in range(2)]; r_Kbuf = [Res(f"Kbuf{k}") for k in range(2)]
            r_Kpos = Res("Kpos")
            for k in range(2):
                P.g("pool", "memset", [], [r_Kpos, r_Kbuf[k]], Kbuf[k][:], 0.0)
            P.dma("sp", Kbuf[0][64:73, :], t_posk_s, writes=[r_Kpos], key="kpos"); P.dma("sp", Kbuf[1][0:9, :], t_posk_s, writes=[r_Kpos], key="kpos")
            Vbuf = [sbt(st, f"Vbuf{k}", [128, NT, 65], bf) for k in range(2)]; r_Vbuf = [Res(f"Vbuf{k}") for k in range(2)]
            r_Kwpos = [Res(f"Kwpos{k}") for k in range(2)]
            HOLD_KW = 1
            Kwb = [sbt(st, f"Kwb{k}", [128, 1024], bf) for k in range(2)]; r_Kwb = [Res(f"Kwb{k}") for k in range(2)]
            for k in range(2):
                P.g("pool", "memset", [], [r_Kwb[k], r_Kwpos[k]], Kwb[k][:], 0.0)
            Vwb = [sbt(st, f"Vwb{k}", [128, 8, 65], bf) for k in range(2)]; r_Vwb = [Res(f"Vwb{k}") for k in range(2)]
            f2 = sbt(st, "f2", [128, 2048], F32); r_f2 = Res("f2")
            oT = hTb; r_oT = r_hTb
            xT = oT; r_xT = r_oT
            ptk = sbt(st, "ptk", [128, 256], F32); pbk = sbt(st, "pbk", [128, 256], bf); r_pt = Res("ptk")
            pT = sbt(st, "pT", [128, 2, 128], bf); r_pT = Res("pT")
            ptc = {"k": 0}

            def transposeN(src, r_src, n, dst, r_dst):
                for h0 in range(0, n, 8):
                    m = min(8, n - h0)
                    b = nb()
                    bb = bankbf(b)
                    for cc in range(m):
                        P.tr(bb[:, cc * 128:(cc + 1) * 128], src[:, (h0 + cc) * 128:(h0 + cc + 1) * 128], identb[:],
                             [r_src, r_identb], [rbank[b]])
                    evac(dst[:, h0:h0 + m, :], bb[:, 0:m * 128].rearrange("p (c t) -> p c t", c=m), [rbank[b]], [r_dst])

            pending = []
            octr = {"k": 0}

            def flush():
                while pending:
                    pending.pop(0)()

            def branch(tiles, Qop, rq, ob, save_pt=False):
                nt_ = len(tiles)
                pts = {}

                def emitS(idx):
                    (kT, rk, extras, vap, rv) = tiles[idx]
                    sbk = nb()
                    Sap = banks[sbk][:, :].rearrange("p (a b) -> p a b", a=4)
                    P.mm(Sap, kT, Qop, True, len(extras) == 0, rk + rq, [rbank[sbk]])
                    for ei, (el, er, rr) in enumerate(extras):
                        P.mm(Sap, el, er, False, ei == len(extras) - 1, [r_tab, r_identb] + rr, [rbank[sbk]])
                    if save_pt:
                        pt_ap, rpt = PTc[:, idx, :], r_PTc[idx]
                    else:
                        k_ = ptc["k"] % 4
                        ptc["k"] += 1
                        pt_ap, rpt = PT[k_][:], r_PT[k_]
                    P.act(pt_ap, banks[sbk][:, :], AF.Exp, [rbank[sbk]], [rpt])
                    pts[idx] = (pt_ap, rpt)

                def emitPV(idx):
                    (kT, rk, extras, vap, rv) = tiles[idx]
                    pt_ap, rpt = pts[idx]
                    P.mm(banks[ob][0:65, :], vap, pt_ap, idx == 0, idx == nt_ - 1, [rv, rpt], [rbank[ob]])

                for idx in range(nt_):
                    emitS(idx)
                    if idx >= 2:
                        emitPV(idx - 2)
                    if idx == min(2, nt_ - 1):
                        flush()
                for idx in range(max(0, nt_ - 2), nt_):
                    emitPV(idx)

            def epi_evac(ob):
                k_ = octr["k"] % 2
                octr["k"] += 1
                evac(Osb[k_][:, :], banks[ob][0:65, :], [rbank[ob]], [r_Osb[k_]], eng="dve")
                return k_

            def epilogue(k_, br, g, first):
                for r in range(4):
                    P.tr(banks[7][:, r * 65:(r + 1) * 65], Osb[k_][0:65, r * 128:(r + 1) * 128], identf[0:65, 0:65],
                         [r_Osb[k_], r_identf], [rbank[7]])
                O3 = banks[7][:, 0:260].rearrange("p (a e) -> p a e", a=4)
                P.g("dve", "tensor_scalar", [rbank[7]], [r_sm], out=sm[:, 0:4], in0=O3[:, :, 64], scalar1=1e-30,
                    scalar2=None, op0=ALU.max)
                P.g("dve", "reciprocal", [r_sm], [r_sm], out=sm[:, 4:8], in_=sm[:, 0:4])
                P.g("dve", "tensor_tensor", [r_sm, cur["r_gates"]], [r_sm], out=sm[:, 8:12], in0=sm[:, 4:8],
                    in1=cur["gates"][:, br * 16 + 4 * g:br * 16 + 4 * g + 4], op=ALU.mult)
                dst = oatt[:, g * 256:(g + 1) * 256].rearrange("p (a e) -> p a e", a=4)
                wb_ = sm[:, 8:12].unsqueeze(2).broadcast_to([128, 4, 64])
                if first:
                    P.g("dve", "tensor_tensor", [rbank[7], r_sm], [r_oatt], out=dst, in0=O3[:, :, 0:64], in1=wb_, op=ALU.mult)
                else:
                    t3 = otmp[:, :].rearrange("p (a e) -> p a e", a=4)
                    P.g("dve", "tensor_tensor", [rbank[7], r_sm], [r_otmp], out=t3, in0=O3[:, :, 0:64], in1=wb_, op=ALU.mult)
                    P.g("dve", "tensor_tensor", [r_otmp, r_oatt], [r_oatt], out=dst, in0=dst, in1=t3, op=ALU.add)

            segs = [(0, 1024, "q"), (0, 48, "gl"), (0, 1024, "za"), (0, 2048, "uv"), (0, 1024, "zb"), (0, 4096, "mg")]
            nblk = int(os.environ.get("KDBG_NB", NOWN))
            for i in range(nblk):
                tok = slice(i * 128, (i + 1) * 128)
                xk = 0
                def load_acts(j):
                    P.dma("sp", ablk2[j % 2][:], ACTS[j * 128:(j + 1) * 128, 0:5120], reads=[rACTS], writes=[r_ablk2[j % 2]])
                    P.dma("sp", gates2[j % 2][:], GATES[j * 128:(j + 1) * 128, :], reads=[rGATES], writes=[r_gates2[j % 2]])
                if i == 0:
                    load_acts(0)
                ablk = ablk2[i % 2]; r_ablk = r_ablk2[i % 2]
                r_qtok = r_za = r_uvg = r_zb = r_ablk
                za = ablk[:, 1024:2048]; zb = ablk[:, 4096:5120]
                cur["gates"] = gates2[i % 2]; cur["r_gates"] = r_gates2[i % 2]
                qb3 = t_qb[i].rearrange("p (g c) -> p g c", g=4)
                for g_ in range(4):
                    pp = slice(64, 73) if g_ % 2 == 0 else slice(0, 9)
                    P.dma("sp", QA[pp, g_, :], qb3[:, g_, :], writes=[r_QA], key="QAqb")
                P.dma("sp", cmt[:], t_cm[i], writes=[r_cmt])
                P.dma("sp", fb2[:], t_fb2[i], writes=[r_fb], key="fb"); P.dma("sp", caus[:], t_caus[i], writes=[r_fb], key="fb")
                bq = nb()
                bbq = bankbf(bq)
                for cc in range(8):
                    P.tr(bbq[:, cc * 128:(cc + 1) * 128], ablk[:, cc * 128:(cc + 1) * 128], identb[:],
                         [r_qtok, r_identb], [rbank[bq]])
                for P2 in range(2):
                    for hf2 in range(2):
                        hs2 = slice(hf2 * 64, hf2 * 64 + 64)
                        evac(QA[hs2, 2 * P2 + hf2, :], bbq[hs2, P2 * 512:(P2 + 1) * 512], [rbank[bq]], [r_QA])
                if i + 1 < nblk:
                    load_acts(i + 1)
                ntc = (32 * i + 31) // 128 + 1
                L = (4 * i + 4) * 128
                T0 = max(0, 4 * i - 4)
                for g in range(4):
                    P_, hf = g // 2, g % 2
                    hs = slice(hf * 64, hf * 64 + 64)
                    kb = g % 2
                    po_ = slice(64, 73) if hf == 0 else slice(0, 9)
                    Qop = QA[:, g, :].rearrange("p (a b) -> p a b", a=4)
                    rq = [r_QA]
                    po_ = slice(64, 73) if hf == 0 else slice(0, 9)
                    P.dma("sp", Kbuf[kb][hs, 0:L], KT[4 + P_][hs, 0:L], reads=[rKT], writes=[r_Kbuf[kb]])
                    P.dma("sp", Vbuf[kb][:, 0:L // 128, :], VS[0, g][:, 0:L // 128, :], reads=[rVS], writes=[r_Vbuf[kb]])
                    P.dma("sp", Kwb[kb][hs, 0:L - T0 * 128], KT[6 + P_][hs, T0 * 128:L], reads=[rKT], writes=[r_Kwb[kb]])
                    P.dma("sp", Kwb[kb][po_, 0:L - T0 * 128], t_posk_s[:, T0 * 128:L], writes=[r_Kwpos[kb]])
                    P.dma("sp", Vwb[kb][:, 0:L // 128 - T0, :], VS[1, g][:, T0:L // 128, :], reads=[rVS], writes=[r_Vwb[kb]])
                    bc4 = lambda ap: ap.unsqueeze(1).broadcast_to([128, 4, 128])
                    tiles = []
                    for nt in range(ntc):
                        tiles.append((kcT[:, g, nt * 128:(nt + 1) * 128], [r_kcT],
                                      [(identb[:], bc4(cmt[:, nt, :]), [r_cmt])], vcA[:, nt, g, :], r_vcA))
                    obC = 5 + (octr["k"] % 2)
                    branch(tiles, Qop, rq, obC, save_pt=True)
                    kC = epi_evac(obC)
                    ub = nb()
                    for r in range(4):
                        for nt in range(ntc):
                            P.mm(banks[ub][:, r * 128:(r + 1) * 128], PTc[:, nt, r * 128:(r + 1) * 128], ov[:, nt, :],
                                 nt == 0, nt == ntc - 1, [r_PTc[nt], r_tab], [rbank[ub]])

                    def after_c(kC=kC, g=g, ub=ub):
                        epilogue(kC, 0, g, True)
                        for r in range(4):
                            if r == 0:
                                P.g("dve", "tensor_scalar", [rbank[ub], r_sm], [r_imp], out=imp[:], in0=banks[ub][:, 0:128],
                                    scalar1=sm[:, 4:5], scalar2=None, op0=ALU.mult)
                            else:
                                P.g("dve", "scalar_tensor_tensor", [rbank[ub], r_sm, r_imp], [r_imp], out=imp[:],
                                    in0=banks[ub][:, r * 128:(r + 1) * 128], scalar=sm[:, 4 + r:5 + r], in1=imp[:],
                                    op0=ALU.mult, op1=ALU.add)
                        P.g("dve", "tensor_tensor", [r_imp, r_fb], [r_imp], out=imp[:], in0=imp[:], in1=caus[:], op=ALU.mult)
                        P.g("dve", "tensor_tensor", [r_imp, r_fb], [r_imp], out=imp[:], in0=imp[:], in1=fb2[:], op=ALU.add)
                        P.g("dve", "max", [r_imp], [r_m8], out=m8[:, 0:8], in_=imp[:])
                        P.g("dve", "match_replace", [r_imp, r_m8], [r_sc2], out=sc2[:], in_to_replace=m8[:, 0:8],
                            in_values=imp[:], imm_value=-2.0)
                        P.g("dve", "max", [r_sc2], [r_m8], out=m8[:, 8:16], in_=sc2[:])
                        P.g("dve", "tensor_scalar", [r_m8], [r_m8], out=m8[:, 0:1], in0=m8[:, 15:16], scalar1=-0.5,
                            scalar2=None, op0=ALU.max)
                        P.g("dve", "tensor_scalar", [r_imp, r_m8], [r_sc2], out=sc2[:], in0=imp[:], scalar1=m8[:, 0:1],
                            scalar2=-NEGM, op0=ALU.is_ge, op1=ALU.mult)
                        P.g("dve", "tensor_scalar", [r_sc2], [r_mbq], out=mbq[:], in0=sc2[:], scalar1=NEGM, scalar2=None,
                            op0=ALU.add)
                    pending.append(after_c)
                    tiles = []
                    for rr_ in range(8):
                        T = 4 * i - 4 + rr_
                        if T < 0:
                            continue
                        tiles.append((Kwb[kb][:, (T - T0) * 128:(T - T0 + 1) * 128], [r_Kwb[kb], r_Kwpos[kb]],
                                      [(identb[:], bc4(wm[:, rr_, :]), [])],
                                      Vwb[kb][:, T - T0, :], r_Vwb[kb]))
                    obW = 5 + (octr["k"] % 2)
                    branch(tiles, Qop, rq, obW)
                    flush()
                    kW = epi_evac(obW)
                    mb_ = nb()
                    P.tr(bankbf(mb_)[:, 0:128], mbq[:], identb[:], [r_mbq, r_identb], [rbank[mb_]])
                    evac(MBT[0:64, 0, :], bankbf(mb_)[0:64, 0:128], [rbank[mb_]], [r_MBT], eng="dve")
                    evac(MBT[64:128, 1, :], bankbf(mb_)[64:128, 0:128], [rbank[mb_]], [r_MBT], eng="dve")
                    pending.append(lambda kW=kW, g=g: epilogue(kW, 2, g, False))
                    tiles = []
                    for kt in range(4 * i + 4):
                        ex = [(ind[:, (kt % 32) * 128:(kt % 32 + 1) * 128],
                               MBT[:, kt // 32, :].unsqueeze(1).broadcast_to([128, 4, 128]), [r_MBT])]
                        if kt >= 4 * i:
                            ex.append((identb[:], bc4(tri[:, kt - 4 * i, :]), []))
                        tiles.append((Kbuf[kb][:, kt * 128:(kt + 1) * 128], [r_Kbuf[kb], r_Kpos],
                                      ex, Vbuf[kb][:, kt, :], r_Vbuf[kb]))
                    obS = 5 + (octr["k"] % 2)
                    branch(tiles, Qop, rq, obS)
                    flush()
                    kS = epi_evac(obS)
                    pending.append(lambda kS=kS, g=g: epilogue(kS, 1, g, False))
                flush()
                P.g("dve", "tensor_tensor", [r_oatt, r_za], [r_oab], out=oab[:, 0:1024], in0=oatt[:], in1=za, op=ALU.mult)
                v_ = ablk[:, 3072:4096]
                P.act(f1[:, 0:1024], v_, AF.Copy, [r_uvg], [r_f1, r_sm], accum_out=sm[:, 16:17])
                P.act(f1[:, 1024:2048], v_, AF.Square, [r_uvg], [r_f1, r_sm], accum_out=sm[:, 17:18])
                P.g("dve", "tensor_scalar", [r_sm], [r_sm], out=sm[:, 18:19], in0=sm[:, 16:17], scalar1=1.0 / 1024,
                    scalar2=None, op0=ALU.mult)
                P.g("dve", "tensor_tensor", [r_sm], [r_sm], out=sm[:, 19:20], in0=sm[:, 18:19], in1=sm[:, 18:19], op=ALU.mult)
                P.g("dve", "scalar_tensor_tensor", [r_sm], [r_sm], out=sm[:, 20:21], in0=sm[:, 17:18], scalar=1.0 / 1024,
                    in1=sm[:, 19:20], op0=ALU.mult, op1=ALU.subtract)
                P.act(sm[:, 21:22], sm[:, 20:21], AF.Ln, [r_sm], [r_sm], bias=EPS)
                P.act(sm[:, 22:23], sm[:, 21:22], AF.Exp, [r_sm], [r_sm], scale=-0.5)
                P.g("dve", "tensor_scalar", [r_uvg, r_sm], [r_f1], out=f1[:, 0:1024], in0=v_, scalar1=sm[:, 18:19],
                    scalar2=sm[:, 22:23], op0=ALU.subtract, op1=ALU.mult)
                P.g("dve", "tensor_tensor", [r_f1, r_ln], [r_f1], out=f1[:, 0:1024], in0=f1[:, 0:1024], in1=lng[:], op=ALU.mult)
                P.g("dve", "tensor_tensor", [r_f1, r_ln], [r_b1], out=b1[:, 0:1024], in0=f1[:, 0:1024], in1=lnb[:], op=ALU.add)
                for half in range(2):
                    b = nb()
                    for gq in range(4):
                        G_ = half * 4 + gq
                        P.mm(banks[b][:, gq * 128:(gq + 1) * 128], wsb[:, G_, :], b1[:, G_ * 128:(G_ + 1) * 128], True, True,
                             [r_sg, r_b1], [rbank[b]])
                    P.g("dve", "tensor_tensor", [rbank[b], r_sg], [r_f2], out=f2[:, half * 512:(half + 1) * 512].rearrange("p (a e) -> p a e", a=4),
                        in0=banks[b][:, :].rearrange("p (a e) -> p a e", a=4),
                        in1=bsT[:, half * 4:(half + 1) * 4].unsqueeze(2).broadcast_to([128, 4, 128]), op=ALU.add)
                P.g("dve", "tensor_tensor", [r_f2, r_uvg], [r_f2], out=f2[:, 0:1024], in0=f2[:, 0:1024], in1=ablk[:, 2048:3072], op=ALU.mult)
                P.g("dve", "tensor_tensor", [r_f2, r_zb], [r_oab], out=oab[:, 1024:2048], in0=f2[:, 0:1024], in1=zb, op=ALU.mult)
                P.dma("sp", OAB[tok, :], oab[:], reads=[r_oab], writes=[rOAB], key="OABw")

        P.barrier()
        X1 = dscr("X1", [2048, D], F32); X2 = dscr("X2", [2048, D], F32)
        rX1 = Res("X1"); rX2 = Res("X2")
        with ExitStack() as st:
            TA = sbt(st, "TA", [128, 16, 2048], bf); r_TA = Res("TA")
            MA = sbt(st, "MA", [128, 16, 2048], bf); r_MA = Res("MA")
            big = [sbt(st, f"big{k}", [128, 16, 512], bf) for k in range(2)]; r_big = [Res(f"big{k}") for k in range(2)]
            wsm = [sbt(st, f"wsm{k}", [128, 8, 512], bf) for k in range(2)]; r_wsm = [Res(f"wsm{k}") for k in range(2)]
            plew = sbt(st, "plew", [128, 2, 512], bf); r_plew = Res("plew")
            tA = [sbt(st, f"tA{k}", [128, 512], F32) for k in range(2)]; r_tA = [Res(f"tA{k}") for k in range(2)]
            tB = [sbt(st, f"tB{k}", [128, 512], F32) for k in range(2)]; r_tB = [Res(f"tB{k}") for k in range(2)]
            tX = [sbt(st, f"tX{k}", [128, 512], F32) for k in range(2)]; r_tX = [Res(f"tX{k}") for k in range(2)]
            tY = [sbt(st, f"tY{k}", [128, 512], F32) for k in range(2)]; r_tY = [Res(f"tY{k}") for k in range(2)]
            pT3 = wsm[0][:].rearrange("p a b -> p (a b)").rearrange("p (c t) -> p c t", c=2); r_pT3 = r_wsm[0]
            ptk3 = sbt(st, "ptk3", [128, 256], F32); pbk3 = sbt(st, "pbk3", [128, 256], bf); r_p3 = Res("p3")
            jk3 = sbt(st, "jk3", [128, 512], bf); r_jk3 = Res("jk3")

            def tr_tiles(src_fn, r_src, i):
                for half in range(2):
                    b = nb()
                    bb = bankbf(b)
                    for c8 in range(8):
                        P.tr(bb[:, c8 * 128:(c8 + 1) * 128], src_fn(half * 8 + c8), identb[:], [r_src, r_identb], [rbank[b]])
                    evac(TA[:, half * 8:(half + 1) * 8, i * 128:(i + 1) * 128], bb.rearrange("p (c t) -> p c t", c=8),
                         [rbank[b]], [r_TA])

            for i in range(nblk):
                k = i % 2
                ldv = big[k][:, 0:4, :]
                P.dma("sp", ldv, OAB[i * 128:(i + 1) * 128, :].rearrange("p (a b) -> p a b", a=4), reads=[rOAB], writes=[r_big[k]])
                tr_tiles(lambda c, k=k: big[k][:, c // 4, (c % 4) * 128:(c % 4 + 1) * 128], r_big[k], i)
            acts3 = ACTS.rearrange("(t p) c -> p t c", p=128)
            for cc in range(4):
                P.dma("sp", big[0][:, 0:nblk, :], acts3[:, 0:nblk, 5120 + cc * 512:5120 + (cc + 1) * 512], reads=[rACTS], writes=[r_big[0]])
                P.dma("sp", big[1][:, 0:nblk, :], acts3[:, 0:nblk, 7168 + cc * 512:7168 + (cc + 1) * 512], reads=[rACTS], writes=[r_big[1]])
                P.dma("sp", wsm[0][:], WT["upa"][cc], reads=[r_WT], writes=[r_wsm[0]])
                P.dma("sp", wsm[1][:], WT["upb"][cc], reads=[r_WT], writes=[r_wsm[1]])
                for i in range(nblk):
                    k = i % 2
                    bA = nb()
                    for ck in range(8):
                        P.mm(banks[bA][:, :], TA[:, ck, i * 128:(i + 1) * 128], wsm[0][:, ck, :], ck == 0, ck == 7, [r_TA, r_wsm[0]], [rbank[bA]])
                    bB = nb()
                    for ck in range(8):
                        P.mm(banks[bB][:, :], TA[:, 8 + ck, i * 128:(i + 1) * 128], wsm[1][:, ck, :], ck == 0, ck == 7, [r_TA, r_wsm[1]], [rbank[bB]])
                    P.g("dve", "tensor_tensor", [rbank[bA], r_big[0]], [r_tA[k]], out=tA[k][:], in0=banks[bA][:, :], in1=big[0][:, i, :], op=ALU.mult)
                    P.g("dve", "tensor_tensor", [rbank[bB], r_big[1]], [r_tB[k]], out=tB[k][:], in0=banks[bB][:, :], in1=big[1][:, i, :], op=ALU.mult)
                    P.g("dve", "tensor_tensor", [r_tA[k], r_tB[k]], [r_MA], out=MA[:, i, cc * 512:(cc + 1) * 512], in0=tA[k][:], in1=tB[k][:], op=ALU.add)
            for i in range(nblk):
                tr_tiles(lambda c, i=i: MA[:, i, c * 128:(c + 1) * 128], r_MA, i)
            for cc in range(4):
                wk = cc % 2
                P.dma("sp", big[wk][:], WT["out"][cc], reads=[r_WT], writes=[r_big[wk]])
                for i in range(nblk):
                    k = i % 2
                    b = nb()
                    for ck in range(16):
                        P.mm(banks[b][:, :], TA[:, ck, i * 128:(i + 1) * 128], big[wk][:, ck, :], ck == 0, ck == 15, [r_TA, r_big[wk]], [rbank[b]])
                    P.dma("sp", tX[k][:], xo[i * 128:(i + 1) * 128, cc * 512:(cc + 1) * 512], writes=[r_tX[k]])
                    P.g("dve", "tensor_tensor", [rbank[b], r_tX[k]], [r_tY[k]], out=tY[k][:], in0=banks[b][:, :], in1=tX[k][:], op=ALU.add)
                    P.dma("sp", X1[i * 128:(i + 1) * 128, cc * 512:(cc + 1) * 512], tY[k][:], reads=[r_tY[k]], writes=[rX1], key="X1w")
                    P.act(MA[:, i, cc * 512:(cc + 1) * 512], tY[k][:], AF.Copy, [r_tY[k]], [r_MA])
            for i in range(nblk):
                tr_tiles(lambda c, i=i: MA[:, i, c * 128:(c + 1) * 128], r_MA, i)
            for i in range(nblk):
                P.dma("sp", ptk3[:], po[i * 128:(i + 1) * 128, :], writes=[r_p3])
                P.act(pbk3[:], ptk3[:], AF.Copy, [r_p3], [r_p3])
                b = nb()
                bb = bankbf(b)
                for c2 in range(2):
                    P.tr(bb[:, c2 * 128:(c2 + 1) * 128], pbk3[:, c2 * 128:(c2 + 1) * 128], identb[:], [r_p3, r_identb], [rbank[b]])
                evac(pT3[:, :, i * 128:(i + 1) * 128], bb[:, 0:256].rearrange("p (c t) -> p c t", c=2), [rbank[b]], [r_pT3])
            for cc in range(4):
                wk = cc % 2
                P.dma("sp", big[wk][:], WT["pg"][cc], reads=[r_WT], writes=[r_big[wk]])
                P.dma("sp", plew[:], WT["ple"][cc], reads=[r_WT], writes=[r_plew])
                for i in range(nblk):
                    k = i % 2
                    bG = nb()
                    for ck in range(16):
                        P.mm(banks[bG][:, :], TA[:, ck, i * 128:(i + 1) * 128], big[wk][:, ck, :], ck == 0, ck == 15, [r_TA, r_big[wk]], [rbank[bG]])
                    P.act(tA[k][:], banks[bG][:, :], AF.Sigmoid, [rbank[bG]], [r_tA[k]])
                    bP = nb()
                    for ck in range(2):
                        P.mm(banks[bP][:, :], pT3[:, ck, i * 128:(i + 1) * 128], plew[:, ck, :], ck == 0, ck == 1, [r_pT3, r_plew], [rbank[bP]])
                    P.dma("sp", tX[k][:], X1[i * 128:(i + 1) * 128, cc * 512:(cc + 1) * 512], reads=[rX1], writes=[r_tX[k]])
                    P.g("dve", "tensor_tensor", [rbank[bP], r_tA[k]], [r_tB[k]], out=tB[k][:], in0=banks[bP][:, :], in1=tA[k][:], op=ALU.mult)
                    P.g("dve", "tensor_tensor", [r_tB[k], r_tX[k]], [r_tY[k]], out=tY[k][:], in0=tB[k][:], in1=tX[k][:], op=ALU.add)
                    P.act(jk3[:], tY[k][:], AF.Square, [r_tY[k]], [r_jk3, r_ssum], accum_out=ssum[:, i * 4 + cc:i * 4 + cc + 1])
                    P.dma("sp", X2[i * 128:(i + 1) * 128, cc * 512:(cc + 1) * 512], tY[k][:], reads=[r_tY[k]], writes=[rX2], key="X2w")
        P.barrier()
        with ExitStack() as st:
            fgt = sbt(st, "fgt", [128, D], F32); r_fgt = Res("fgt")
            P.dma("sp", fgt[:], final_g, writes=[r_fgt])
            xt3 = [sbt(st, f"xt3{k}", [128, D], F32) for k in range(2)]; r_xt3 = [Res(f"xt3{k}") for k in range(2)]
            ot3 = [sbt(st, f"ot3{k}", [128, D], F32) for k in range(2)]; r_ot3 = [Res(f"ot3{k}") for k in range(2)]
            st3 = sbt(st, "st3", [128, 8], F32); r_st3 = Res("st3")
            for i in range(nblk):
                k = i % 2
                P.dma("sp", xt3[k][:], X2[i * 128:(i + 1) * 128, :], reads=[rX2], writes=[r_xt3[k]])
                P.g("dve", "tensor_tensor", [r_ssum], [r_st3], out=st3[:, 0:2], in0=ssum[:, i * 4:i * 4 + 2], in1=ssum[:, i * 4 + 2:i * 4 + 4], op=ALU.add)
                P.g("dve", "tensor_tensor", [r_st3], [r_st3], out=st3[:, 2:3], in0=st3[:, 0:1], in1=st3[:, 1:2], op=ALU.add)
                P.act(st3[:, 3:4], st3[:, 2:3], AF.Ln, [r_st3], [r_st3], scale=1.0 / D, bias=EPS)
                P.act(st3[:, 4:5], st3[:, 3:4], AF.Exp, [r_st3], [r_st3], scale=-0.5)
                P.g("dve", "scalar_tensor_tensor", [r_xt3[k], r_st3, r_fgt], [r_ot3[k]], out=ot3[k][:], in0=xt3[k][:], scalar=st3[:, 4:5],
                    in1=fgt[:], op0=ALU.mult, op1=ALU.mult)
                P.dma("sp", out[i * 128:(i + 1) * 128, :], ot3[k][:], reads=[r_ot3[k]], writes=[rOUT], key="outw")
        P.finalize()
        P.emit()
    return nc


def _make_w_own(w):
    qcols = []
    for P_ in range(2):
        for r in range(4):
            for hd in (8 * P_ + r, 8 * P_ + 4 + r):
                qcols.extend(range(hd * 64, hd * 64 + 64))
    return np.ascontiguousarray(np.concatenate([w[:, qcols], w[:, 2560:]], axis=1))


def _prep_inputs(inputs):
    f = lambda a: np.ascontiguousarray(np.asarray(a, dtype=np.float32))
    x = f(inputs["x"]); p = f(inputs["p"])[0]
    com = _common_tables()
    shared = {
        "norm_g": np.ascontiguousarray(np.broadcast_to(f(inputs["norm_g"])[0][None, :], (128, D))),
        "final_g": np.ascontiguousarray(np.broadcast_to(f(inputs["final_g"])[None, :], (128, D))),
        "w_in": f(inputs["w_in"])[0],
        "w_own": _make_w_own(f(inputs["w_in"])[0]),
        "pe_k": np.ascontiguousarray(f(inputs["cmp_pe_k"])[0].T), "w1_k": np.ascontiguousarray(f(inputs["cmp_w1_k"])[0].transpose(1, 0, 2)), "w2_k": f(inputs["cmp_w2_k"])[0],
        "pe_v": np.ascontiguousarray(f(inputs["cmp_pe_v"])[0].T), "w1_v": np.ascontiguousarray(f(inputs["cmp_w1_v"])[0].transpose(1, 0, 2)), "w2_v": f(inputs["cmp_w2_v"])[0],
        "ln_g": np.ascontiguousarray(np.broadcast_to(f(inputs["ln_v_g"])[0][None, :], (128, 1024))),
        "ln_b": np.ascontiguousarray(np.broadcast_to(f(inputs["ln_v_b"])[0][None, :], (128, 1024))),
        "sgu_wT": np.ascontiguousarray(f(inputs["sgu_w"])[0].transpose(2, 0, 1)),
        "sgu_bT": np.ascontiguousarray(f(inputs["sgu_b"])[0].T),
        "w_up_a": f(inputs["w_up_a"])[0], "w_up_b": f(inputs["w_up_b"])[0],
        "w_out": f(inputs["w_out"])[0], "w_ple": f(inputs["w_ple"])[0], "w_pg": f(inputs["w_ple_gate"])[0],
    }
    shared.update(com)
    in_maps = []
    for core in range(8):
        b, c = core // 4, core % 4
        m = dict(shared)
        m["xb"] = x[b]
        xr = x[b].reshape(16, 4, 128, D)[:, c].reshape(2048, D)
        m["xo"] = np.ascontiguousarray(xr)
        m["po"] = np.ascontiguousarray(p[b].reshape(16, 4, 128, 256)[:, c].reshape(2048, 256))
        m.update(_core_tables(c))
        in_maps.append(m)
    return in_maps


def kernel(**inputs):
    in_maps = _prep_inputs(inputs)
    nc = build()
    res = run_bass_kernel_spmd(nc, in_maps, core_ids=list(range(8)))
    outp = np.zeros((2, S, D), np.float32)
    o = outp.reshape(2, 16, 4, 128, D)
    for core in range(8):
        b, c = core // 4, core % 4
        o[b, :, c] = res.results[core]["out"].reshape(16, 128, D)
    return outp
```

```python
import os
import numpy as np
from contextlib import ExitStack
import ml_dtypes
import concourse.bass as bass
import concourse.mybir as mybir
from concourse.bass_utils import run_bass_kernel_spmd

F32 = mybir.dt.float32
BF16 = mybir.dt.bfloat16
AF = mybir.ActivationFunctionType
ALU = mybir.AluOpType

S = 8192
D = 2048
NT = 64
NOWN = 16
NEGM = -30000.0
EPS = 1e-6
CW = 512
SKIP = os.environ.get('KDBG_SKIP', '')


class Res:
    __slots__ = ("name", "w", "r")

    def __init__(self, name):
        self.name = name
        self.w = None
        self.r = []


class Op:
    __slots__ = ("eng", "fn", "deps", "inc", "sem", "val", "dma")


class Prog:
    ENGS = ["pe", "act", "dve", "pool", "sp"]

    def __init__(self, nc, stack):
        self.nc = nc
        self.stack = stack
        self.ops = {e: [] for e in self.ENGS}
        self.dma_sems = {}
        self.nsem = 0
        self.final = []

    def newsem(self, name):
        self.nsem += 1
        return self.stack.enter_context(self.nc.semaphore(f"{name}{self.nsem}"))

    def _add(self, eng, fn, reads, writes, dma_key=None):
        op = Op()
        op.eng = eng
        op.fn = fn
        op.inc = False
        op.sem = None
        op.val = 0
        op.dma = dma_key
        writes = list(writes) + [r for r in reads if r.name.startswith("bank") and r not in writes]
        deps = []
        for r in reads:
            if r.w is not None:
                deps.append(r.w)
        for w in writes:
            if w.w is not None:
                deps.append(w.w)
            deps.extend(w.r)
        op.deps = [d for d in deps
                   if not (d.eng == "pe" and eng == "pe" and d.dma is None and dma_key is None)]
        for r in reads:
            r.r.append(op)
        for w in writes:
            w.w = op
            w.r = []
        self.ops[eng].append(op)
        return op

    def g(self, eng, name, reads, writes, *args, **kw):
        return self._add(eng, lambda e: getattr(e, name)(*args, **kw), reads, writes)

    def mm(self, out, lhsT, rhs, start, stop, reads, writes):
        return self._add("pe", lambda e: e.matmul(out, lhsT=lhsT, rhs=rhs, start=start, stop=stop),
                         reads, writes)

    def tr(self, out, in_, ident, reads, writes):
        return self._add("pe", lambda e: e.transpose(out=out, in_=in_, identity=ident), reads, writes)

    def act(self, out, in_, func, reads, writes, eng="act", **kw):
        return self._add(eng, lambda e: e.activation(out=out, in_=in_, func=func, **kw), reads, writes)

    def dma(self, eng, out, in_, reads=(), writes=(), key=None):
        if key is None:
            key = (writes[0].name if writes else reads[0].name)
        return self._add(eng, lambda e: e.dma_start(out=out, in_=in_), reads, writes, dma_key=key)

    def barrier(self):
        deps = []
        for e in self.ENGS:
            for o in reversed(self.ops[e]):
                if o.fn is not None and o.dma is None:
                    deps.append(o)
                    break
        deps += getattr(self, "dma_pending", [])
        self.dma_pending = []
        for e in self.ENGS:
            op = Op()
            op.eng = e
            op.fn = None
            op.inc = False
            op.sem = None
            op.val = 0
            op.dma = None
            op.deps = list(deps)
            self.ops[e].append(op)

    def finalize(self):
        allops = [o for e in self.ENGS for o in self.ops[e]]
        for o in allops:
            for d in o.deps:
                d.inc = True
        fin = [r.w for r in self.final if r.w is not None]
        for o in fin:
            o.inc = True
        MAXV = 30000
        order = {}
        for e in self.ENGS:
            sem = None
            cnt = 0
            for o in self.ops[e]:
                if o.dma is not None:
                    continue
                if o.inc:
                    if sem is None or cnt >= MAXV:
                        sem = self.newsem("c" + e)
                        cnt = 0
                    cnt += 1
                    o.sem, o.val = sem, cnt
        for o in self.dma_order:
            if o.dma not in self.dma_sems:
                self.dma_sems[o.dma] = [self.newsem("d"), 0]
            ent = self.dma_sems[o.dma]
            if ent[1] + 16 > MAXV:
                ent[0] = self.newsem("d")
                ent[1] = 0
            ent[1] += 16
            o.sem, o.val = ent[0], ent[1]
            o.inc = True
        self.fin_ops = fin

    def emit(self):
        nc = self.nc

        def run(ename, eng):
            known = {}
            for o in self.ops[ename]:
                for d in o.deps:
                    k = id(d.sem)
                    if known.get(k, 0) < d.val:
                        eng.wait_ge(d.sem, d.val)
                        known[k] = d.val
                if o.fn is None:
                    continue
                ins = o.fn(eng)
                if o.inc:
                    ins.then_inc(o.sem, 16 if o.dma is not None else 1)
            if ename == "sp":
                for d in self.fin_ops:
                    k = id(d.sem)
                    if known.get(k, 0) < d.val:
                        eng.wait_ge(d.sem, d.val)
                        known[k] = d.val

        with nc.Block() as block:
            @block.tensor
            def _(e):
                run("pe", e)

            @block.scalar
            def _(e):
                run("act", e)

            @block.vector
            def _(e):
                run("dve", e)

            @block.gpsimd
            def _(e):
                run("pool", e)

            @block.sync
            def _(e):
                run("sp", e)


_orig_add = Prog._add


def _add_wrapped(self, eng, fn, reads, writes, dma_key=None):
    op = _orig_add(self, eng, fn, reads, writes, dma_key)
    if dma_key is not None:
        if not hasattr(self, "dma_order"):
            self.dma_order = []
        self.dma_order.append(op)
        if not hasattr(self, "dma_pending"):
            self.dma_pending = []
        self.dma_pending.append(op)
    return op


Prog._add = _add_wrapped


def _split3(a):
    a = np.asarray(a, np.float64)
    hi = a.astype(np.float32).astype(ml_dtypes.bfloat16)
    r1 = a - hi.astype(np.float64)
    mid = r1.astype(np.float32).astype(ml_dtypes.bfloat16)
    r2 = r1 - mid.astype(np.float64)
    lo = r2.astype(np.float32).astype(ml_dtypes.bfloat16)
    return hi, mid, lo


def _common_tables():
    bf = ml_dtypes.bfloat16
    t = {}
    pos = np.arange(S)
    pk = np.zeros((9, S), np.float32)
    pk[0:3] = 1.0
    pk[3:6] = 128.0 * (pos // 128)
    pk[6:9] = pos % 128
    t["posk_s"] = pk.astype(bf)
    n = np.arange(512)
    ce = 16 * n + 31
    pc = np.zeros((9, 512), np.float32)
    pc[0:3] = 1.0
    pc[3:6] = 128.0 * (ce // 128)
    pc[6:9] = ce % 128
    t["posk_c"] = pc.astype(bf)
    ng = (np.arange(4)[None, :, None] * 128 + np.arange(128)[:, None, None])
    jj = np.arange(128)[None, None, :]
    t["ov"] = ((ng >= 4 * jj - 1) & (ng <= 4 * jj + 3)).astype(np.float32).astype(bf)
    t["ind"] = ((np.arange(128)[:, None] % 64) == (np.arange(4096)[None, :] // 64)).astype(np.float32).astype(bf)
    t["identb"] = np.eye(128, dtype=np.float32).astype(bf)
    t["identf"] = np.eye(128, dtype=np.float32)
    t["trilT"] = (np.arange(128)[:, None] <= np.arange(128)[None, :]).astype(np.float32)
    return t


def _core_tables(c):
    bf = ml_dtypes.bfloat16
    t = {}
    h = np.arange(16)
    slopes = np.power(2.0, -8.0 * (h + 1) / 16.0)
    slopes = np.power(np.float32(2.0), (-8.0 * (h + 1).astype(np.float32) / 16)).astype(np.float32).astype(np.float64)
    s_hi, s_mid, s_lo = _split3(slopes)
    qb = np.zeros((NOWN, 9, 16, 128), bf)
    tq = np.arange(128)
    for i in range(NOWN):
        j = 4 * i + c
        tt = 128 * j + tq
        A = -slopes[:, None] * tt[None, :].astype(np.float64)
        a_hi, a_mid, a_lo = _split3(A)
        qb[i, 0], qb[i, 1], qb[i, 2] = a_hi, a_mid, a_lo
        for k, sv in enumerate((s_hi, s_mid, s_lo)):
            qb[i, 3 + k] = np.broadcast_to(sv[:, None], (16, 128))
            qb[i, 6 + k] = np.broadcast_to(sv[:, None], (16, 128))
    t["qb"] = qb.reshape(NOWN, 9, 2048)
    cm = np.zeros((NOWN, 128, 4, 128), np.float32)
    fb2 = np.zeros((NOWN, 128, 128), np.float32)
    caus = np.zeros((NOWN, 128, 128), np.float32)
    blk = np.arange(128)
    for i in range(NOWN):
        j = 4 * i + c
        tt = 128 * j + tq
        ng = np.arange(4)[None, :, None] * 128 + np.arange(128)[:, None, None]
        ok = (16 * ng + 31 <= tt[None, None, :]) & (ng <= 510)
        cm[i] = np.where(ok, 0.0, NEGM)
        cur = tt // 64
        forced = (blk[None, :] == 0) | (blk[None, :] == cur[:, None]) | (blk[None, :] == cur[:, None] - 1)
        cz = blk[None, :] * 64 <= tt[:, None]
        fb2[i] = np.where(cz, np.where(forced, 1000.0, 0.0), -1.0)
        caus[i] = cz.astype(np.float32)
    t["cm"] = cm.astype(bf)
    t["fb2"] = fb2
    t["caus"] = caus
    k = np.arange(128)
    tri = np.zeros((128, 4, 128), np.float32)
    for r in range(4):
        d = 128 * (c - r) + tq[None, :] - k[:, None]
        tri[:, r, :] = np.where(d >= 0, 0.0, NEGM)
    t["tri"] = tri.astype(bf)
    wm = np.zeros((128, 8, 128), np.float32)
    for r in range(8):
        d = 128 * (c + 4 - r) + tq[None, :] - k[:, None]
        wm[:, r, :] = np.where((d >= 0) & (d < 512), 0.0, NEGM)
    t["wm"] = wm.astype(bf)
    return t


def build(debug=()):
    nc = bass.Bass("TRN2", target_bir_lowering=False)
    bf = BF16

    def din(name, shape, dt=F32):
        return nc.dram_tensor(name, list(shape), dt, kind="ExternalInput").ap()

    def dscr(name, shape, dt):
        kind = "ExternalOutput" if name in debug else "Internal"
        return nc.dram_tensor(name, list(shape), dt, kind=kind).ap()

    xb = din("xb", [S, D])
    xo = din("xo", [2048, D])
    po = din("po", [2048, 256])
    norm_g = din("norm_g", [128, D])
    final_g = din("final_g", [128, D])
    w_in = din("w_in", [D, 10800])
    w_own = din("w_own", [D, 9264])
    pe_k = din("pe_k", [64, 32]); w1_k = din("w1_k", [64, 32, 64]); w2_k = din("w2_k", [64, 64])
    pe_v = din("pe_v", [64, 32]); w1_v = din("w1_v", [64, 32, 64]); w2_v = din("w2_v", [64, 64])
    ln_g = din("ln_g", [128, 1024]); ln_b = din("ln_b", [128, 1024])
    sgu_wT = din("sgu_wT", [128, 8, 128])
    sgu_bT = din("sgu_bT", [128, 8])
    w_up_a = din("w_up_a", [1024, D]); w_up_b = din("w_up_b", [1024, D])
    w_out = din("w_out", [D, D]); w_ple = din("w_ple", [256, D]); w_pg = din("w_pg", [D, D])
    t_posk_s = din("posk_s", [9, S], bf); t_posk_c = din("posk_c", [9, 512], bf)
    t_ov = din("ov", [128, 4, 128], bf); t_ind = din("ind", [128, 4096], bf)
    t_identb = din("identb", [128, 128], bf); t_identf = din("identf", [128, 128])
    t_trilT = din("trilT", [128, 128])
    t_qb = din("qb", [NOWN, 9, 2048], bf); t_cm = din("cm", [NOWN, 128, 4, 128], bf)
    t_fb2 = din("fb2", [NOWN, 128, 128]); t_caus = din("caus", [NOWN, 128, 128])
    t_tri = din("tri", [128, 4, 128], bf); t_wm = din("wm", [128, 8, 128], bf)

    out = nc.dram_tensor("out", [2048, D], F32, kind="ExternalOutput").ap()

    KT = dscr("KT", [8, 128, S + 128], bf)
    VS = dscr("VS", [2, 4, 128, NT, 65], bf)
    rKT = Res("KT"); rVS = Res("VS")
    rOUT = Res("OUT")

    with ExitStack() as top:
        P = Prog(nc, top)
        P.final.append(rOUT)
        for nm in debug:
            pass

        def sbt(st, name, shape, dt):
            return st.enter_context(nc.sbuf_tensor("s_" + name, list(shape), dt))

        banks = [top.enter_context(nc.psum_tensor(f"bank{k}", [128, 512], F32)) for k in range(8)]
        rbank = [Res(f"bank{k}") for k in range(8)]
        bstate = {"k": 0}

        def nb():
            k = bstate["k"]
            bstate["k"] = (k + 1) % 5
            return k

        def bankbf(k):
            return banks[k][:].bitcast(bf)

        identb = sbt(top, "identb", [128, 128], bf); r_identb = Res("identb")
        identf = sbt(top, "identf", [128, 128], F32); r_identf = Res("identf")
        P.dma("sp", identb[:], t_identb, writes=[r_identb])
        P.dma("sp", identf[:], t_identf, writes=[r_identf])
        kcT = sbt(top, "kcA", [128, 4, 512], bf); r_kcT = Res("kcA")
        vcA = sbt(top, "vcA", [128, 4, 4, 65], bf); r_vcA = Res("vcA")

        evac_rr = {"k": 0}

        def evac(out_ap, in_ap, reads, writes, eng=None, func=AF.Copy, **kw):
            if eng is None:
                eng = "act" if (evac_rr["k"] % 2 == 0) else "dve"
                evac_rr["k"] += 1
            if eng == "act" or func != AF.Copy or kw:
                return P.act(out_ap, in_ap, func, reads, writes, **kw)
            return P.g("dve", "tensor_copy", reads, writes, out=out_ap, in_=in_ap)

        def make_norm(st, gsrc, tag, nx=2, junk_=None, xn_=None):
            gt = sbt(st, "gt" + tag, [128, D], F32); r_gt = Res("gt" + tag)
            P.dma("sp", gt[:], gsrc, writes=[r_gt])
            xts = [sbt(st, f"xt{tag}{k}", [128, D], F32) for k in range(nx)]
            r_xts = [Res(f"xt{tag}{k}") for k in range(nx)]
            junk, r_junk = junk_ if junk_ else (sbt(st, "junk" + tag, [128, D], bf), Res("junk" + tag))
            xn, r_xn = xn_ if xn_ else (sbt(st, "xn" + tag, [128, D], bf), Res("xn" + tag))
            stat = sbt(st, "stat" + tag, [128, 4], F32); r_stat = Res("stat" + tag)
            return dict(gt=gt, r_gt=r_gt, xts=xts, r_xts=r_xts, junk=junk, r_junk=r_junk, xn=xn,
                        r_xn=r_xn, stat=stat, r_stat=r_stat)

        def norm_load(N, k, src):
            P.dma("sp", N["xts"][k][:], src, writes=[N["r_xts"][k]])

        def norm_compute(N, k, dst_fn, r_dst):
            xt = N["xts"][k]; rxt = N["r_xts"][k]
            stat = N["stat"]; rs = N["r_stat"]
            P.act(N["junk"][:], xt[:], AF.Square, [rxt], [N["r_junk"], rs], accum_out=stat[:, 0:1])
            P.act(stat[:, 1:2], stat[:, 0:1], AF.Ln, [rs], [rs], scale=1.0 / D, bias=EPS)
            P.act(stat[:, 2:3], stat[:, 1:2], AF.Exp, [rs], [rs], scale=-0.5)
            P.g("dve", "scalar_tensor_tensor", [rxt, rs, N["r_gt"]], [N["r_xn"]], out=N["xn"][:], in0=xt[:],
                scalar=stat[:, 2:3], in1=N["gt"][:], op0=ALU.mult, op1=ALU.mult)
            for half in range(2):
                b = nb()
                bb = bankbf(b)
                for cc in range(8):
                    ck = half * 8 + cc
                    P.tr(bb[:, cc * 128:(cc + 1) * 128], N["xn"][:, ck * 128:(ck + 1) * 128], identb[:],
                         [N["r_xn"], r_identb], [rbank[b]])
                evac(dst_fn(half), bb.rearrange("p (c t) -> p c t", c=8), [rbank[b]], [r_dst])

        WT = {}
        r_WT = Res("WTscr")
        wlist = [("q", w_own, 0, 1024, 16), ("gl", w_own, 1024, 48, 16), ("za", w_own, 1072, 1024, 16),
                 ("uv", w_own, 2096, 2048, 16), ("zb", w_own, 4144, 1024, 16), ("mg", w_own, 5168, 4096, 16),
                 ("upa", w_up_a, 0, 2048, 8), ("upb", w_up_b, 0, 2048, 8), ("out", w_out, 0, 2048, 16),
                 ("pg", w_pg, 0, 2048, 16), ("ple", w_ple, 0, 2048, 2)]
        def conv_units(st):
            cst = [sbt(st, f"cst{k}", [128, 2048], F32) for k in range(4)]; r_cst = [Res(f"cst{k}") for k in range(4)]
            cbf = [sbt(st, f"cbf{k}", [128, 2048], bf) for k in range(4)]; r_cbf = [Res(f"cbf{k}") for k in range(4)]
            cc_ = 0
            for (nm, src, col0, ncol, KC) in wlist:
                scr = WT[nm]
                for kc in range(KC):
                    for c0 in range(0, ncol, 2048):
                        w_ = min(2048, ncol - c0)
                        k_ = cc_ % 4
                        cc_ += 1
                        P.dma("sp", cst[k_][:, 0:w_], src[kc * 128:(kc + 1) * 128, col0 + c0:col0 + c0 + w_], writes=[r_cst[k_]])
                        ceng = "pool" if cc_ % 3 == 0 else "dve"
                        P.g(ceng, "tensor_copy", [r_cst[k_]], [r_cbf[k_]], out=cbf[k_][:, 0:w_], in_=cst[k_][:, 0:w_])

                        def store(scr=scr, c0=c0, w_=w_, kc=kc, k_=k_):
                            if w_ % CW == 0:
                                P.dma("sp", scr[c0 // CW:(c0 + w_) // CW, :, kc, :].rearrange("c p w -> p c w"),
                                      cbf[k_][:, 0:w_].rearrange("p (c w) -> p c w", w=CW), reads=[r_cbf[k_]], writes=[r_WT], key="WTw")
                            else:
                                P.dma("sp", scr[c0 // CW, :, kc, 0:w_], cbf[k_][:, 0:w_], reads=[r_cbf[k_]], writes=[r_WT], key="WTw")
                        yield store

        for (nm, src, col0, ncol, KC) in wlist:
            WT[nm] = dscr("wb_" + nm, [(ncol + CW - 1) // CW, 128, KC, CW], bf)
        with ExitStack() as st:
            N = make_norm(st, norm_g, "a")
            wkv = sbt(st, "wkv", [128, 16, 1536], bf); r_wkv = Res("wkv")
            wst = [sbt(st, f"wsta{k}", [128, 1536], F32) for k in range(2)]
            r_wst = [Res(f"wsta{k}") for k in range(2)]
            for ck in range(16):
                P.dma("sp", wst[ck % 2][:], w_in[ck * 128:(ck + 1) * 128, 1024:2560], writes=[r_wst[ck % 2]])
                P.g("pool", "tensor_copy", [r_wst[ck % 2]], [r_wkv], out=wkv[:, ck, :], in_=wst[ck % 2][:])
            hT = [sbt(st, f"hTa{k}", [128, 16, 128], bf) for k in range(2)]
            r_hT = [Res(f"hTa{k}") for k in range(2)]
            kvtok = sbt(st, "kvtok", [128, 1024], bf); r_kvtok = Res("kvtok")
            KTst = [sbt(st, f"KTst{k}", [128, 8, 512], bf) for k in range(2)]
            r_KTst = [Res(f"KTst{k}") for k in range(2)]
            Vst = [sbt(st, f"Vst{k}", [128, 2, 4, 4, 65], bf) for k in range(2)]
            r_Vst = [Res(f"Vst{k}") for k in range(2)]
            for k in range(2):
                P.g("pool", "memset", [], [r_Vst[k]], Vst[k][:], 1.0)
            cgen = conv_units(st)
            cpend = []

            def conv_step(n):
                while cpend:
                    cpend.pop(0)()
                for _ in range(n):
                    try:
                        cpend.append(next(cgen))
                    except StopIteration:
                        break
            norm_load(N, 0, xb[0:128, :])
            kvst = []
            for T in range(int(os.environ.get('KDBG_NT', NT))):
                grp, tt = T // 4, T % 4
                sk = grp % 2
                if T + 1 < NT:
                    norm_load(N, (T + 1) % 2, xb[(T + 1) * 128:(T + 2) * 128, :])
                while kvst:
                    kvst.pop(0)()
                conv_step(3)
                hk = T % 2
                if 'n' not in SKIP:
                  norm_compute(N, T % 2, lambda half, hk=hk: hT[hk][:, half * 8:(half + 1) * 8, :], r_hT[hk])
                if 'k' in SKIP:
                    continue
                bks = [nb() for _ in range(3)]
                for cb in range(3):
                    for ck in range(16):
                        if 'm' in SKIP:
                            break
                        P.mm(banks[bks[cb]][:, :], hT[hk][:, ck, :], wkv[:, ck, cb * 512:(cb + 1) * 512],
                             ck == 0, ck == 15, [r_hT[hk], r_wkv], [rbank[bks[cb]]])
                if 'e' in SKIP:
                    continue
                EV = os.environ.get('KDBG_EV', '12345')
                if '1' in EV:
                    evac(kvtok[:, 0:512], banks[bks[0]][:, :], [rbank[bks[0]]], [r_kvtok], eng="act")
                if '2' in EV:
                    evac(kvtok[:, 512:768], banks[bks[1]][:, 0:256], [rbank[bks[1]]], [r_kvtok], eng="dve")
                if '3' in EV:
                    evac(Vst[sk][:, 0, tt, :, 0:64], banks[bks[1]][:, 256:512].rearrange("p (g e) -> p g e", g=4),
                         [rbank[bks[1]]], [r_Vst[sk]], eng="dve")
                if '4' in EV:
                    evac(kvtok[:, 768:1024], banks[bks[2]][:, 0:256], [rbank[bks[2]]], [r_kvtok], eng="act")
                if '5' in EV:
                    evac(Vst[sk][:, 1, tt, :, 0:64], banks[bks[2]][:, 256:512].rearrange("p (g e) -> p g e", g=4),
                         [rbank[bks[2]]], [r_Vst[sk]], eng="dve")
                if 't' in SKIP:
                    continue
                b = nb()
                bb = bankbf(b)
                for m in range(8):
                    P.tr(bb[:, m * 128:(m + 1) * 128], kvtok[:, m * 128:(m + 1) * 128], identb[:],
                         [r_kvtok, r_identb], [rbank[b]])
                evac(KTst[sk][:, :, tt * 128:(tt + 1) * 128], bb.rearrange("p (m t) -> p m t", m=8),
                     [rbank[b]], [r_KTst[sk]])
                if tt == 3 and 's' not in SKIP:
                    def kv_store(grp=grp, sk=sk):
                        P.dma("sp", KT.rearrange("m p t -> p m t")[:, :, grp * 512:(grp + 1) * 512], KTst[sk][:],
                              reads=[r_KTst[sk]], writes=[rKT], key="KTw")
                        for y in range(2):
                            for gq in range(4):
                                P.dma("sp", VS[y, gq][:, grp * 4:(grp + 1) * 4, :],
                                      Vst[sk][:, y, :, gq, :], reads=[r_Vst[sk]], writes=[rVS], key="VSw")
                    kvst.append(kv_store)

            while kvst:
                kvst.pop(0)()
            for _ in range(400):
                conv_step(3)
            conv_step(0)

        P.barrier()
        with ExitStack() as st:
          if not os.environ.get('KDBG_NOCMP'):
              w1b = [sbt(st, f"w1b{y}", [128, 32, 128], bf) for y in range(2)]
              w2b = [sbt(st, f"w2b{y}", [128, 128], bf) for y in range(2)]
              peT = [sbt(st, f"peT{y}", [128, 32], bf) for y in range(2)]
              r_cw = Res("cmpw")
              for y, (w1, w2, pe) in enumerate(((w1_k, w2_k, pe_k), (w1_v, w2_v, pe_v))):
                  P.g("pool", "memset", [], [r_cw], w1b[y][:], 0.0)
                  P.g("pool", "memset", [], [r_cw], w2b[y][:], 0.0)
                  for hf in range(2):
                      ps_ = slice(hf * 64, hf * 64 + 64)
                      P.dma("pool", w1b[y][ps_, :, hf * 64:hf * 64 + 64], w1,
                            writes=[r_cw], key="cmpw")
                      P.dma("pool", w2b[y][ps_, hf * 64:hf * 64 + 64], w2, writes=[r_cw], key="cmpw")
                      P.dma("pool", peT[y][ps_, :], pe, writes=[r_cw], key="cmpw")
              cbias = sbt(st, "cbias", [128, 2], F32); r_cb = Res("cbias")
              for y in range(2):
                  b = nb()
                  for l in range(32):
                      P.mm(banks[b][:, 0:1], w1b[y][:, l, :], peT[y][:, l:l + 1], l == 0, l == 31, [r_cw], [rbank[b]])
                  evac(cbias[:, y:y + 1], banks[b][:, 0:1], [rbank[b]], [r_cb], eng="dve")
              P.g("pool", "memset", [], [r_kcT], kcT[:], 0.0)
              for g_ in range(4):
                  pp = slice(64, 73) if g_ % 2 == 0 else slice(0, 9)
                  P.dma("sp", kcT[pp, g_, :], t_posk_c, writes=[r_kcT], key="kcApos")
              P.g("pool", "memset", [], [r_vcA], vcA[:], 0.0)
              P.g("pool", "memset", [], [r_vcA], vcA[:, :, :, 64:65], 1.0)
              kin = [sbt(st, f"kin{k}", [128, S + 128], bf) for k in range(2)]
              r_kin = [Res(f"kin{k}") for k in range(2)]
              hid = sbt(st, "hid", [128, 128], bf); r_hid = Res("hid")
              cnt = 0
              for y in range(2):
                  for pr in range(2):
                      kb = cnt % 2
                      cnt += 1
                      P.dma("sp", kin[kb][:, 0:S], KT[y * 2 + pr][:, 0:S], reads=[rKT], writes=[r_kin[kb]])
                      for nt in range(4):
                          nn = 128 if nt < 3 else 127
                          b = nb()
                          for l in range(32):
                              base = 16 * 128 * nt + l
                              rhs = kin[kb][:, base:base + 16 * (nn - 1) + 1:16]
                              P.mm(banks[b][:, 0:nn], w1b[y][:, l, :], rhs, l == 0, l == 31,
                                   [r_cw, r_kin[kb]], [rbank[b]])
                          if nn < 128:
                              P.g("dve", "memset", [], [r_hid], hid[:], 0.0)
                          P.act(hid[:, 0:nn], banks[b][:, 0:nn], AF.Silu, [rbank[b], r_cb], [r_hid],
                                bias=cbias[:, y:y + 1])
                          b2 = nb()
                          if y == 0:
                              P.mm(banks[b2][:, 0:128], w2b[0][:], hid[:], True, True, [r_cw, r_hid], [rbank[b2]])
                              evac(kcT[0:64, 2 * pr, nt * 128:(nt + 1) * 128], banks[b2][0:64, 0:128], [rbank[b2]], [r_kcT],
                                   eng="dve")
                              evac(kcT[64:128, 2 * pr + 1, nt * 128:(nt + 1) * 128], banks[b2][64:128, 0:128], [rbank[b2]], [r_kcT],
                                   eng="dve")
                          else:
                              P.mm(banks[b2][:, 0:128], hid[:], w2b[1][:], True, True, [r_cw, r_hid], [rbank[b2]])
                              evac(vcA[:, nt, pr * 2:pr * 2 + 2, 0:64],
                                   banks[b2][:, 0:128].rearrange("p (g e) -> p g e", g=2), [rbank[b2]], [r_vcA],
                                   eng="dve")
              if "dbg_kc" in debug:
                  dk = nc.dram_tensor("dbg_kc", [128, 4, 512], bf, kind="ExternalOutput").ap()
                  dv = nc.dram_tensor("dbg_vc", [128, 4, 4, 65], bf, kind="ExternalOutput").ap()
                  rd = Res("dbgkc")
                  P.dma("sp", dk, kcT[:], reads=[r_kcT], writes=[rd], key="dbg")
                  P.dma("sp", dv, vcA[:], reads=[r_vcA], writes=[rd], key="dbg")
                  P.final.append(rd)

        if "stop1" in debug:
            P.final.extend([rKT, rVS])
            P.finalize()
            P.emit()
            return nc


        P.barrier()
        ACTS = dscr("ACTS", [2048, 9216], bf)
        GATES = dscr("GATES", [2048, 48], F32)
        rACTS = Res("ACTS"); rGATES = Res("GATES")
        acol = {"q": 0, "za": 1024, "uv": 2048, "zb": 4096, "mg": 5120}
        nblk = int(os.environ.get("KDBG_NB", NOWN))
        with ExitStack() as st:
            N = make_norm(st, norm_g, "o", nx=2)
            hTo = sbt(st, "hTo", [128, 16, 2048], bf); r_hTo = Res("hTo")
            norm_load(N, 0, xo[0:128, :])
            for i in range(nblk):
                if i + 1 < nblk:
                    norm_load(N, (i + 1) % 2, xo[(i + 1) * 128:(i + 2) * 128, :])
                norm_compute(N, i % 2, lambda half, i=i: hTo[:, half * 8:(half + 1) * 8, i * 128:(i + 1) * 128], r_hTo)
            wbufa = [sbt(st, f"wbufa{k}", [128, 16, CW], bf) for k in range(2)]; r_wbufa = [Res(f"wbufa{k}") for k in range(2)]
            stg = [sbt(st, f"stg{k}", [128, CW], bf) for k in range(4)]; r_stg = [Res(f"stg{k}") for k in range(4)]
            gstg = [sbt(st, f"gstg{k}", [128, 48], F32) for k in range(2)]; r_gstg = [Res(f"gstg{k}") for k in range(2)]
            ca = {"w": 0, "s": 0}
            funcs = {"q": AF.Copy, "gl": AF.Sigmoid, "za": AF.Silu, "uv": AF.Gelu, "zb": AF.Silu, "mg": AF.Sigmoid}
            for (nm, wd) in (("gl", 48), ("q", 1024), ("za", 1024), ("uv", 2048), ("zb", 1024), ("mg", 4096)):
                for cc in range(0, wd, CW):
                    w_ = min(CW, wd - cc)
                    wk = ca["w"] % 2
                    ca["w"] += 1
                    P.dma("sp", wbufa[wk][:, :, 0:w_], WT[nm][cc // CW][:, :, 0:w_], reads=[r_WT], writes=[r_wbufa[wk]])
                    for i in range(nblk):
                        b = nb()
                        for ck in range(16):
                            P.mm(banks[b][:, 0:w_], hTo[:, ck, i * 128:(i + 1) * 128], wbufa[wk][:, ck, 0:w_], ck == 0, ck == 15,
                                 [r_hTo, r_wbufa[wk]], [rbank[b]])
                        if nm == "gl":
                            k_ = i % 2
                            P.act(gstg[k_][:], banks[b][:, 0:48], AF.Sigmoid, [rbank[b]], [r_gstg[k_]])
                            P.dma("sp", GATES[i * 128:(i + 1) * 128, :], gstg[k_][:], reads=[r_gstg[k_]], writes=[rGATES], key="GATESw")
                        else:
                            k_ = ca["s"] % 4
                            ca["s"] += 1
                            if nm == "q":
                                P.act(stg[k_][:, 0:w_], banks[b][:, 0:w_], AF.Copy, [rbank[b]], [r_stg[k_]], scale=0.125)
                            else:
                                P.act(stg[k_][:, 0:w_], banks[b][:, 0:w_], funcs[nm], [rbank[b]], [r_stg[k_]])
                            P.dma("sp", ACTS[i * 128:(i + 1) * 128, acol[nm] + cc:acol[nm] + cc + w_], stg[k_][:, 0:w_],
                                  reads=[r_stg[k_]], writes=[rACTS], key="ACTSw")

        OAB = dscr("OAB", [2048, 2048], bf); rOAB = Res("OAB")
        ssum = sbt(top, "ssum", [128, 64], F32); r_ssum = Res("ssum")
        P.barrier()
        with ExitStack() as st:
            b1 = sbt(st, "b1", [128, 2048], bf); r_b1 = Res("b1")
            oab = sbt(st, "oab", [128, 2048], bf); r_oab = Res("oab")
            N = None
            lng = sbt(st, "lng", [128, 1024], F32); lnb = sbt(st, "lnb", [128, 1024], F32); r_ln = Res("ln")
            P.dma("sp", lng[:], ln_g, writes=[r_ln], key="lnc"); P.dma("sp", lnb[:], ln_b, writes=[r_ln], key="lnc")
            ind = sbt(st, "ind", [128, 4096], bf)
            ov = sbt(st, "ov", [128, 4, 128], bf); tri = sbt(st, "tri", [128, 4, 128], bf); wm = sbt(st, "wm", [128, 8, 128], bf)
            r_tab = Res("tab")
            for dst, src in ((ind, t_ind), (ov, t_ov), (tri, t_tri), (wm, t_wm)):
                P.dma("sp", dst[:], src, writes=[r_tab], key="tab")
            f1 = sbt(st, "f1", [128, 2048], F32); r_f1 = Res("f1")
            wsf = f1[:, 0:1024].rearrange("p (a b) -> p a b", a=8); trl = sbt(st, "trl", [128, 128], F32)
            wsb = sbt(st, "wsb", [128, 8, 128], bf); bsT = sbt(st, "bsT", [128, 8], F32); r_sg = Res("sgw")
            P.dma("sp", wsf, sgu_wT, writes=[r_sg, r_f1], key="sgw"); P.dma("sp", trl[:], t_trilT, writes=[r_sg], key="sgw")
            P.dma("sp", bsT[:], sgu_bT, writes=[r_sg], key="sgw")
            P.g("dve", "tensor_tensor", [r_sg, r_f1], [r_sg], out=wsb[:], in0=wsf,
                in1=trl[:].unsqueeze(1).broadcast_to([128, 8, 128]), op=ALU.mult)

            hTb = sbt(st, "hTb", [128, 16, 128], bf); r_hTb = Res("hTb")
            wbuf = None; r_wbuf = None
            wctr = {"c": 0}

            def dense(lhsT_fn, r_l, KC, wname, c0, width, epi):
                for s0 in range(0, width, CW):
                    dense1(lhsT_fn, r_l, KC, wname, c0 + s0, min(CW, width - s0), lambda b, w, s0=s0: epi(b, w, s0))

            def dense1(lhsT_fn, r_l, KC, wname, c0, width, epi):
                wk = wctr["c"] % 2
                wctr["c"] += 1
                P.dma("sp", wbuf[wk][:, 0:KC, 0:width], WT[wname][c0 // CW][:, :, 0:width], reads=[r_WT], writes=[r_wbuf[wk]])
                b = nb()
                for ck in range(KC):
                    P.mm(banks[b][:, 0:width], lhsT_fn(ck), wbuf[wk][:, ck, 0:width], ck == 0, ck == KC - 1,
                         [r_l, r_wbuf[wk]], [rbank[b]])
                epi(b, width)

            ablk2 = [sbt(st, f"ablk{k}", [128, 5120], bf) for k in range(2)]; r_ablk2 = [Res(f"ablk{k}") for k in range(2)]
            gates2 = [sbt(st, f"gates{k}", [128, 48], F32) for k in range(2)]; r_gates2 = [Res(f"gates{k}") for k in range(2)]
            cur = {}
            QA = sbt(st, "QA", [128, 4, 512], bf); r_QA = Res("QA")
            P.g("pool", "memset", [], [r_QA], QA[:], 0.0)
            cmt = sbt(st, "cmt", [128, 4, 128], bf); r_cmt = Res("cmt")
            fb2 = sbt(st, "fb2", [128, 128], F32); caus = sbt(st, "caus", [128, 128], F32); r_fb = Res("fb")
            PT = [sbt(st, f"PT{k}", [128, 512], bf) for k in range(4)]; r_PT = [Res(f"PT{k}") for k in range(4)]
            PTc = sbt(st, "PTc", [128, 4, 512], bf); r_PTc = [Res(f"PTc{k}") for k in range(4)]
            Osb = [sbt(st, f"Osb{k}", [65, 512], F32) for k in range(2)]; r_Osb = [Res(f"Osb{k}") for k in range(2)]
            oatt = sbt(st, "oatt", [128, 1024], F32); r_oatt = Res("oatt")
            otmp = sbt(st, "otmp", [128, 256], F32); r_otmp = Res("otmp")
            sm = sbt(st, "sm", [128, 32], F32); r_sm = Res("sm")
            imp = sbt(st, "imp", [128, 128], F32); r_imp = Res("imp")
            sc2 = sbt(st, "sc2", [128, 128], F32); r_sc2 = Res("sc2")
            m8 = sbt(st, "m8", [128, 16], F32); r_m8 = Res("m8")
            mbq = sbt(st, "mbq", [128, 128], bf); r_mbq = Res("mbq")
            MBT = sbt(st, "MBT", [128, 2, 128], bf); r_MBT = Res("MBT")
            P.g("pool", "memset", [], [r_MBT], MBT[:], 0.0)
            Kbuf = [sbt(st, f"Kbuf{k}", [128, S], bf) for k in range(2)]; r_Kbuf = [Res(f"Kbuf{k}") for k in range(2)]
            r_Kpos = Res("Kpos")
            for k in range(2):
                P.g("pool", "memset", [], [r_Kpos, r_Kbuf[k]], Kbuf[k][:], 0.0)
            P.dma("sp", Kbuf[0][64:73, :], t_posk_s, writes=[r_Kpos], key="kpos"); P.dma("sp", Kbuf[1][0:9, :], t_posk_s, writes=[r_Kpos], key="kpos")
            Vbuf = [sbt(st, f"Vbuf{k}", [128, NT, 65], bf) for k in range(2)]; r_Vbuf = [Res(f"Vbuf{k}") for k in range(2)]
            r_Kwpos = [Res(f"Kwpos{k}") for k in range(2)]
            HOLD_KW = 1
            Kwb = [sbt(st, f"Kwb{k}", [128, 1024], bf) for k in range(2)]; r_Kwb = [Res(f"Kwb{k}") for k in range(2)]
            for k in range(2):
                P.g("pool", "memset", [], [r_Kwb[k], r_Kwpos[k]], Kwb[k][:], 0.0)
            Vwb = [sbt(st, f"Vwb{k}", [128, 8, 65], bf) for k in range(2)]; r_Vwb = [Res(f"Vwb{k}") for k in range(2)]
            f2 = sbt(st, "f2", [128, 2048], F32); r_f2 = Res("f2")
            oT = hTb; r_oT = r_hTb
            xT = oT; r_xT = r_oT
            ptk = sbt(st, "ptk", [128, 256], F32); pbk = sbt(st, "pbk", [128, 256], bf); r_pt = Res("ptk")
            pT = sbt(st, "pT", [128, 2, 128], bf); r_pT = Res("pT")
            ptc = {"k": 0}

            def transposeN(src, r_src, n, dst, r_dst):
                for h0 in range(0, n, 8):
                    m = min(8, n - h0)
                    b = nb()
                    bb = bankbf(b)
                    for cc in range(m):
                        P.tr(bb[:, cc * 128:(cc + 1) * 128], src[:, (h0 + cc) * 128:(h0 + cc + 1) * 128], identb[:],
                             [r_src, r_identb], [rbank[b]])
                    evac(dst[:, h0:h0 + m, :], bb[:, 0:m * 128].rearrange("p (c t) -> p c t", c=m), [rbank[b]], [r_dst])

            pending = []
            octr = {"k": 0}

            def flush():
                while pending:
                    pending.pop(0)()

            def branch(tiles, Qop, rq, ob, save_pt=False):
                nt_ = len(tiles)
                pts = {}

                def emitS(idx):
                    (kT, rk, extras, vap, rv) = tiles[idx]
                    sbk = nb()
                    Sap = banks[sbk][:, :].rearrange("p (a b) -> p a b", a=4)
                    P.mm(Sap, kT, Qop, True, len(extras) == 0, rk + rq, [rbank[sbk]])
                    for ei, (el, er, rr) in enumerate(extras):
                        P.mm(Sap, el, er, False, ei == len(extras) - 1, [r_tab, r_identb] + rr, [rbank[sbk]])
                    if save_pt:
                        pt_ap, rpt = PTc[:, idx, :], r_PTc[idx]
                    else:
                        k_ = ptc["k"] % 4
                        ptc["k"] += 1
                        pt_ap, rpt = PT[k_][:], r_PT[k_]
                    P.act(pt_ap, banks[sbk][:, :], AF.Exp, [rbank[sbk]], [rpt])
                    pts[idx] = (pt_ap, rpt)

                def emitPV(idx):
                    (kT, rk, extras, vap, rv) = tiles[idx]
                    pt_ap, rpt = pts[idx]
                    P.mm(banks[ob][0:65, :], vap, pt_ap, idx == 0, idx == nt_ - 1, [rv, rpt], [rbank[ob]])

                for idx in range(nt_):
                    emitS(idx)
                    if idx >= 2:
                        emitPV(idx - 2)
                    if idx == min(2, nt_ - 1):
                        flush()
                for idx in range(max(0, nt_ - 2), nt_):
                    emitPV(idx)

            def epi_evac(ob):
                k_ = octr["k"] % 2
                octr["k"] += 1
                evac(Osb[k_][:, :], banks[ob][0:65, :], [rbank[ob]], [r_Osb[k_]], eng="dve")
                return k_

            def epilogue(k_, br, g, first):
                for r in range(4):
                    P.tr(banks[7][:, r * 65:(r + 1) * 65], Osb[k_][0:65, r * 128:(r + 1) * 128], identf[0:65, 0:65],
                         [r_Osb[k_], r_identf], [rbank[7]])
                O3 = banks[7][:, 0:260].rearrange("p (a e) -> p a e", a=4)
                P.g("dve", "tensor_scalar", [rbank[7]], [r_sm], out=sm[:, 0:4], in0=O3[:, :, 64], scalar1=1e-30,
                    scalar2=None, op0=ALU.max)
                P.g("dve", "reciprocal", [r_sm], [r_sm], out=sm[:, 4:8], in_=sm[:, 0:4])
                P.g("dve", "tensor_tensor", [r_sm, cur["r_gates"]], [r_sm], out=sm[:, 8:12], in0=sm[:, 4:8],
                    in1=cur["gates"][:, br * 16 + 4 * g:br * 16 + 4 * g + 4], op=ALU.mult)
                dst = oatt[:, g * 256:(g + 1) * 256].rearrange("p (a e) -> p a e", a=4)
                wb_ = sm[:, 8:12].unsqueeze(2).broadcast_to([128, 4, 64])
                if first:
                    P.g("dve", "tensor_tensor", [rbank[7], r_sm], [r_oatt], out=dst, in0=O3[:, :, 0:64], in1=wb_, op=ALU.mult)
                else:
                    t3 = otmp[:, :].rearrange("p (a e) -> p a e", a=4)
                    P.g("dve", "tensor_tensor", [rbank[7], r_sm], [r_otmp], out=t3, in0=O3[:, :, 0:64], in1=wb_, op=ALU.mult)
                    P.g("dve", "tensor_tensor", [r_otmp, r_oatt], [r_oatt], out=dst, in0=dst, in1=t3, op=ALU.add)

            segs = [(0, 1024, "q"), (0, 48, "gl"), (0, 1024, "za"), (0, 2048, "uv"), (0, 1024, "zb"), (0, 4096, "mg")]
            nblk = int(os.environ.get("KDBG_NB", NOWN))
            for i in range(nblk):
                tok = slice(i * 128, (i + 1) * 128)
                xk = 0
                def load_acts(j):
                    P.dma("sp", ablk2[j % 2][:], ACTS[j * 128:(j + 1) * 128, 0:5120], reads=[rACTS], writes=[r_ablk2[j % 2]])
                    P.dma("sp", gates2[j % 2][:], GATES[j * 128:(j + 1) * 128, :], reads=[rGATES], writes=[r_gates2[j % 2]])
                if i == 0:
                    load_acts(0)
                ablk = ablk2[i % 2]; r_ablk = r_ablk2[i % 2]
                r_qtok = r_za = r_uvg = r_zb = r_ablk
                za = ablk[:, 1024:2048]; zb = ablk[:, 4096:5120]
                cur["gates"] = gates2[i % 2]; cur["r_gates"] = r_gates2[i % 2]
                qb3 = t_qb[i].rearrange("p (g c) -> p g c", g=4)
                for g_ in range(4):
                    pp = slice(64, 73) if g_ % 2 == 0 else slice(0, 9)
                    P.dma("sp", QA[pp, g_, :], qb3[:, g_, :], writes=[r_QA], key="QAqb")
                P.dma("sp", cmt[:], t_cm[i], writes=[r_cmt])
                P.dma("sp", fb2[:], t_fb2[i], writes=[r_fb], key="fb"); P.dma("sp", caus[:], t_caus[i], writes=[r_fb], key="fb")
                bq = nb()
                bbq = bankbf(bq)
                for cc in range(8):
                    P.tr(bbq[:, cc * 128:(cc + 1) * 128], ablk[:, cc * 128:(cc + 1) * 128], identb[:],
                         [r_qtok, r_identb], [rbank[bq]])
                for P2 in range(2):
                    for hf2 in range(2):
                        hs2 = slice(hf2 * 64, hf2 * 64 + 64)
                        evac(QA[hs2, 2 * P2 + hf2, :], bbq[hs2, P2 * 512:(P2 + 1) * 512], [rbank[bq]], [r_QA])
                if i + 1 < nblk:
                    load_acts(i + 1)
                ntc = (32 * i + 31) // 128 + 1
                L = (4 * i + 4) * 128
                T0 = max(0, 4 * i - 4)
                for g in range(4):
                    P_, hf = g // 2, g % 2
                    hs = slice(hf * 64, hf * 64 + 64)
                    kb = g % 2
                    po_ = slice(64, 73) if hf == 0 else slice(0, 9)
                    Qop = QA[:, g, :].rearrange("p (a b) -> p a b", a=4)
                    rq = [r_QA]
                    po_ = slice(64, 73) if hf == 0 else slice(0, 9)
                    P.dma("sp", Kbuf[kb][hs, 0:L], KT[4 + P_][hs, 0:L], reads=[rKT], writes=[r_Kbuf[kb]])
                    P.dma("sp", Vbuf[kb][:, 0:L // 128, :], VS[0, g][:, 0:L // 128, :], reads=[rVS], writes=[r_Vbuf[kb]])
                    P.dma("sp", Kwb[kb][hs, 0:L - T0 * 128], KT[6 + P_][hs, T0 * 128:L], reads=[rKT], writes=[r_Kwb[kb]])
                    P.dma("sp", Kwb[kb][po_, 0:L - T0 * 128], t_posk_s[:, T0 * 128:L], writes=[r_Kwpos[kb]])
                    P.dma("sp", Vwb[kb][:, 0:L // 128 - T0, :], VS[1, g][:, T0:L // 128, :], reads=[rVS], writes=[r_Vwb[kb]])
                    bc4 = lambda ap: ap.unsqueeze(1).broadcast_to([128, 4, 128])
                    tiles = []
                    for nt in range(ntc):
                        tiles.append((kcT[:, g, nt * 128:(nt + 1) * 128], [r_kcT],
                                      [(identb[:], bc4(cmt[:, nt, :]), [r_cmt])], vcA[:, nt, g, :], r_vcA))
                    obC = 5 + (octr["k"] % 2)
                    branch(tiles, Qop, rq, obC, save_pt=True)
                    kC = epi_evac(obC)
                    ub = nb()
                    for r in range(4):
                        for nt in range(ntc):
                            P.mm(banks[ub][:, r * 128:(r + 1) * 128], PTc[:, nt, r * 128:(r + 1) * 128], ov[:, nt, :],
                                 nt == 0, nt == ntc - 1, [r_PTc[nt], r_tab], [rbank[ub]])

                    def after_c(kC=kC, g=g, ub=ub):
                        epilogue(kC, 0, g, True)
                        for r in range(4):
                            if r == 0:
                                P.g("dve", "tensor_scalar", [rbank[ub], r_sm], [r_imp], out=imp[:], in0=banks[ub][:, 0:128],
                                    scalar1=sm[:, 4:5], scalar2=None, op0=ALU.mult)
                            else:
                                P.g("dve", "scalar_tensor_tensor", [rbank[ub], r_sm, r_imp], [r_imp], out=imp[:],
                                    in0=banks[ub][:, r * 128:(r + 1) * 128], scalar=sm[:, 4 + r:5 + r], in1=imp[:],
                                    op0=ALU.mult, op1=ALU.add)
                        P.g("dve", "tensor_tensor", [r_imp, r_fb], [r_imp], out=imp[:], in0=imp[:], in1=caus[:], op=ALU.mult)
                        P.g("dve", "tensor_tensor", [r_imp, r_fb], [r_imp], out=imp[:], in0=imp[:], in1=fb2[:], op=ALU.add)
                        P.g("dve", "max", [r_imp], [r_m8], out=m8[:, 0:8], in_=imp[:])
                        P.g("dve", "match_replace", [r_imp, r_m8], [r_sc2], out=sc2[:], in_to_replace=m8[:, 0:8],
                            in_values=imp[:], imm_value=-2.0)
                        P.g("dve", "max", [r_sc2], [r_m8], out=m8[:, 8:16], in_=sc2[:])
                        P.g("dve", "tensor_scalar", [r_m8], [r_m8], out=m8[:, 0:1], in0=m8[:, 15:16], scalar1=-0.5,
                            scalar2=None, op0=ALU.max)
                        P.g("dve", "tensor_scalar", [r_imp, r_m8], [r_sc2], out=sc2[:], in0=imp[:], scalar1=m8[:, 0:1],
                            scalar2=-NEGM, op0=ALU.is_ge, op1=ALU.mult)
                        P.g("dve", "tensor_scalar", [r_sc2], [r_mbq], out=mbq[:], in0=sc2[:], scalar1=NEGM, scalar2=None,
                            op0=ALU.add)
                    pending.append(after_c)
                    tiles = []
                    for rr_ in range(8):
                        T = 4 * i - 4 + rr_
                        if T < 0:
                            continue
                        tiles.append((Kwb[kb][:, (T - T0) * 128:(T - T0 + 1) * 128], [r_Kwb[kb], r_Kwpos[kb]],
                                      [(identb[:], bc4(wm[:, rr_, :]), [])],
                                      Vwb[kb][:, T - T0, :], r_Vwb[kb]))
                    obW = 5 + (octr["k"] % 2)
                    branch(tiles, Qop, rq, obW)
                    flush()
                    kW = epi_evac(obW)
                    mb_ = nb()
                    P.tr(bankbf(mb_)[:, 0:128], mbq[:], identb[:], [r_mbq, r_identb], [rbank[mb_]])
                    evac(MBT[0:64, 0, :], bankbf(mb_)[0:64, 0:128], [rbank[mb_]], [r_MBT], eng="dve")
                    evac(MBT[64:128, 1, :], bankbf(mb_)[64:128, 0:128], [rbank[mb_]], [r_MBT], eng="dve")
                    pending.append(lambda kW=kW, g=g: epilogue(kW, 2, g, False))
                    tiles = []
                    for kt in range(4 * i + 4):
                        ex = [(ind[:, (kt % 32) * 128:(kt % 32 + 1) * 128],
                               MBT[:, kt // 32, :].unsqueeze(1).broadcast_to([128, 4, 128]), [r_MBT])]
                        if kt >= 4 * i:
                            ex.append((identb[:], bc4(tri[:, kt - 4 * i, :]), []))
                        tiles.append((Kbuf[kb][:, kt * 128:(kt + 1) * 128], [r_Kbuf[kb], r_Kpos],
                                      ex, Vbuf[kb][:, kt, :], r_Vbuf[kb]))
                    obS = 5 + (octr["k"] % 2)
                    branch(tiles, Qop, rq, obS)
                    flush()
                    kS = epi_evac(obS)
                    pending.append(lambda kS=kS, g=g: epilogue(kS, 1, g, False))
                flush()
                P.g("dve", "tensor_tensor", [r_oatt, r_za], [r_oab], out=oab[:, 0:1024], in0=oatt[:], in1=za, op=ALU.mult)
                v_ = ablk[:, 3072:4096]
                P.act(f1[:, 0:1024], v_, AF.Copy, [r_uvg], [r_f1, r_sm], accum_out=sm[:, 16:17])
                P.act(f1[:, 1024:2048], v_, AF.Square, [r_uvg], [r_f1, r_sm], accum_out=sm[:, 17:18])
                P.g("dve", "tensor_scalar", [r_sm], [r_sm], out=sm[:, 18:19], in0=sm[:, 16:17], scalar1=1.0 / 1024,
                    scalar2=None, op0=ALU.mult)
                P.g("dve", "tensor_tensor", [r_sm], [r_sm], out=sm[:, 19:20], in0=sm[:, 18:19], in1=sm[:, 18:19], op=ALU.mult)
                P.g("dve", "scalar_tensor_tensor", [r_sm], [r_sm], out=sm[:, 20:21], in0=sm[:, 17:18], scalar=1.0 / 1024,
                    in1=sm[:, 19:20], op0=ALU.mult, op1=ALU.subtract)
                P.act(sm[:, 21:22], sm[:, 20:21], AF.Ln, [r_sm], [r_sm], bias=EPS)
                P.act(sm[:, 22:23], sm[:, 21:22], AF.Exp, [r_sm], [r_sm], scale=-0.5)
                P.g("dve", "tensor_scalar", [r_uvg, r_sm], [r_f1], out=f1[:, 0:1024], in0=v_, scalar1=sm[:, 18:19],
                    scalar2=sm[:, 22:23], op0=ALU.subtract, op1=ALU.mult)
                P.g("dve", "tensor_tensor", [r_f1, r_ln], [r_f1], out=f1[:, 0:1024], in0=f1[:, 0:1024], in1=lng[:], op=ALU.mult)
                P.g("dve", "tensor_tensor", [r_f1, r_ln], [r_b1], out=b1[:, 0:1024], in0=f1[:, 0:1024], in1=lnb[:], op=ALU.add)
                for half in range(2):
                    b = nb()
                    for gq in range(4):
                        G_ = half * 4 + gq
                        P.mm(banks[b][:, gq * 128:(gq + 1) * 128], wsb[:, G_, :], b1[:, G_ * 128:(G_ + 1) * 128], True, True,
                             [r_sg, r_b1], [rbank[b]])
                    P.g("dve", "tensor_tensor", [rbank[b], r_sg], [r_f2], out=f2[:, half * 512:(half + 1) * 512].rearrange("p (a e) -> p a e", a=4),
                        in0=banks[b][:, :].rearrange("p (a e) -> p a e", a=4),
                        in1=bsT[:, half * 4:(half + 1) * 4].unsqueeze(2).broadcast_to([128, 4, 128]), op=ALU.add)
                P.g("dve", "tensor_tensor", [r_f2, r_uvg], [r_f2], out=f2[:, 0:1024], in0=f2[:, 0:1024], in1=ablk[:, 2048:3072], op=ALU.mult)
                P.g("dve", "tensor_tensor", [r_f2, r_zb], [r_oab], out=oab[:, 1024:2048], in0=f2[:, 0:1024], in1=zb, op=ALU.mult)
                P.dma("sp", OAB[tok, :], oab[:], reads=[r_oab], writes=[rOAB], key="OABw")

        P.barrier()
        X1 = dscr("X1", [2048, D], F32); X2 = dscr("X2", [2048, D], F32)
        rX1 = Res("X1"); rX2 = Res("X2")
        with ExitStack() as st:
            TA = sbt(st, "TA", [128, 16, 2048], bf); r_TA = Res("TA")
            MA = sbt(st, "MA", [128, 16, 2048], bf); r_MA = Res("MA")
            big = [sbt(st, f"big{k}", [128, 16, 512], bf) for k in range(2)]; r_big = [Res(f"big{k}") for k in range(2)]
            wsm = [sbt(st, f"wsm{k}", [128, 8, 512], bf) for k in range(2)]; r_wsm = [Res(f"wsm{k}") for k in range(2)]
            plew = sbt(st, "plew", [128, 2, 512], bf); r_plew = Res("plew")
            tA = [sbt(st, f"tA{k}", [128, 512], F32) for k in range(2)]; r_tA = [Res(f"tA{k}") for k in range(2)]
            tB = [sbt(st, f"tB{k}", [128, 512], F32) for k in range(2)]; r_tB = [Res(f"tB{k}") for k in range(2)]
            tX = [sbt(st, f"tX{k}", [128, 512], F32) for k in range(2)]; r_tX = [Res(f"tX{k}") for k in range(2)]
            tY = [sbt(st, f"tY{k}", [128, 512], F32) for k in range(2)]; r_tY = [Res(f"tY{k}") for k in range(2)]
            pT3 = wsm[0][:].rearrange("p a b -> p (a b)").rearrange("p (c t) -> p c t", c=2); r_pT3 = r_wsm[0]
            ptk3 = sbt(st, "ptk3", [128, 256], F32); pbk3 = sbt(st, "pbk3", [128, 256], bf); r_p3 = Res("p3")
            jk3 = sbt(st, "jk3", [128, 512], bf); r_jk3 = Res("jk3")

            def tr_tiles(src_fn, r_src, i):
                for half in range(2):
                    b = nb()
                    bb = bankbf(b)
                    for c8 in range(8):
                        P.tr(bb[:, c8 * 128:(c8 + 1) * 128], src_fn(half * 8 + c8), identb[:], [r_src, r_identb], [rbank[b]])
                    evac(TA[:, half * 8:(half + 1) * 8, i * 128:(i + 1) * 128], bb.rearrange("p (c t) -> p c t", c=8),
                         [rbank[b]], [r_TA])

            for i in range(nblk):
                k = i % 2
                ldv = big[k][:, 0:4, :]
                P.dma("sp", ldv, OAB[i * 128:(i + 1) * 128, :].rearrange("p (a b) -> p a b", a=4), reads=[rOAB], writes=[r_big[k]])
                tr_tiles(lambda c, k=k: big[k][:, c // 4, (c % 4) * 128:(c % 4 + 1) * 128], r_big[k], i)
            acts3 = ACTS.rearrange("(t p) c -> p t c", p=128)
            for cc in range(4):
                P.dma("sp", big[0][:, 0:nblk, :], acts3[:, 0:nblk, 5120 + cc * 512:5120 + (cc + 1) * 512], reads=[rACTS], writes=[r_big[0]])
                P.dma("sp", big[1][:, 0:nblk, :], acts3[:, 0:nblk, 7168 + cc * 512:7168 + (cc + 1) * 512], reads=[rACTS], writes=[r_big[1]])
                P.dma("sp", wsm[0][:], WT["upa"][cc], reads=[r_WT], writes=[r_wsm[0]])
                P.dma("sp", wsm[1][:], WT["upb"][cc], reads=[r_WT], writes=[r_wsm[1]])
                for i in range(nblk):
                    k = i % 2
                    bA = nb()
                    for ck in range(8):
                        P.mm(banks[bA][:, :], TA[:, ck, i * 128:(i + 1) * 128], wsm[0][:, ck, :], ck == 0, ck == 7, [r_TA, r_wsm[0]], [rbank[bA]])
                    bB = nb()
                    for ck in range(8):
                        P.mm(banks[bB][:, :], TA[:, 8 + ck, i * 128:(i + 1) * 128], wsm[1][:, ck, :], ck == 0, ck == 7, [r_TA, r_wsm[1]], [rbank[bB]])
                    P.g("dve", "tensor_tensor", [rbank[bA], r_big[0]], [r_tA[k]], out=tA[k][:], in0=banks[bA][:, :], in1=big[0][:, i, :], op=ALU.mult)
                    P.g("dve", "tensor_tensor", [rbank[bB], r_big[1]], [r_tB[k]], out=tB[k][:], in0=banks[bB][:, :], in1=big[1][:, i, :], op=ALU.mult)
                    P.g("dve", "tensor_tensor", [r_tA[k], r_tB[k]], [r_MA], out=MA[:, i, cc * 512:(cc + 1) * 512], in0=tA[k][:], in1=tB[k][:], op=ALU.add)
            for i in range(nblk):
                tr_tiles(lambda c, i=i: MA[:, i, c * 128:(c + 1) * 128], r_MA, i)
            for cc in range(4):
                wk = cc % 2
                P.dma("sp", big[wk][:], WT["out"][cc], reads=[r_WT], writes=[r_big[wk]])
                for i in range(nblk):
                    k = i % 2
                    b = nb()
                    for ck in range(16):
                        P.mm(banks[b][:, :], TA[:, ck, i * 128:(i + 1) * 128], big[wk][:, ck, :], ck == 0, ck == 15, [r_TA, r_big[wk]], [rbank[b]])
                    P.dma("sp", tX[k][:], xo[i * 128:(i + 1) * 128, cc * 512:(cc + 1) * 512], writes=[r_tX[k]])
                    P.g("dve", "tensor_tensor", [rbank[b], r_tX[k]], [r_tY[k]], out=tY[k][:], in0=banks[b][:, :], in1=tX[k][:], op=ALU.add)
                    P.dma("sp", X1[i * 128:(i + 1) * 128, cc * 512:(cc + 1) * 512], tY[k][:], reads=[r_tY[k]], writes=[rX1], key="X1w")
                    P.act(MA[:, i, cc * 512:(cc + 1) * 512], tY[k][:], AF.Copy, [r_tY[k]], [r_MA])
            for i in range(nblk):
                tr_tiles(lambda c, i=i: MA[:, i, c * 128:(c + 1) * 128], r_MA, i)
            for i in range(nblk):
                P.dma("sp", ptk3[:], po[i * 128:(i + 1) * 128, :], writes=[r_p3])
                P.act(pbk3[:], ptk3[:], AF.Copy, [r_p3], [r_p3])
                b = nb()
                bb = bankbf(b)
                for c2 in range(2):
                    P.tr(bb[:, c2 * 128:(c2 + 1) * 128], pbk3[:, c2 * 128:(c2 + 1) * 128], identb[:], [r_p3, r_identb], [rbank[b]])
                evac(pT3[:, :, i * 128:(i + 1) * 128], bb[:, 0:256].rearrange("p (c t) -> p c t", c=2), [rbank[b]], [r_pT3])
            for cc in range(4):
                wk = cc % 2
                P.dma("sp", big[wk][:], WT["pg"][cc], reads=[r_WT], writes=[r_big[wk]])
                P.dma("sp", plew[:], WT["ple"][cc], reads=[r_WT], writes=[r_plew])
                for i in range(nblk):
                    k = i % 2
                    bG = nb()
                    for ck in range(16):
                        P.mm(banks[bG][:, :], TA[:, ck, i * 128:(i + 1) * 128], big[wk][:, ck, :], ck == 0, ck == 15, [r_TA, r_big[wk]], [rbank[bG]])
                    P.act(tA[k][:], banks[bG][:, :], AF.Sigmoid, [rbank[bG]], [r_tA[k]])
                    bP = nb()
                    for ck in range(2):
                        P.mm(banks[bP][:, :], pT3[:, ck, i * 128:(i + 1) * 128], plew[:, ck, :], ck == 0, ck == 1, [r_pT3, r_plew], [rbank[bP]])
                    P.dma("sp", tX[k][:], X1[i * 128:(i + 1) * 128, cc * 512:(cc + 1) * 512], reads=[rX1], writes=[r_tX[k]])
                    P.g("dve", "tensor_tensor", [rbank[bP], r_tA[k]], [r_tB[k]], out=tB[k][:], in0=banks[bP][:, :], in1=tA[k][:], op=ALU.mult)
                    P.g("dve", "tensor_tensor", [r_tB[k], r_tX[k]], [r_tY[k]], out=tY[k][:], in0=tB[k][:], in1=tX[k][:], op=ALU.add)
                    P.act(jk3[:], tY[k][:], AF.Square, [r_tY[k]], [r_jk3, r_ssum], accum_out=ssum[:, i * 4 + cc:i * 4 + cc + 1])
                    P.dma("sp", X2[i * 128:(i + 1) * 128, cc * 512:(cc + 1) * 512], tY[k][:], reads=[r_tY[k]], writes=[rX2], key="X2w")
        P.barrier()
        with ExitStack() as st:
            fgt = sbt(st, "fgt", [128, D], F32); r_fgt = Res("fgt")
            P.dma("sp", fgt[:], final_g, writes=[r_fgt])
            xt3 = [sbt(st, f"xt3{k}", [128, D], F32) for k in range(2)]; r_xt3 = [Res(f"xt3{k}") for k in range(2)]
            ot3 = [sbt(st, f"ot3{k}", [128, D], F32) for k in range(2)]; r_ot3 = [Res(f"ot3{k}") for k in range(2)]
            st3 = sbt(st, "st3", [128, 8], F32); r_st3 = Res("st3")
            for i in range(nblk):
                k = i % 2
                P.dma("sp", xt3[k][:], X2[i * 128:(i + 1) * 128, :], reads=[rX2], writes=[r_xt3[k]])
                P.g("dve", "tensor_tensor", [r_ssum], [r_st3], out=st3[:, 0:2], in0=ssum[:, i * 4:i * 4 + 2], in1=ssum[:, i * 4 + 2:i * 4 + 4], op=ALU.add)
                P.g("dve", "tensor_tensor", [r_st3], [r_st3], out=st3[:, 2:3], in0=st3[:, 0:1], in1=st3[:, 1:2], op=ALU.add)
                P.act(st3[:, 3:4], st3[:, 2:3], AF.Ln, [r_st3], [r_st3], scale=1.0 / D, bias=EPS)
                P.act(st3[:, 4:5], st3[:, 3:4], AF.Exp, [r_st3], [r_st3], scale=-0.5)
                P.g("dve", "scalar_tensor_tensor", [r_xt3[k], r_st3, r_fgt], [r_ot3[k]], out=ot3[k][:], in0=xt3[k][:], scalar=st3[:, 4:5],
                    in1=fgt[:], op0=ALU.mult, op1=ALU.mult)
                P.dma("sp", out[i * 128:(i + 1) * 128, :], ot3[k][:], reads=[r_ot3[k]], writes=[rOUT], key="outw")
        P.finalize()
        P.emit()
    return nc


def _make_w_own(w):
    qcols = []
    for P_ in range(2):
        for r in range(4):
            for hd in (8 * P_ + r, 8 * P_ + 4 + r):
                qcols.extend(range(hd * 64, hd * 64 + 64))
    return np.ascontiguousarray(np.concatenate([w[:, qcols], w[:, 2560:]], axis=1))


def _prep_inputs(inputs):
    f = lambda a: np.ascontiguousarray(np.asarray(a, dtype=np.float32))
    x = f(inputs["x"]); p = f(inputs["p"])[0]
    com = _common_tables()
    shared = {
        "norm_g": np.ascontiguousarray(np.broadcast_to(f(inputs["norm_g"])[0][None, :], (128, D))),
        "final_g": np.ascontiguousarray(np.broadcast_to(f(inputs["final_g"])[None, :], (128, D))),
        "w_in": f(inputs["w_in"])[0],
        "w_own": _make_w_own(f(inputs["w_in"])[0]),
        "pe_k": np.ascontiguousarray(f(inputs["cmp_pe_k"])[0].T), "w1_k": np.ascontiguousarray(f(inputs["cmp_w1_k"])[0].transpose(1, 0, 2)), "w2_k": f(inputs["cmp_w2_k"])[0],
        "pe_v": np.ascontiguousarray(f(inputs["cmp_pe_v"])[0].T), "w1_v": np.ascontiguousarray(f(inputs["cmp_w1_v"])[0].transpose(1, 0, 2)), "w2_v": f(inputs["cmp_w2_v"])[0],
        "ln_g": np.ascontiguousarray(np.broadcast_to(f(inputs["ln_v_g"])[0][None, :], (128, 1024))),
        "ln_b": np.ascontiguousarray(np.broadcast_to(f(inputs["ln_v_b"])[0][None, :], (128, 1024))),
        "sgu_wT": np.ascontiguousarray(f(inputs["sgu_w"])[0].transpose(2, 0, 1)),
        "sgu_bT": np.ascontiguousarray(f(inputs["sgu_b"])[0].T),
        "w_up_a": f(inputs["w_up_a"])[0], "w_up_b": f(inputs["w_up_b"])[0],
        "w_out": f(inputs["w_out"])[0], "w_ple": f(inputs["w_ple"])[0], "w_pg": f(inputs["w_ple_gate"])[0],
    }
    shared.update(com)
    in_maps = []
    for core in range(8):
        b, c = core // 4, core % 4
        m = dict(shared)
        m["xb"] = x[b]
        xr = x[b].reshape(16, 4, 128, D)[:, c].reshape(2048, D)
        m["xo"] = np.ascontiguousarray(xr)
        m["po"] = np.ascontiguousarray(p[b].reshape(16, 4, 128, 256)[:, c].reshape(2048, 256))
        m.update(_core_tables(c))
        in_maps.append(m)
    return in_maps


def kernel(**inputs):
    in_maps = _prep_inputs(inputs)
    nc = build()
    res = run_bass_kernel_spmd(nc, in_maps, core_ids=list(range(8)))
    outp = np.zeros((2, S, D), np.float32)
    o = outp.reshape(2, 16, 4, 128, D)
    for core in range(8):
        b, c = core // 4, core % 4
        o[b, :, c] = res.results[core]["out"].reshape(16, 128, D)
    return outp
```

```python
import os
import numpy as np
from contextlib import ExitStack
import ml_dtypes
import concourse.bass as bass
import concourse.mybir as mybir
from concourse.bass_utils import run_bass_kernel_spmd

F32 = mybir.dt.float32
BF16 = mybir.dt.bfloat16
AF = mybir.ActivationFunctionType
ALU = mybir.AluOpType

S = 8192
D = 2048
NT = 64
NOWN = 16
NEGM = -30000.0
EPS = 1e-6
CW = 512
SKIP = os.environ.get('KDBG_SKIP', '')


class Res:
    __slots__ = ("name", "w", "r")

    def __init__(self, name):
        self.name = name
        self.w = None
        self.r = []


class Op:
    __slots__ = ("eng", "fn", "deps", "inc", "sem", "val", "dma")


class Prog:
    ENGS = ["pe", "act", "dve", "pool", "sp"]

    def __init__(self, nc, stack):
        self.nc = nc
        self.stack = stack
        self.ops = {e: [] for e in self.ENGS}
        self.dma_sems = {}
        self.nsem = 0
        self.final = []

    def newsem(self, name):
        self.nsem += 1
        return self.stack.enter_context(self.nc.semaphore(f"{name}{self.nsem}"))

    def _add(self, eng, fn, reads, writes, dma_key=None):
        op = Op()
        op.eng = eng
        op.fn = fn
        op.inc = False
        op.sem = None
        op.val = 0
        op.dma = dma_key
        writes = list(writes) + [r for r in reads if r.name.startswith("bank") and r not in writes]
        deps = []
        for r in reads:
            if r.w is not None:
                deps.append(r.w)
        for w in writes:
            if w.w is not None:
                deps.append(w.w)
            deps.extend(w.r)
        op.deps = [d for d in deps
                   if not (d.eng == "pe" and eng == "pe" and d.dma is None and dma_key is None)]
        for r in reads:
            r.r.append(op)
        for w in writes:
            w.w = op
            w.r = []
        self.ops[eng].append(op)
        return op

    def g(self, eng, name, reads, writes, *args, **kw):
        return self._add(eng, lambda e: getattr(e, name)(*args, **kw), reads, writes)

    def mm(self, out, lhsT, rhs, start, stop, reads, writes):
        return self._add("pe", lambda e: e.matmul(out, lhsT=lhsT, rhs=rhs, start=start, stop=stop),
                         reads, writes)

    def tr(self, out, in_, ident, reads, writes):
        return self._add("pe", lambda e: e.transpose(out=out, in_=in_, identity=ident), reads, writes)

    def act(self, out, in_, func, reads, writes, eng="act", **kw):
        return self._add(eng, lambda e: e.activation(out=out, in_=in_, func=func, **kw), reads, writes)

    def dma(self, eng, out, in_, reads=(), writes=(), key=None):
        if key is None:
            key = (writes[0].name if writes else reads[0].name)
        return self._add(eng, lambda e: e.dma_start(out=out, in_=in_), reads, writes, dma_key=key)

    def barrier(self):
        deps = []
        for e in self.ENGS:
            for o in reversed(self.ops[e]):
                if o.fn is not None and o.dma is None:
                    deps.append(o)
                    break
        deps += getattr(self, "dma_pending", [])
        self.dma_pending = []
        for e in self.ENGS:
            op = Op()
            op.eng = e
            op.fn = None
            op.inc = False
            op.sem = None
            op.val = 0
            op.dma = None
            op.deps = list(deps)
            self.ops[e].append(op)

    def finalize(self):
        allops = [o for e in self.ENGS for o in self.ops[e]]
        for o in allops:
            for d in o.deps:
                d.inc = True
        fin = [r.w for r in self.final if r.w is not None]
        for o in fin:
            o.inc = True
        MAXV = 30000
        order = {}
        for e in self.ENGS:
            sem = None
            cnt = 0
            for o in self.ops[e]:
                if o.dma is not None:
                    continue
                if o.inc:
                    if sem is None or cnt >= MAXV:
                        sem = self.newsem("c" + e)
                        cnt = 0
                    cnt += 1
                    o.sem, o.val = sem, cnt
        for o in self.dma_order:
            if o.dma not in self.dma_sems:
                self.dma_sems[o.dma] = [self.newsem("d"), 0]
            ent = self.dma_sems[o.dma]
            if ent[1] + 16 > MAXV:
                ent[0] = self.newsem("d")
                ent[1] = 0
            ent[1] += 16
            o.sem, o.val = ent[0], ent[1]
            o.inc = True
        self.fin_ops = fin

    def emit(self):
        nc = self.nc

        def run(ename, eng):
            known = {}
            for o in self.ops[ename]:
                for d in o.deps:
                    k = id(d.sem)
                    if known.get(k, 0) < d.val:
                        eng.wait_ge(d.sem, d.val)
                        known[k] = d.val
                if o.fn is None:
                    continue
                ins = o.fn(eng)
                if o.inc:
                    ins.then_inc(o.sem, 16 if o.dma is not None else 1)
            if ename == "sp":
                for d in self.fin_ops:
                    k = id(d.sem)
                    if known.get(k, 0) < d.val:
                        eng.wait_ge(d.sem, d.val)
                        known[k] = d.val

        with nc.Block() as block:
            @block.tensor
            def _(e):
                run("pe", e)

            @block.scalar
            def _(e):
                run("act", e)

            @block.vector
            def _(e):
                run("dve", e)

            @block.gpsimd
            def _(e):
                run("pool", e)

            @block.sync
            def _(e):
                run("sp", e)


_orig_add = Prog._add


def _add_wrapped(self, eng, fn, reads, writes, dma_key=None):
    op = _orig_add(self, eng, fn, reads, writes, dma_key)
    if dma_key is not None:
        if not hasattr(self, "dma_order"):
            self.dma_order = []
        self.dma_order.append(op)
        if not hasattr(self, "dma_pending"):
            self.dma_pending = []
        self.dma_pending.append(op)
    return op


Prog._add = _add_wrapped


def _split3(a):
    a = np.asarray(a, np.float64)
    hi = a.astype(np.float32).astype(ml_dtypes.bfloat16)
    r1 = a - hi.astype(np.float64)
    mid = r1.astype(np.float32).astype(ml_dtypes.bfloat16)
    r2 = r1 - mid.astype(np.float64)
    lo = r2.astype(np.float32).astype(ml_dtypes.bfloat16)
    return hi, mid, lo


def _common_tables():
    bf = ml_dtypes.bfloat16
    t = {}
    pos = np.arange(S)
    pk = np.zeros((9, S), np.float32)
    pk[0:3] = 1.0
    pk[3:6] = 128.0 * (pos // 128)
    pk[6:9] = pos % 128
    t["posk_s"] = pk.astype(bf)
    n = np.arange(512)
    ce = 16 * n + 31
    pc = np.zeros((9, 512), np.float32)
    pc[0:3] = 1.0
    pc[3:6] = 128.0 * (ce // 128)
    pc[6:9] = ce % 128
    t["posk_c"] = pc.astype(bf)
    ng = (np.arange(4)[None, :, None] * 128 + np.arange(128)[:, None, None])
    jj = np.arange(128)[None, None, :]
    t["ov"] = ((ng >= 4 * jj - 1) & (ng <= 4 * jj + 3)).astype(np.float32).astype(bf)
    t["ind"] = ((np.arange(128)[:, None] % 64) == (np.arange(4096)[None, :] // 64)).astype(np.float32).astype(bf)
    t["identb"] = np.eye(128, dtype=np.float32).astype(bf)
    t["identf"] = np.eye(128, dtype=np.float32)
    t["trilT"] = (np.arange(128)[:, None] <= np.arange(128)[None, :]).astype(np.float32)
    return t


def _core_tables(c):
    bf = ml_dtypes.bfloat16
    t = {}
    h = np.arange(16)
    slopes = np.power(2.0, -8.0 * (h + 1) / 16.0)
    slopes = np.power(np.float32(2.0), (-8.0 * (h + 1).astype(np.float32) / 16)).astype(np.float32).astype(np.float64)
    s_hi, s_mid, s_lo = _split3(slopes)
    qb = np.zeros((NOWN, 9, 16, 128), bf)
    tq = np.arange(128)
    for i in range(NOWN):
        j = 4 * i + c
        tt = 128 * j + tq
        A = -slopes[:, None] * tt[None, :].astype(np.float64)
        a_hi, a_mid, a_lo = _split3(A)
        qb[i, 0], qb[i, 1], qb[i, 2] = a_hi, a_mid, a_lo
        for k, sv in enumerate((s_hi, s_mid, s_lo)):
            qb[i, 3 + k] = np.broadcast_to(sv[:, None], (16, 128))
            qb[i, 6 + k] = np.broadcast_to(sv[:, None], (16, 128))
    t["qb"] = qb.reshape(NOWN, 9, 2048)
    cm = np.zeros((NOWN, 128, 4, 128), np.float32)
    fb2 = np.zeros((NOWN, 128, 128), np.float32)
    caus = np.zeros((NOWN, 128, 128), np.float32)
    blk = np.arange(128)
    for i in range(NOWN):
        j = 4 * i + c
        tt = 128 * j + tq
        ng = np.arange(4)[None, :, None] * 128 + np.arange(128)[:, None, None]
        ok = (16 * ng + 31 <= tt[None, None, :]) & (ng <= 510)
        cm[i] = np.where(ok, 0.0, NEGM)
        cur = tt // 64
        forced = (blk[None, :] == 0) | (blk[None, :] == cur[:, None]) | (blk[None, :] == cur[:, None] - 1)
        cz = blk[None, :] * 64 <= tt[:, None]
        fb2[i] = np.where(cz, np.where(forced, 1000.0, 0.0), -1.0)
        caus[i] = cz.astype(np.float32)
    t["cm"] = cm.astype(bf)
    t["fb2"] = fb2
    t["caus"] = caus
    k = np.arange(128)
    tri = np.zeros((128, 4, 128), np.float32)
    for r in range(4):
        d = 128 * (c - r) + tq[None, :] - k[:, None]
        tri[:, r, :] = np.where(d >= 0, 0.0, NEGM)
    t["tri"] = tri.astype(bf)
    wm = np.zeros((128, 8, 128), np.float32)
    for r in range(8):
        d = 128 * (c + 4 - r) + tq[None, :] - k[:, None]
        wm[:, r, :] = np.where((d >= 0) & (d < 512), 0.0, NEGM)
    t["wm"] = wm.astype(bf)
    return t


def build(debug=()):
    nc = bass.Bass("TRN2", target_bir_lowering=False)
    bf = BF16

    def din(name, shape, dt=F32):
        return nc.dram_tensor(name, list(shape), dt, kind="ExternalInput").ap()

    def dscr(name, shape, dt):
        kind = "ExternalOutput" if name in debug else "Internal"
        return nc.dram_tensor(name, list(shape), dt, kind=kind).ap()

    xb = din("xb", [S, D])
    xo = din("xo", [2048, D])
    po = din("po", [2048, 256])
    norm_g = din("norm_g", [128, D])
    final_g = din("final_g", [128, D])
    w_in = din("w_in", [D, 10800])
    w_own = din("w_own", [D, 9264])
    pe_k = din("pe_k", [64, 32]); w1_k = din("w1_k", [64, 32, 64]); w2_k = din("w2_k", [64, 64])
    pe_v = din("pe_v", [64, 32]); w1_v = din("w1_v", [64, 32, 64]); w2_v = din("w2_v", [64, 64])
    ln_g = din("ln_g", [128, 1024]); ln_b = din("ln_b", [128, 1024])
    sgu_wT = din("sgu_wT", [128, 8, 128])
    sgu_bT = din("sgu_bT", [128, 8])
    w_up_a = din("w_up_a", [1024, D]); w_up_b = din("w_up_b", [1024, D])
    w_out = din("w_out", [D, D]); w_ple = din("w_ple", [256, D]); w_pg = din("w_pg", [D, D])
    t_posk_s = din("posk_s", [9, S], bf); t_posk_c = din("posk_c", [9, 512], bf)
    t_ov = din("ov", [128, 4, 128], bf); t_ind = din("ind", [128, 4096], bf)
    t_identb = din("identb", [128, 128], bf); t_identf = din("identf", [128, 128])
    t_trilT = din("trilT", [128, 128])
    t_qb = din("qb", [NOWN, 9, 2048], bf); t_cm = din("cm", [NOWN, 128, 4, 128], bf)
    t_fb2 = din("fb2", [NOWN, 128, 128]); t_caus = din("caus", [NOWN, 128, 128])
    t_tri = din("tri", [128, 4, 128], bf); t_wm = din("wm", [128, 8, 128], bf)

    out = nc.dram_tensor("out", [2048, D], F32, kind="ExternalOutput").ap()

    KT = dscr("KT", [8, 128, S + 128], bf)
    VS = dscr("VS", [2, 4, 128, NT, 65], bf)
    rKT = Res("KT"); rVS = Res("VS")
    rOUT = Res("OUT")

    with ExitStack() as top:
        P = Prog(nc, top)
        P.final.append(rOUT)
        for nm in debug:
            pass

        def sbt(st, name, shape, dt):
            return st.enter_context(nc.sbuf_tensor("s_" + name, list(shape), dt))

        banks = [top.enter_context(nc.psum_tensor(f"bank{k}", [128, 512], F32)) for k in range(8)]
        rbank = [Res(f"bank{k}") for k in range(8)]
        bstate = {"k": 0}

        def nb():
            k = bstate["k"]
            bstate["k"] = (k + 1) % 5
            return k

        def bankbf(k):
            return banks[k][:].bitcast(bf)

        identb = sbt(top, "identb", [128, 128], bf); r_identb = Res("identb")
        identf = sbt(top, "identf", [128, 128], F32); r_identf = Res("identf")
        P.dma("sp", identb[:], t_identb, writes=[r_identb])
        P.dma("sp", identf[:], t_identf, writes=[r_identf])
        kcT = sbt(top, "kcA", [128, 4, 512], bf); r_kcT = Res("kcA")
        vcA = sbt(top, "vcA", [128, 4, 4, 65], bf); r_vcA = Res("vcA")

        evac_rr = {"k": 0}

        def evac(out_ap, in_ap, reads, writes, eng=None, func=AF.Copy, **kw):
            if eng is None:
                eng = "act" if (evac_rr["k"] % 2 == 0) else "dve"
                evac_rr["k"] += 1
            if eng == "act" or func != AF.Copy or kw:
                return P.act(out_ap, in_ap, func, reads, writes, **kw)
            return P.g("dve", "tensor_copy", reads, writes, out=out_ap, in_=in_ap)

        def make_norm(st, gsrc, tag, nx=2, junk_=None, xn_=None):
            gt = sbt(st, "gt" + tag, [128, D], F32); r_gt = Res("gt" + tag)
            P.dma("sp", gt[:], gsrc, writes=[r_gt])
            xts = [sbt(st, f"xt{tag}{k}", [128, D], F32) for k in range(nx)]
            r_xts = [Res(f"xt{tag}{k}") for k in range(nx)]
            junk, r_junk = junk_ if junk_ else (sbt(st, "junk" + tag, [128, D], bf), Res("junk" + tag))
            xn, r_xn = xn_ if xn_ else (sbt(st, "xn" + tag, [128, D], bf), Res("xn" + tag))
            stat = sbt(st, "stat" + tag, [128, 4], F32); r_stat = Res("stat" + tag)
            return dict(gt=gt, r_gt=r_gt, xts=xts, r_xts=r_xts, junk=junk, r_junk=r_junk, xn=xn,
                        r_xn=r_xn, stat=stat, r_stat=r_stat)

        def norm_load(N, k, src):
            P.dma("sp", N["xts"][k][:], src, writes=[N["r_xts"][k]])

        def norm_compute(N, k, dst_fn, r_dst):
            xt = N["xts"][k]; rxt = N["r_xts"][k]
            stat = N["stat"]; rs = N["r_stat"]
            P.act(N["junk"][:], xt[:], AF.Square, [rxt], [N["r_junk"], rs], accum_out=stat[:, 0:1])
            P.act(stat[:, 1:2], stat[:, 0:1], AF.Ln, [rs], [rs], scale=1.0 / D, bias=EPS)
            P.act(stat[:, 2:3], stat[:, 1:2], AF.Exp, [rs], [rs], scale=-0.5)
            P.g("dve", "scalar_tensor_tensor", [rxt, rs, N["r_gt"]], [N["r_xn"]], out=N["xn"][:], in0=xt[:],
                scalar=stat[:, 2:3], in1=N["gt"][:], op0=ALU.mult, op1=ALU.mult)
            for half in range(2):
                b = nb()
                bb = bankbf(b)
                for cc in range(8):
                    ck = half * 8 + cc
                    P.tr(bb[:, cc * 128:(cc + 1) * 128], N["xn"][:, ck * 128:(ck + 1) * 128], identb[:],
                         [N["r_xn"], r_identb], [rbank[b]])
                evac(dst_fn(half), bb.rearrange("p (c t) -> p c t", c=8), [rbank[b]], [r_dst])

        WT = {}
        r_WT = Res("WTscr")
        wlist = [("q", w_own, 0, 1024, 16), ("gl", w_own, 1024, 48, 16), ("za", w_own, 1072, 1024, 16),
                 ("uv", w_own, 2096, 2048, 16), ("zb", w_own, 4144, 1024, 16), ("mg", w_own, 5168, 4096, 16),
                 ("upa", w_up_a, 0, 2048, 8), ("upb", w_up_b, 0, 2048, 8), ("out", w_out, 0, 2048, 16),
                 ("pg", w_pg, 0, 2048, 16), ("ple", w_ple, 0, 2048, 2)]
        def conv_units(st, wl, tag, nbuf, engs):
            cst = [sbt(st, f"cst{tag}{k}", [128, 2048], F32) for k in range(nbuf)]; r_cst = [Res(f"cst{tag}{k}") for k in range(nbuf)]
            cbf = [sbt(st, f"cbf{tag}{k}", [128, 2048], bf) for k in range(nbuf)]; r_cbf = [Res(f"cbf{tag}{k}") for k in range(nbuf)]
            cc_ = 0
            for (nm, src, col0, ncol, KC) in wl:
                scr = WT[nm]
                for kc in range(KC):
                    for c0 in range(0, ncol, 2048):
                        w_ = min(2048, ncol - c0)
                        k_ = cc_ % nbuf
                        cc_ += 1
                        P.dma("sp", cst[k_][:, 0:w_], src[kc * 128:(kc + 1) * 128, col0 + c0:col0 + c0 + w_], writes=[r_cst[k_]])
                        ceng = engs[cc_ % len(engs)]
                        P.g(ceng, "tensor_copy", [r_cst[k_]], [r_cbf[k_]], out=cbf[k_][:, 0:w_], in_=cst[k_][:, 0:w_])

                        def store(scr=scr, c0=c0, w_=w_, kc=kc, k_=k_):
                            if w_ % CW == 0:
                                P.dma("sp", scr[c0 // CW:(c0 + w_) // CW, :, kc, :].rearrange("c p w -> p c w"),
                                      cbf[k_][:, 0:w_].rearrange("p (c w) -> p c w", w=CW), reads=[r_cbf[k_]], writes=[r_WT], key="WTw")
                            else:
                                P.dma("sp", scr[c0 // CW, :, kc, 0:w_], cbf[k_][:, 0:w_], reads=[r_cbf[k_]], writes=[r_WT], key="WTw")
                        yield store

        for (nm, src, col0, ncol, KC) in wlist:
            WT[nm] = dscr("wb_" + nm, [(ncol + CW - 1) // CW, 128, KC, CW], bf)
        with ExitStack() as st:
            N = make_norm(st, norm_g, "a")
            wkv = sbt(st, "wkv", [128, 16, 1536], bf); r_wkv = Res("wkv")
            wst = [sbt(st, f"wsta{k}", [128, 1536], F32) for k in range(2)]
            r_wst = [Res(f"wsta{k}") for k in range(2)]
            for ck in range(16):
                P.dma("sp", wst[ck % 2][:], w_in[ck * 128:(ck + 1) * 128, 1024:2560], writes=[r_wst[ck % 2]])
                P.g("pool", "tensor_copy", [r_wst[ck % 2]], [r_wkv], out=wkv[:, ck, :], in_=wst[ck % 2][:])
            hT = [sbt(st, f"hTa{k}", [128, 16, 128], bf) for k in range(2)]
            r_hT = [Res(f"hTa{k}") for k in range(2)]
            kvtok = sbt(st, "kvtok", [128, 1024], bf); r_kvtok = Res("kvtok")
            KTst = [sbt(st, f"KTst{k}", [128, 8, 512], bf) for k in range(2)]
            r_KTst = [Res(f"KTst{k}") for k in range(2)]
            Vst = [sbt(st, f"Vst{k}", [128, 2, 4, 4, 65], bf) for k in range(2)]
            r_Vst = [Res(f"Vst{k}") for k in range(2)]
            for k in range(2):
                P.g("pool", "memset", [], [r_Vst[k]], Vst[k][:], 1.0)
            cgen = conv_units(st, wlist[:6], "a", 4, ["pool", "dve", "dve"])
            cpend = []

            def conv_step(n):
                while cpend:
                    cpend.pop(0)()
                for _ in range(n):
                    try:
                        cpend.append(next(cgen))
                    except StopIteration:
                        break
            norm_load(N, 0, xb[0:128, :])
            kvst = []
            for T in range(int(os.environ.get('KDBG_NT', NT))):
                grp, tt = T // 4, T % 4
                sk = grp % 2
                if T + 1 < NT:
                    norm_load(N, (T + 1) % 2, xb[(T + 1) * 128:(T + 2) * 128, :])
                while kvst:
                    kvst.pop(0)()
                conv_step(2)
                hk = T % 2
                if 'n' not in SKIP:
                  norm_compute(N, T % 2, lambda half, hk=hk: hT[hk][:, half * 8:(half + 1) * 8, :], r_hT[hk])
                if 'k' in SKIP:
                    continue
                bks = [nb() for _ in range(3)]
                for cb in range(3):
                    for ck in range(16):
                        if 'm' in SKIP:
                            break
                        P.mm(banks[bks[cb]][:, :], hT[hk][:, ck, :], wkv[:, ck, cb * 512:(cb + 1) * 512],
                             ck == 0, ck == 15, [r_hT[hk], r_wkv], [rbank[bks[cb]]])
                if 'e' in SKIP:
                    continue
                EV = os.environ.get('KDBG_EV', '12345')
                if '1' in EV:
                    evac(kvtok[:, 0:512], banks[bks[0]][:, :], [rbank[bks[0]]], [r_kvtok], eng="act")
                if '2' in EV:
                    evac(kvtok[:, 512:768], banks[bks[1]][:, 0:256], [rbank[bks[1]]], [r_kvtok], eng="dve")
                if '3' in EV:
                    evac(Vst[sk][:, 0, tt, :, 0:64], banks[bks[1]][:, 256:512].rearrange("p (g e) -> p g e", g=4),
                         [rbank[bks[1]]], [r_Vst[sk]], eng="dve")
                if '4' in EV:
                    evac(kvtok[:, 768:1024], banks[bks[2]][:, 0:256], [rbank[bks[2]]], [r_kvtok], eng="act")
                if '5' in EV:
                    evac(Vst[sk][:, 1, tt, :, 0:64], banks[bks[2]][:, 256:512].rearrange("p (g e) -> p g e", g=4),
                         [rbank[bks[2]]], [r_Vst[sk]], eng="dve")
                if 't' in SKIP:
                    continue
                b = nb()
                bb = bankbf(b)
                for m in range(8):
                    P.tr(bb[:, m * 128:(m + 1) * 128], kvtok[:, m * 128:(m + 1) * 128], identb[:],
                         [r_kvtok, r_identb], [rbank[b]])
                evac(KTst[sk][:, :, tt * 128:(tt + 1) * 128], bb.rearrange("p (m t) -> p m t", m=8),
                     [rbank[b]], [r_KTst[sk]])
                if tt == 3 and 's' not in SKIP:
                    def kv_store(grp=grp, sk=sk):
                        P.dma("sp", KT.rearrange("m p t -> p m t")[:, :, grp * 512:(grp + 1) * 512], KTst[sk][:],
                              reads=[r_KTst[sk]], writes=[rKT], key="KTw")
                        for y in range(2):
                            for gq in range(4):
                                P.dma("sp", VS[y, gq][:, grp * 4:(grp + 1) * 4, :],
                                      Vst[sk][:, y, :, gq, :], reads=[r_Vst[sk]], writes=[rVS], key="VSw")
                    kvst.append(kv_store)

            while kvst:
                kvst.pop(0)()
            for _ in range(400):
                conv_step(3)
            conv_step(0)

        P.barrier()
        with ExitStack() as st:
          if not os.environ.get('KDBG_NOCMP'):
              w1b = [sbt(st, f"w1b{y}", [128, 32, 128], bf) for y in range(2)]
              w2b = [sbt(st, f"w2b{y}", [128, 128], bf) for y in range(2)]
              peT = [sbt(st, f"peT{y}", [128, 32], bf) for y in range(2)]
              r_cw = Res("cmpw")
              for y, (w1, w2, pe) in enumerate(((w1_k, w2_k, pe_k), (w1_v, w2_v, pe_v))):
                  P.g("pool", "memset", [], [r_cw], w1b[y][:], 0.0)
                  P.g("pool", "memset", [], [r_cw], w2b[y][:], 0.0)
                  for hf in range(2):
                      ps_ = slice(hf * 64, hf * 64 + 64)
                      P.dma("pool", w1b[y][ps_, :, hf * 64:hf * 64 + 64], w1,
                            writes=[r_cw], key="cmpw")
                      P.dma("pool", w2b[y][ps_, hf * 64:hf * 64 + 64], w2, writes=[r_cw], key="cmpw")
                      P.dma("pool", peT[y][ps_, :], pe, writes=[r_cw], key="cmpw")
              cbias = sbt(st, "cbias", [128, 2], F32); r_cb = Res("cbias")
              for y in range(2):
                  b = nb()
                  for l in range(32):
                      P.mm(banks[b][:, 0:1], w1b[y][:, l, :], peT[y][:, l:l + 1], l == 0, l == 31, [r_cw], [rbank[b]])
                  evac(cbias[:, y:y + 1], banks[b][:, 0:1], [rbank[b]], [r_cb], eng="dve")
              P.g("pool", "memset", [], [r_kcT], kcT[:], 0.0)
              for g_ in range(4):
                  pp = slice(64, 73) if g_ % 2 == 0 else slice(0, 9)
                  P.dma("sp", kcT[pp, g_, :], t_posk_c, writes=[r_kcT], key="kcApos")
              P.g("pool", "memset", [], [r_vcA], vcA[:], 0.0)
              P.g("pool", "memset", [], [r_vcA], vcA[:, :, :, 64:65], 1.0)
              kin = [sbt(st, f"kin{k}", [128, S + 128], bf) for k in range(2)]
              r_kin = [Res(f"kin{k}") for k in range(2)]
              hid = sbt(st, "hid", [128, 128], bf); r_hid = Res("hid")
              cnt = 0
              for y in range(2):
                  for pr in range(2):
                      kb = cnt % 2
                      cnt += 1
                      P.dma("sp", kin[kb][:, 0:S], KT[y * 2 + pr][:, 0:S], reads=[rKT], writes=[r_kin[kb]])
                      for nt in range(4):
                          nn = 128 if nt < 3 else 127
                          b = nb()
                          for l in range(32):
                              base = 16 * 128 * nt + l
                              rhs = kin[kb][:, base:base + 16 * (nn - 1) + 1:16]
                              P.mm(banks[b][:, 0:nn], w1b[y][:, l, :], rhs, l == 0, l == 31,
                                   [r_cw, r_kin[kb]], [rbank[b]])
                          if nn < 128:
                              P.g("dve", "memset", [], [r_hid], hid[:], 0.0)
                          P.act(hid[:, 0:nn], banks[b][:, 0:nn], AF.Silu, [rbank[b], r_cb], [r_hid],
                                bias=cbias[:, y:y + 1])
                          b2 = nb()
                          if y == 0:
                              P.mm(banks[b2][:, 0:128], w2b[0][:], hid[:], True, True, [r_cw, r_hid], [rbank[b2]])
                              evac(kcT[0:64, 2 * pr, nt * 128:(nt + 1) * 128], banks[b2][0:64, 0:128], [rbank[b2]], [r_kcT],
                                   eng="dve")
                              evac(kcT[64:128, 2 * pr + 1, nt * 128:(nt + 1) * 128], banks[b2][64:128, 0:128], [rbank[b2]], [r_kcT],
                                   eng="dve")
                          else:
                              P.mm(banks[b2][:, 0:128], hid[:], w2b[1][:], True, True, [r_cw, r_hid], [rbank[b2]])
                              evac(vcA[:, nt, pr * 2:pr * 2 + 2, 0:64],
                                   banks[b2][:, 0:128].rearrange("p (g e) -> p g e", g=2), [rbank[b2]], [r_vcA],
                                   eng="dve")
              if "dbg_kc" in debug:
                  dk = nc.dram_tensor("dbg_kc", [128, 4, 512], bf, kind="ExternalOutput").ap()
                  dv = nc.dram_tensor("dbg_vc", [128, 4, 4, 65], bf, kind="ExternalOutput").ap()
                  rd = Res("dbgkc")
                  P.dma("sp", dk, kcT[:], reads=[r_kcT], writes=[rd], key="dbg")
                  P.dma("sp", dv, vcA[:], reads=[r_vcA], writes=[rd], key="dbg")
                  P.final.append(rd)

        if "stop1" in debug:
            P.final.extend([rKT, rVS])
            P.finalize()
            P.emit()
            return nc


        P.barrier()
        ACTS = dscr("ACTS", [2048, 9216], bf)
        GATES = dscr("GATES", [2048, 48], F32)
        rACTS = Res("ACTS"); rGATES = Res("GATES")
        acol = {"q": 0, "za": 1024, "uv": 2048, "zb": 4096, "mg": 5120}
        nblk = int(os.environ.get("KDBG_NB", NOWN))
        with ExitStack() as st:
            N = make_norm(st, norm_g, "o", nx=2)
            hTo = sbt(st, "hTo", [128, 16, 2048], bf); r_hTo = Res("hTo")
            norm_load(N, 0, xo[0:128, :])
            for i in range(nblk):
                if i + 1 < nblk:
                    norm_load(N, (i + 1) % 2, xo[(i + 1) * 128:(i + 2) * 128, :])
                norm_compute(N, i % 2, lambda half, i=i: hTo[:, half * 8:(half + 1) * 8, i * 128:(i + 1) * 128], r_hTo)
            wbufa = [sbt(st, f"wbufa{k}", [128, 16, CW], bf) for k in range(2)]; r_wbufa = [Res(f"wbufa{k}") for k in range(2)]
            stg = [sbt(st, f"stg{k}", [128, CW], bf) for k in range(4)]; r_stg = [Res(f"stg{k}") for k in range(4)]
            gstg = [sbt(st, f"gstg{k}", [128, 48], F32) for k in range(2)]; r_gstg = [Res(f"gstg{k}") for k in range(2)]
            ca = {"w": 0, "s": 0}
            funcs = {"q": AF.Copy, "gl": AF.Sigmoid, "za": AF.Silu, "uv": AF.Gelu, "zb": AF.Silu, "mg": AF.Sigmoid}
            for (nm, wd) in (("gl", 48), ("q", 1024), ("za", 1024), ("uv", 2048), ("zb", 1024), ("mg", 4096)):
                for cc in range(0, wd, CW):
                    w_ = min(CW, wd - cc)
                    wk = ca["w"] % 2
                    ca["w"] += 1
                    P.dma("sp", wbufa[wk][:, :, 0:w_], WT[nm][cc // CW][:, :, 0:w_], reads=[r_WT], writes=[r_wbufa[wk]])
                    for i in range(nblk):
                        b = nb()
                        for ck in range(16):
                            P.mm(banks[b][:, 0:w_], hTo[:, ck, i * 128:(i + 1) * 128], wbufa[wk][:, ck, 0:w_], ck == 0, ck == 15,
                                 [r_hTo, r_wbufa[wk]], [rbank[b]])
                        if nm == "gl":
                            k_ = i % 2
                            P.act(gstg[k_][:], banks[b][:, 0:48], AF.Sigmoid, [rbank[b]], [r_gstg[k_]])
                            P.dma("sp", GATES[i * 128:(i + 1) * 128, :], gstg[k_][:], reads=[r_gstg[k_]], writes=[rGATES], key="GATESw")
                        else:
                            k_ = ca["s"] % 4
                            ca["s"] += 1
                            if nm == "q":
                                P.act(stg[k_][:, 0:w_], banks[b][:, 0:w_], AF.Copy, [rbank[b]], [r_stg[k_]], scale=0.125)
                            else:
                                P.act(stg[k_][:, 0:w_], banks[b][:, 0:w_], funcs[nm], [rbank[b]], [r_stg[k_]])
                            P.dma("sp", ACTS[i * 128:(i + 1) * 128, acol[nm] + cc:acol[nm] + cc + w_], stg[k_][:, 0:w_],
                                  reads=[r_stg[k_]], writes=[rACTS], key="ACTSw")

        OAB = dscr("OAB", [2048, 2048], bf); rOAB = Res("OAB")
        ssum = sbt(top, "ssum", [128, 64], F32); r_ssum = Res("ssum")
        P.barrier()
        with ExitStack() as st:
            b1 = sbt(st, "b1", [128, 2048], bf); r_b1 = Res("b1")
            oab = sbt(st, "oab", [128, 2048], bf); r_oab = Res("oab")
            N = None
            lng = sbt(st, "lng", [128, 1024], F32); lnb = sbt(st, "lnb", [128, 1024], F32); r_ln = Res("ln")
            P.dma("sp", lng[:], ln_g, writes=[r_ln], key="lnc"); P.dma("sp", lnb[:], ln_b, writes=[r_ln], key="lnc")
            ind = sbt(st, "ind", [128, 4096], bf)
            ov = sbt(st, "ov", [128, 4, 128], bf); tri = sbt(st, "tri", [128, 4, 128], bf); wm = sbt(st, "wm", [128, 8, 128], bf)
            r_tab = Res("tab")
            for dst, src in ((ind, t_ind), (ov, t_ov), (tri, t_tri), (wm, t_wm)):
                P.dma("sp", dst[:], src, writes=[r_tab], key="tab")
            f1 = sbt(st, "f1", [128, 2048], F32); r_f1 = Res("f1")
            wsf = f1[:, 0:1024].rearrange("p (a b) -> p a b", a=8); trl = sbt(st, "trl", [128, 128], F32)
            wsb = sbt(st, "wsb", [128, 8, 128], bf); bsT = sbt(st, "bsT", [128, 8], F32); r_sg = Res("sgw")
            P.dma("sp", wsf, sgu_wT, writes=[r_sg, r_f1], key="sgw"); P.dma("sp", trl[:], t_trilT, writes=[r_sg], key="sgw")
            P.dma("sp", bsT[:], sgu_bT, writes=[r_sg], key="sgw")
            P.g("dve", "tensor_tensor", [r_sg, r_f1], [r_sg], out=wsb[:], in0=wsf,
                in1=trl[:].unsqueeze(1).broadcast_to([128, 8, 128]), op=ALU.mult)

            hTb = sbt(st, "hTb", [128, 16, 128], bf); r_hTb = Res("hTb")
            wbuf = None; r_wbuf = None
            wctr = {"c": 0}

            def dense(lhsT_fn, r_l, KC, wname, c0, width, epi):
                for s0 in range(0, width, CW):
                    dense1(lhsT_fn, r_l, KC, wname, c0 + s0, min(CW, width - s0), lambda b, w, s0=s0: epi(b, w, s0))

            def dense1(lhsT_fn, r_l, KC, wname, c0, width, epi):
                wk = wctr["c"] % 2
                wctr["c"] += 1
                P.dma("sp", wbuf[wk][:, 0:KC, 0:width], WT[wname][c0 // CW][:, :, 0:width], reads=[r_WT], writes=[r_wbuf[wk]])
                b = nb()
                for ck in range(KC):
                    P.mm(banks[b][:, 0:width], lhsT_fn(ck), wbuf[wk][:, ck, 0:width], ck == 0, ck == KC - 1,
                         [r_l, r_wbuf[wk]], [rbank[b]])
                epi(b, width)

            cgen2 = conv_units(st, wlist[6:], "b", 2, ["pool"])
            cpend2 = []

            def conv_step2(n):
                while cpend2:
                    cpend2.pop(0)()
                for _ in range(n):
                    try:
                        cpend2.append(next(cgen2))
                    except StopIteration:
                        break
            ablk2 = [sbt(st, f"ablk{k}", [128, 5120], bf) for k in range(2)]; r_ablk2 = [Res(f"ablk{k}") for k in range(2)]
            gates2 = [sbt(st, f"gates{k}", [128, 48], F32) for k in range(2)]; r_gates2 = [Res(f"gates{k}") for k in range(2)]
            cur = {}
            QA = sbt(st, "QA", [128, 4, 512], bf); r_QA = Res("QA")
            P.g("pool", "memset", [], [r_QA], QA[:], 0.0)
            cmt = sbt(st, "cmt", [128, 4, 128], bf); r_cmt = Res("cmt")
            fb2 = sbt(st, "fb2", [128, 128], F32); caus = sbt(st, "caus", [128, 128], F32); r_fb = Res("fb")
            PT = [sbt(st, f"PT{k}", [128, 512], bf) for k in range(4)]; r_PT = [Res(f"PT{k}") for k in range(4)]
            PTc = sbt(st, "PTc", [128, 4, 512], bf); r_PTc = [Res(f"PTc{k}") for k in range(4)]
            Osb = [sbt(st, f"Osb{k}", [65, 512], F32) for k in range(2)]; r_Osb = [Res(f"Osb{k}") for k in range(2)]
            oatt = sbt(st, "oatt", [128, 1024], F32); r_oatt = Res("oatt")
            otmp = sbt(st, "otmp", [128, 256], F32); r_otmp = Res("otmp")
            sm = sbt(st, "sm", [128, 32], F32); r_sm = Res("sm")
            imp = sbt(st, "imp", [128, 128], F32); r_imp = Res("imp")
            sc2 = sbt(st, "sc2", [128, 128], F32); r_sc2 = Res("sc2")
            m8 = sbt(st, "m8", [128, 16], F32); r_m8 = Res("m8")
            mbq = sbt(st, "mbq", [128, 128], bf); r_mbq = Res("mbq")
            MBT = sbt(st, "MBT", [128, 2, 128], bf); r_MBT = Res("MBT")
            P.g("pool", "memset", [], [r_MBT], MBT[:], 0.0)
            Kbuf = [sbt(st, f"Kbuf{k}", [128, S], bf) for k in range(2)]; r_Kbuf = [Res(f"Kbuf{k}") for k in range(2)]
            r_Kpos = Res("Kpos")
            for k in range(2):
                P.g("pool", "memset", [], [r_Kpos, r_Kbuf[k]], Kbuf[k][:], 0.0)
            P.dma("sp", Kbuf[0][64:73, :], t_posk_s, writes=[r_Kpos], key="kpos"); P.dma("sp", Kbuf[1][0:9, :], t_posk_s, writes=[r_Kpos], key="kpos")
            Vbuf = [sbt(st, f"Vbuf{k}", [128, NT, 65], bf) for k in range(2)]; r_Vbuf = [Res(f"Vbuf{k}") for k in range(2)]
            r_Kwpos = [Res(f"Kwpos{k}") for k in range(2)]
            HOLD_KW = 1
            Kwb = [sbt(st, f"Kwb{k}", [128, 1024], bf) for k in range(2)]; r_Kwb = [Res(f"Kwb{k}") for k in range(2)]
            for k in range(2):
                P.g("pool", "memset", [], [r_Kwb[k], r_Kwpos[k]], Kwb[k][:], 0.0)
            Vwb = [sbt(st, f"Vwb{k}", [128, 8, 65], bf) for k in range(2)]; r_Vwb = [Res(f"Vwb{k}") for k in range(2)]
            f2 = sbt(st, "f2", [128, 2048], F32); r_f2 = Res("f2")
            oT = hTb; r_oT = r_hTb
            xT = oT; r_xT = r_oT
            ptk = sbt(st, "ptk", [128, 256], F32); pbk = sbt(st, "pbk", [128, 256], bf); r_pt = Res("ptk")
            pT = sbt(st, "pT", [128, 2, 128], bf); r_pT = Res("pT")
            ptc = {"k": 0}

            def transposeN(src, r_src, n, dst, r_dst):
                for h0 in range(0, n, 8):
                    m = min(8, n - h0)
                    b = nb()
                    bb = bankbf(b)
                    for cc in range(m):
                        P.tr(bb[:, cc * 128:(cc + 1) * 128], src[:, (h0 + cc) * 128:(h0 + cc + 1) * 128], identb[:],
                             [r_src, r_identb], [rbank[b]])
                    evac(dst[:, h0:h0 + m, :], bb[:, 0:m * 128].rearrange("p (c t) -> p c t", c=m), [rbank[b]], [r_dst])

            pending = []
            octr = {"k": 0}

            def flush():
                while pending:
                    pending.pop(0)()

            def branch(tiles, Qop, rq, ob, save_pt=False):
                nt_ = len(tiles)
                pts = {}

                def emitS(idx):
                    (kT, rk, extras, vap, rv) = tiles[idx]
                    sbk = nb()
                    Sap = banks[sbk][:, :].rearrange("p (a b) -> p a b", a=4)
                    P.mm(Sap, kT, Qop, True, len(extras) == 0, rk + rq, [rbank[sbk]])
                    for ei, (el, er, rr) in enumerate(extras):
                        P.mm(Sap, el, er, False, ei == len(extras) - 1, [r_tab, r_identb] + rr, [rbank[sbk]])
                    if save_pt:
                        pt_ap, rpt = PTc[:, idx, :], r_PTc[idx]
                    else:
                        k_ = ptc["k"] % 4
                        ptc["k"] += 1
                        pt_ap, rpt = PT[k_][:], r_PT[k_]
                    P.act(pt_ap, banks[sbk][:, :], AF.Exp, [rbank[sbk]], [rpt])
                    pts[idx] = (pt_ap, rpt)

                def emitPV(idx):
                    (kT, rk, extras, vap, rv) = tiles[idx]
                    pt_ap, rpt = pts[idx]
                    P.mm(banks[ob][0:65, :], vap, pt_ap, idx == 0, idx == nt_ - 1, [rv, rpt], [rbank[ob]])

                for idx in range(nt_):
                    emitS(idx)
                    if idx >= 2:
                        emitPV(idx - 2)
                    if idx == min(2, nt_ - 1):
                        flush()
                for idx in range(max(0, nt_ - 2), nt_):
                    emitPV(idx)

            def epi_evac(ob):
                k_ = octr["k"] % 2
                octr["k"] += 1
                evac(Osb[k_][:, :], banks[ob][0:65, :], [rbank[ob]], [r_Osb[k_]], eng="dve")
                return k_

            def epilogue(k_, br, g, first):
                for r in range(4):
                    P.tr(banks[7][:, r * 65:(r + 1) * 65], Osb[k_][0:65, r * 128:(r + 1) * 128], identf[0:65, 0:65],
                         [r_Osb[k_], r_identf], [rbank[7]])
                O3 = banks[7][:, 0:260].rearrange("p (a e) -> p a e", a=4)
                P.g("dve", "tensor_scalar", [rbank[7]], [r_sm], out=sm[:, 0:4], in0=O3[:, :, 64], scalar1=1e-30,
                    scalar2=None, op0=ALU.max)
                P.g("dve", "reciprocal", [r_sm], [r_sm], out=sm[:, 4:8], in_=sm[:, 0:4])
                P.g("dve", "tensor_tensor", [r_sm, cur["r_gates"]], [r_sm], out=sm[:, 8:12], in0=sm[:, 4:8],
                    in1=cur["gates"][:, br * 16 + 4 * g:br * 16 + 4 * g + 4], op=ALU.mult)
                dst = oatt[:, g * 256:(g + 1) * 256].rearrange("p (a e) -> p a e", a=4)
                wb_ = sm[:, 8:12].unsqueeze(2).broadcast_to([128, 4, 64])
                if first:
                    P.g("dve", "tensor_tensor", [rbank[7], r_sm], [r_oatt], out=dst, in0=O3[:, :, 0:64], in1=wb_, op=ALU.mult)
                else:
                    t3 = otmp[:, :].rearrange("p (a e) -> p a e", a=4)
                    P.g("dve", "tensor_tensor", [rbank[7], r_sm], [r_otmp], out=t3, in0=O3[:, :, 0:64], in1=wb_, op=ALU.mult)
                    P.g("dve", "tensor_tensor", [r_otmp, r_oatt], [r_oatt], out=dst, in0=dst, in1=t3, op=ALU.add)

            segs = [(0, 1024, "q"), (0, 48, "gl"), (0, 1024, "za"), (0, 2048, "uv"), (0, 1024, "zb"), (0, 4096, "mg")]
            nblk = int(os.environ.get("KDBG_NB", NOWN))
            for i in range(nblk):
                tok = slice(i * 128, (i + 1) * 128)
                xk = 0
                def load_acts(j):
                    P.dma("sp", ablk2[j % 2][:], ACTS[j * 128:(j + 1) * 128, 0:5120], reads=[rACTS], writes=[r_ablk2[j % 2]])
                    P.dma("sp", gates2[j % 2][:], GATES[j * 128:(j + 1) * 128, :], reads=[rGATES], writes=[r_gates2[j % 2]])
                if i == 0:
                    load_acts(0)
                ablk = ablk2[i % 2]; r_ablk = r_ablk2[i % 2]
                r_qtok = r_za = r_uvg = r_zb = r_ablk
                za = ablk[:, 1024:2048]; zb = ablk[:, 4096:5120]
                cur["gates"] = gates2[i % 2]; cur["r_gates"] = r_gates2[i % 2]
                qb3 = t_qb[i].rearrange("p (g c) -> p g c", g=4)
                for g_ in range(4):
                    pp = slice(64, 73) if g_ % 2 == 0 else slice(0, 9)
                    P.dma("sp", QA[pp, g_, :], qb3[:, g_, :], writes=[r_QA], key="QAqb")
                P.dma("sp", cmt[:], t_cm[i], writes=[r_cmt])
                P.dma("sp", fb2[:], t_fb2[i], writes=[r_fb], key="fb"); P.dma("sp", caus[:], t_caus[i], writes=[r_fb], key="fb")
                bq = nb()
                bbq = bankbf(bq)
                for cc in range(8):
                    P.tr(bbq[:, cc * 128:(cc + 1) * 128], ablk[:, cc * 128:(cc + 1) * 128], identb[:],
                         [r_qtok, r_identb], [rbank[bq]])
                for P2 in range(2):
                    for hf2 in range(2):
                        hs2 = slice(hf2 * 64, hf2 * 64 + 64)
                        evac(QA[hs2, 2 * P2 + hf2, :], bbq[hs2, P2 * 512:(P2 + 1) * 512], [rbank[bq]], [r_QA])
                if i + 1 < nblk:
                    load_acts(i + 1)
                ntc = (32 * i + 31) // 128 + 1
                L = (4 * i + 4) * 128
                T0 = max(0, 4 * i - 4)
                for g in range(4):
                    conv_step2(1)
                    P_, hf = g // 2, g % 2
                    hs = slice(hf * 64, hf * 64 + 64)
                    kb = g % 2
                    po_ = slice(64, 73) if hf == 0 else slice(0, 9)
                    Qop = QA[:, g, :].rearrange("p (a b) -> p a b", a=4)
                    rq = [r_QA]
                    po_ = slice(64, 73) if hf == 0 else slice(0, 9)
                    P.dma("sp", Kbuf[kb][hs, 0:L], KT[4 + P_][hs, 0:L], reads=[rKT], writes=[r_Kbuf[kb]])
                    P.dma("sp", Vbuf[kb][:, 0:L // 128, :], VS[0, g][:, 0:L // 128, :], reads=[rVS], writes=[r_Vbuf[kb]])
                    P.dma("sp", Kwb[kb][hs, 0:L - T0 * 128], KT[6 + P_][hs, T0 * 128:L], reads=[rKT], writes=[r_Kwb[kb]])
                    P.dma("sp", Kwb[kb][po_, 0:L - T0 * 128], t_posk_s[:, T0 * 128:L], writes=[r_Kwpos[kb]])
                    P.dma("sp", Vwb[kb][:, 0:L // 128 - T0, :], VS[1, g][:, T0:L // 128, :], reads=[rVS], writes=[r_Vwb[kb]])
                    bc4 = lambda ap: ap.unsqueeze(1).broadcast_to([128, 4, 128])
                    tiles = []
                    for nt in range(ntc):
                        tiles.append((kcT[:, g, nt * 128:(nt + 1) * 128], [r_kcT],
                                      [(identb[:], bc4(cmt[:, nt, :]), [r_cmt])], vcA[:, nt, g, :], r_vcA))
                    obC = 5 + (octr["k"] % 2)
                    branch(tiles, Qop, rq, obC, save_pt=True)
                    kC = epi_evac(obC)
                    ub = nb()
                    for r in range(4):
                        for nt in range(ntc):
                            P.mm(banks[ub][:, r * 128:(r + 1) * 128], PTc[:, nt, r * 128:(r + 1) * 128], ov[:, nt, :],
                                 nt == 0, nt == ntc - 1, [r_PTc[nt], r_tab], [rbank[ub]])

                    def after_c(kC=kC, g=g, ub=ub):
                        epilogue(kC, 0, g, True)
                        for r in range(4):
                            if r == 0:
                                P.g("dve", "tensor_scalar", [rbank[ub], r_sm], [r_imp], out=imp[:], in0=banks[ub][:, 0:128],
                                    scalar1=sm[:, 4:5], scalar2=None, op0=ALU.mult)
                            else:
                                P.g("dve", "scalar_tensor_tensor", [rbank[ub], r_sm, r_imp], [r_imp], out=imp[:],
                                    in0=banks[ub][:, r * 128:(r + 1) * 128], scalar=sm[:, 4 + r:5 + r], in1=imp[:],
                                    op0=ALU.mult, op1=ALU.add)
                        P.g("dve", "tensor_tensor", [r_imp, r_fb], [r_imp], out=imp[:], in0=imp[:], in1=caus[:], op=ALU.mult)
                        P.g("dve", "tensor_tensor", [r_imp, r_fb], [r_imp], out=imp[:], in0=imp[:], in1=fb2[:], op=ALU.add)
                        P.g("dve", "max", [r_imp], [r_m8], out=m8[:, 0:8], in_=imp[:])
                        P.g("dve", "match_replace", [r_imp, r_m8], [r_sc2], out=sc2[:], in_to_replace=m8[:, 0:8],
                            in_values=imp[:], imm_value=-2.0)
                        P.g("dve", "max", [r_sc2], [r_m8], out=m8[:, 8:16], in_=sc2[:])
                        P.g("dve", "tensor_scalar", [r_m8], [r_m8], out=m8[:, 0:1], in0=m8[:, 15:16], scalar1=-0.5,
                            scalar2=None, op0=ALU.max)
                        P.g("dve", "tensor_scalar", [r_imp, r_m8], [r_sc2], out=sc2[:], in0=imp[:], scalar1=m8[:, 0:1],
                            scalar2=-NEGM, op0=ALU.is_ge, op1=ALU.mult)
                        P.g("dve", "tensor_scalar", [r_sc2], [r_mbq], out=mbq[:], in0=sc2[:], scalar1=NEGM, scalar2=None,
                            op0=ALU.add)
                    pending.append(after_c)
                    tiles = []
                    for rr_ in range(8):
                        T = 4 * i - 4 + rr_
                        if T < 0:
                            continue
                        tiles.append((Kwb[kb][:, (T - T0) * 128:(T - T0 + 1) * 128], [r_Kwb[kb], r_Kwpos[kb]],
                                      [(identb[:], bc4(wm[:, rr_, :]), [])],
                                      Vwb[kb][:, T - T0, :], r_Vwb[kb]))
                    obW = 5 + (octr["k"] % 2)
                    branch(tiles, Qop, rq, obW)
                    flush()
                    kW = epi_evac(obW)
                    mb_ = nb()
                    P.tr(bankbf(mb_)[:, 0:128], mbq[:], identb[:], [r_mbq, r_identb], [rbank[mb_]])
                    evac(MBT[0:64, 0, :], bankbf(mb_)[0:64, 0:128], [rbank[mb_]], [r_MBT], eng="dve")
                    evac(MBT[64:128, 1, :], bankbf(mb_)[64:128, 0:128], [rbank[mb_]], [r_MBT], eng="dve")
                    pending.append(lambda kW=kW, g=g: epilogue(kW, 2, g, False))
                    tiles = []
                    for kt in range(4 * i + 4):
                        ex = [(ind[:, (kt % 32) * 128:(kt % 32 + 1) * 128],
                               MBT[:, kt // 32, :].unsqueeze(1).broadcast_to([128, 4, 128]), [r_MBT])]
                        if kt >= 4 * i:
                            ex.append((identb[:], bc4(tri[:, kt - 4 * i, :]), []))
                        tiles.append((Kbuf[kb][:, kt * 128:(kt + 1) * 128], [r_Kbuf[kb], r_Kpos],
                                      ex, Vbuf[kb][:, kt, :], r_Vbuf[kb]))
                    obS = 5 + (octr["k"] % 2)
                    branch(tiles, Qop, rq, obS)
                    flush()
                    kS = epi_evac(obS)
                    pending.append(lambda kS=kS, g=g: epilogue(kS, 1, g, False))
                flush()
                P.g("dve", "tensor_tensor", [r_oatt, r_za], [r_oab], out=oab[:, 0:1024], in0=oatt[:], in1=za, op=ALU.mult)
                v_ = ablk[:, 3072:4096]
                P.act(f1[:, 0:1024], v_, AF.Copy, [r_uvg], [r_f1, r_sm], accum_out=sm[:, 16:17])
                P.act(f1[:, 1024:2048], v_, AF.Square, [r_uvg], [r_f1, r_sm], accum_out=sm[:, 17:18])
                P.g("dve", "tensor_scalar", [r_sm], [r_sm], out=sm[:, 18:19], in0=sm[:, 16:17], scalar1=1.0 / 1024,
                    scalar2=None, op0=ALU.mult)
                P.g("dve", "tensor_tensor", [r_sm], [r_sm], out=sm[:, 19:20], in0=sm[:, 18:19], in1=sm[:, 18:19], op=ALU.mult)
                P.g("dve", "scalar_tensor_tensor", [r_sm], [r_sm], out=sm[:, 20:21], in0=sm[:, 17:18], scalar=1.0 / 1024,
                    in1=sm[:, 19:20], op0=ALU.mult, op1=ALU.subtract)
                P.act(sm[:, 21:22], sm[:, 20:21], AF.Ln, [r_sm], [r_sm], bias=EPS)
                P.act(sm[:, 22:23], sm[:, 21:22], AF.Exp, [r_sm], [r_sm], scale=-0.5)
                P.g("dve", "tensor_scalar", [r_uvg, r_sm], [r_f1], out=f1[:, 0:1024], in0=v_, scalar1=sm[:, 18:19],
                    scalar2=sm[:, 22:23], op0=ALU.subtract, op1=ALU.mult)
                P.g("dve", "tensor_tensor", [r_f1, r_ln], [r_f1], out=f1[:, 0:1024], in0=f1[:, 0:1024], in1=lng[:], op=ALU.mult)
                P.g("dve", "tensor_tensor", [r_f1, r_ln], [r_b1], out=b1[:, 0:1024], in0=f1[:, 0:1024], in1=lnb[:], op=ALU.add)
                for half in range(2):
                    b = nb()
                    for gq in range(4):
                        G_ = half * 4 + gq
                        P.mm(banks[b][:, gq * 128:(gq + 1) * 128], wsb[:, G_, :], b1[:, G_ * 128:(G_ + 1) * 128], True, True,
                             [r_sg, r_b1], [rbank[b]])
                    P.g("dve", "tensor_tensor", [rbank[b], r_sg], [r_f2], out=f2[:, half * 512:(half + 1) * 512].rearrange("p (a e) -> p a e", a=4),
                        in0=banks[b][:, :].rearrange("p (a e) -> p a e", a=4),
                        in1=bsT[:, half * 4:(half + 1) * 4].unsqueeze(2).broadcast_to([128, 4, 128]), op=ALU.add)
                P.g("dve", "tensor_tensor", [r_f2, r_uvg], [r_f2], out=f2[:, 0:1024], in0=f2[:, 0:1024], in1=ablk[:, 2048:3072], op=ALU.mult)
                P.g("dve", "tensor_tensor", [r_f2, r_zb], [r_oab], out=oab[:, 1024:2048], in0=f2[:, 0:1024], in1=zb, op=ALU.mult)
                P.dma("sp", OAB[tok, :], oab[:], reads=[r_oab], writes=[rOAB], key="OABw")

            for _ in range(100):
                conv_step2(2)
            conv_step2(0)

        P.barrier()
        X1 = dscr("X1", [2048, D], F32); X2 = dscr("X2", [2048, D], F32)
        rX1 = Res("X1"); rX2 = Res("X2")
        with ExitStack() as st:
            TA = sbt(st, "TA", [128, 16, 2048], bf); r_TA = Res("TA")
            MA = sbt(st, "MA", [128, 16, 2048], bf); r_MA = Res("MA")
            big = [sbt(st, f"big{k}", [128, 16, 512], bf) for k in range(2)]; r_big = [Res(f"big{k}") for k in range(2)]
            wsm = [sbt(st, f"wsm{k}", [128, 8, 512], bf) for k in range(2)]; r_wsm = [Res(f"wsm{k}") for k in range(2)]
            plew = sbt(st, "plew", [128, 2, 512], bf); r_plew = Res("plew")
            tA = [sbt(st, f"tA{k}", [128, 512], F32) for k in range(2)]; r_tA = [Res(f"tA{k}") for k in range(2)]
            tB = [sbt(st, f"tB{k}", [128, 512], F32) for k in range(2)]; r_tB = [Res(f"tB{k}") for k in range(2)]
            tX = [sbt(st, f"tX{k}", [128, 512], F32) for k in range(2)]; r_tX = [Res(f"tX{k}") for k in range(2)]
            tY = [sbt(st, f"tY{k}", [128, 512], F32) for k in range(2)]; r_tY = [Res(f"tY{k}") for k in range(2)]
            pT3 = wsm[0][:].rearrange("p a b -> p (a b)").rearrange("p (c t) -> p c t", c=2); r_pT3 = r_wsm[0]
            ptk3 = sbt(st, "ptk3", [128, 256], F32); pbk3 = sbt(st, "pbk3", [128, 256], bf); r_p3 = Res("p3")
            jk3 = sbt(st, "jk3", [128, 512], bf); r_jk3 = Res("jk3")

            def tr_tiles(src_fn, r_src, i):
                for half in range(2):
                    b = nb()
                    bb = bankbf(b)
                    for c8 in range(8):
                        P.tr(bb[:, c8 * 128:(c8 + 1) * 128], src_fn(half * 8 + c8), identb[:], [r_src, r_identb], [rbank[b]])
                    evac(TA[:, half * 8:(half + 1) * 8, i * 128:(i + 1) * 128], bb.rearrange("p (c t) -> p c t", c=8),
                         [rbank[b]], [r_TA])

            for i in range(nblk):
                k = i % 2
                ldv = big[k][:, 0:4, :]
                P.dma("sp", ldv, OAB[i * 128:(i + 1) * 128, :].rearrange("p (a b) -> p a b", a=4), reads=[rOAB], writes=[r_big[k]])
                tr_tiles(lambda c, k=k: big[k][:, c // 4, (c % 4) * 128:(c % 4 + 1) * 128], r_big[k], i)
            acts3 = ACTS.rearrange("(t p) c -> p t c", p=128)
            for cc in range(4):
                P.dma("sp", big[0][:, 0:nblk, :], acts3[:, 0:nblk, 5120 + cc * 512:5120 + (cc + 1) * 512], reads=[rACTS], writes=[r_big[0]])
                P.dma("sp", big[1][:, 0:nblk, :], acts3[:, 0:nblk, 7168 + cc * 512:7168 + (cc + 1) * 512], reads=[rACTS], writes=[r_big[1]])
                P.dma("sp", wsm[0][:], WT["upa"][cc], reads=[r_WT], writes=[r_wsm[0]])
                P.dma("sp", wsm[1][:], WT["upb"][cc], reads=[r_WT], writes=[r_wsm[1]])
                for i in range(nblk):
                    k = i % 2
                    bA = nb()
                    for ck in range(8):
                        P.mm(banks[bA][:, :], TA[:, ck, i * 128:(i + 1) * 128], wsm[0][:, ck, :], ck == 0, ck == 7, [r_TA, r_wsm[0]], [rbank[bA]])
                    bB = nb()
                    for ck in range(8):
                        P.mm(banks[bB][:, :], TA[:, 8 + ck, i * 128:(i + 1) * 128], wsm[1][:, ck, :], ck == 0, ck == 7, [r_TA, r_wsm[1]], [rbank[bB]])
                    P.g("dve", "tensor_tensor", [rbank[bA], r_big[0]], [r_tA[k]], out=tA[k][:], in0=banks[bA][:, :], in1=big[0][:, i, :], op=ALU.mult)
                    P.g("dve", "tensor_tensor", [rbank[bB], r_big[1]], [r_tB[k]], out=tB[k][:], in0=banks[bB][:, :], in1=big[1][:, i, :], op=ALU.mult)
                    P.g("dve", "tensor_tensor", [r_tA[k], r_tB[k]], [r_MA], out=MA[:, i, cc * 512:(cc + 1) * 512], in0=tA[k][:], in1=tB[k][:], op=ALU.add)
            for i in range(nblk):
                tr_tiles(lambda c, i=i: MA[:, i, c * 128:(c + 1) * 128], r_MA, i)
            for cc in range(4):
                wk = cc % 2
                P.dma("sp", big[wk][:], WT["out"][cc], reads=[r_WT], writes=[r_big[wk]])
                for i in range(nblk):
                    k = i % 2
                    b = nb()
                    for ck in range(16):
                        P.mm(banks[b][:, :], TA[:, ck, i * 128:(i + 1) * 128], big[wk][:, ck, :], ck == 0, ck == 15, [r_TA, r_big[wk]], [rbank[b]])
                    P.dma("sp", tX[k][:], xo[i * 128:(i + 1) * 128, cc * 512:(cc + 1) * 512], writes=[r_tX[k]])
                    P.g("dve", "tensor_tensor", [rbank[b], r_tX[k]], [r_tY[k]], out=tY[k][:], in0=banks[b][:, :], in1=tX[k][:], op=ALU.add)
                    P.dma("sp", X1[i * 128:(i + 1) * 128, cc * 512:(cc + 1) * 512], tY[k][:], reads=[r_tY[k]], writes=[rX1], key="X1w")
                    P.act(MA[:, i, cc * 512:(cc + 1) * 512], tY[k][:], AF.Copy, [r_tY[k]], [r_MA])
            for i in range(nblk):
                tr_tiles(lambda c, i=i: MA[:, i, c * 128:(c + 1) * 128], r_MA, i)
            for i in range(nblk):
                P.dma("sp", ptk3[:], po[i * 128:(i + 1) * 128, :], writes=[r_p3])
                P.act(pbk3[:], ptk3[:], AF.Copy, [r_p3], [r_p3])
                b = nb()
                bb = bankbf(b)
                for c2 in range(2):
                    P.tr(bb[:, c2 * 128:(c2 + 1) * 128], pbk3[:, c2 * 128:(c2 + 1) * 128], identb[:], [r_p3, r_identb], [rbank[b]])
                evac(pT3[:, :, i * 128:(i + 1) * 128], bb[:, 0:256].rearrange("p (c t) -> p c t", c=2), [rbank[b]], [r_pT3])
            for cc in range(4):
                wk = cc % 2
                P.dma("sp", big[wk][:], WT["pg"][cc], reads=[r_WT], writes=[r_big[wk]])
                P.dma("sp", plew[:], WT["ple"][cc], reads=[r_WT], writes=[r_plew])
                for i in range(nblk):
                    k = i % 2
                    bG = nb()
                    for ck in range(16):
                        P.mm(banks[bG][:, :], TA[:, ck, i * 128:(i + 1) * 128], big[wk][:, ck, :], ck == 0, ck == 15, [r_TA, r_big[wk]], [rbank[bG]])
                    P.act(tA[k][:], banks[bG][:, :], AF.Sigmoid, [rbank[bG]], [r_tA[k]])
                    bP = nb()
                    for ck in range(2):
                        P.mm(banks[bP][:, :], pT3[:, ck, i * 128:(i + 1) * 128], plew[:, ck, :], ck == 0, ck == 1, [r_pT3, r_plew], [rbank[bP]])
                    P.dma("sp", tX[k][:], X1[i * 128:(i + 1) * 128, cc * 512:(cc + 1) * 512], reads=[rX1], writes=[r_tX[k]])
                    P.g("dve", "tensor_tensor", [rbank[bP], r_tA[k]], [r_tB[k]], out=tB[k][:], in0=banks[bP][:, :], in1=tA[k][:], op=ALU.mult)
                    P.g("dve", "tensor_tensor", [r_tB[k], r_tX[k]], [r_tY[k]], out=tY[k][:], in0=tB[k][:], in1=tX[k][:], op=ALU.add)
                    P.act(jk3[:], tY[k][:], AF.Square, [r_tY[k]], [r_jk3, r_ssum], accum_out=ssum[:, i * 4 + cc:i * 4 + cc + 1])
                    P.dma("sp", X2[i * 128:(i + 1) * 128, cc * 512:(cc + 1) * 512], tY[k][:], reads=[r_tY[k]], writes=[rX2], key="X2w")
        P.barrier()
        with ExitStack() as st:
            fgt = sbt(st, "fgt", [128, D], F32); r_fgt = Res("fgt")
            P.dma("sp", fgt[:], final_g, writes=[r_fgt])
            xt3 = [sbt(st, f"xt3{k}", [128, D], F32) for k in range(2)]; r_xt3 = [Res(f"xt3{k}") for k in range(2)]
            ot3 = [sbt(st, f"ot3{k}", [128, D], F32) for k in range(2)]; r_ot3 = [Res(f"ot3{k}") for k in range(2)]
            st3 = sbt(st, "st3", [128, 8], F32); r_st3 = Res("st3")
            for i in range(nblk):
                k = i % 2
                P.dma("sp", xt3[k][:], X2[i * 128:(i + 1) * 128, :], reads=[rX2], writes=[r_xt3[k]])
                P.g("dve", "tensor_tensor", [r_ssum], [r_st3], out=st3[:, 0:2], in0=ssum[:, i * 4:i * 4 + 2], in1=ssum[:, i * 4 + 2:i * 4 + 4], op=ALU.add)
                P.g("dve", "tensor_tensor", [r_st3], [r_st3], out=st3[:, 2:3], in0=st3[:, 0:1], in1=st3[:, 1:2], op=ALU.add)
                P.act(st3[:, 3:4], st3[:, 2:3], AF.Ln, [r_st3], [r_st3], scale=1.0 / D, bias=EPS)
                P.act(st3[:, 4:5], st3[:, 3:4], AF.Exp, [r_st3], [r_st3], scale=-0.5)
                P.g("dve", "scalar_tensor_tensor", [r_xt3[k], r_st3, r_fgt], [r_ot3[k]], out=ot3[k][:], in0=xt3[k][:], scalar=st3[:, 4:5],
                    in1=fgt[:], op0=ALU.mult, op1=ALU.mult)
                P.dma("sp", out[i * 128:(i + 1) * 128, :], ot3[k][:], reads=[r_ot3[k]], writes=[rOUT], key="outw")
        P.finalize()
        P.emit()
    return nc


def _make_w_own(w):
    qcols = []
    for P_ in range(2):
        for r in range(4):
            for hd in (8 * P_ + r, 8 * P_ + 4 + r):
                qcols.extend(range(hd * 64, hd * 64 + 64))
    return np.ascontiguousarray(np.concatenate([w[:, qcols], w[:, 2560:]], axis=1))


def _prep_inputs(inputs):
    f = lambda a: np.ascontiguousarray(np.asarray(a, dtype=np.float32))
    x = f(inputs["x"]); p = f(inputs["p"])[0]
    com = _common_tables()
    shared = {
        "norm_g": np.ascontiguousarray(np.broadcast_to(f(inputs["norm_g"])[0][None, :], (128, D))),
        "final_g": np.ascontiguousarray(np.broadcast_to(f(inputs["final_g"])[None, :], (128, D))),
        "w_in": f(inputs["w_in"])[0],
        "w_own": _make_w_own(f(inputs["w_in"])[0]),
        "pe_k": np.ascontiguousarray(f(inputs["cmp_pe_k"])[0].T), "w1_k": np.ascontiguousarray(f(inputs["cmp_w1_k"])[0].transpose(1, 0, 2)), "w2_k": f(inputs["cmp_w2_k"])[0],
        "pe_v": np.ascontiguousarray(f(inputs["cmp_pe_v"])[0].T), "w1_v": np.ascontiguousarray(f(inputs["cmp_w1_v"])[0].transpose(1, 0, 2)), "w2_v": f(inputs["cmp_w2_v"])[0],
        "ln_g": np.ascontiguousarray(np.broadcast_to(f(inputs["ln_v_g"])[0][None, :], (128, 1024))),
        "ln_b": np.ascontiguousarray(np.broadcast_to(f(inputs["ln_v_b"])[0][None, :], (128, 1024))),
        "sgu_wT": np.ascontiguousarray(f(inputs["sgu_w"])[0].transpose(2, 0, 1)),
        "sgu_bT": np.ascontiguousarray(f(inputs["sgu_b"])[0].T),
        "w_up_a": f(inputs["w_up_a"])[0], "w_up_b": f(inputs["w_up_b"])[0],
        "w_out": f(inputs["w_out"])[0], "w_ple": f(inputs["w_ple"])[0], "w_pg": f(inputs["w_ple_gate"])[0],
    }
    shared.update(com)
    in_maps = []
    for core in range(8):
        b, c = core // 4, core % 4
        m = dict(shared)
        m["xb"] = x[b]
        xr = x[b].reshape(16, 4, 128, D)[:, c].reshape(2048, D)
        m["xo"] = np.ascontiguousarray(xr)
        m["po"] = np.ascontiguousarray(p[b].reshape(16, 4, 128, 256)[:, c].reshape(2048, 256))
        m.update(_core_tables(c))
        in_maps.append(m)
    return in_maps


def kernel(**inputs):
    in_maps = _prep_inputs(inputs)
    nc = build()
    res = run_bass_kernel_spmd(nc, in_maps, core_ids=list(range(8)))
    outp = np.zeros((2, S, D), np.float32)
    o = outp.reshape(2, 16, 4, 128, D)
    for core in range(8):
        b, c = core // 4, core % 4
        o[b, :, c] = res.results[core]["out"].reshape(16, 128, D)
    return outp
```

```python
import os
import numpy as np
from contextlib import ExitStack
import ml_dtypes
import concourse.bass as bass
import concourse.mybir as mybir
from concourse.bass_utils import run_bass_kernel_spmd

F32 = mybir.dt.float32
BF16 = mybir.dt.bfloat16
AF = mybir.ActivationFunctionType
ALU = mybir.AluOpType

S = 8192
D = 2048
NT = 64
NOWN = 16
NEGM = -30000.0
EPS = 1e-6
CW = 512
SKIP = os.environ.get('KDBG_SKIP', '')


class Res:
    __slots__ = ("name", "w", "r")

    def __init__(self, name):
        self.name = name
        self.w = None
        self.r = []


class Op:
    __slots__ = ("eng", "fn", "deps", "inc", "sem", "val", "dma")


class Prog:
    ENGS = ["pe", "act", "dve", "pool", "sp"]

    def __init__(self, nc, stack):
        self.nc = nc
        self.stack = stack
        self.ops = {e: [] for e in self.ENGS}
        self.dma_sems = {}
        self.nsem = 0
        self.final = []

    def newsem(self, name):
        self.nsem += 1
        return self.stack.enter_context(self.nc.semaphore(f"{name}{self.nsem}"))

    def _add(self, eng, fn, reads, writes, dma_key=None):
        op = Op()
        op.eng = eng
        op.fn = fn
        op.inc = False
        op.sem = None
        op.val = 0
        op.dma = dma_key
        writes = list(writes) + [r for r in reads if r.name.startswith("bank") and r not in writes]
        deps = []
        for r in reads:
            if r.w is not None:
                deps.append(r.w)
        for w in writes:
            if w.w is not None:
                deps.append(w.w)
            deps.extend(w.r)
        op.deps = [d for d in deps
                   if not (d.eng == "pe" and eng == "pe" and d.dma is None and dma_key is None)]
        for r in reads:
            r.r.append(op)
        for w in writes:
            w.w = op
            w.r = []
        self.ops[eng].append(op)
        return op

    def g(self, eng, name, reads, writes, *args, **kw):
        return self._add(eng, lambda e: getattr(e, name)(*args, **kw), reads, writes)

    def mm(self, out, lhsT, rhs, start, stop, reads, writes):
        return self._add("pe", lambda e: e.matmul(out, lhsT=lhsT, rhs=rhs, start=start, stop=stop),
                         reads, writes)

    def tr(self, out, in_, ident, reads, writes):
        return self._add("pe", lambda e: e.transpose(out=out, in_=in_, identity=ident), reads, writes)

    def act(self, out, in_, func, reads, writes, eng="act", **kw):
        return self._add(eng, lambda e: e.activation(out=out, in_=in_, func=func, **kw), reads, writes)

    def dma(self, eng, out, in_, reads=(), writes=(), key=None):
        if key is None:
            key = (writes[0].name if writes else reads[0].name)
        return self._add(eng, lambda e: e.dma_start(out=out, in_=in_), reads, writes, dma_key=key)

    def barrier(self):
        deps = []
        for e in self.ENGS:
            for o in reversed(self.ops[e]):
                if o.fn is not None and o.dma is None:
                    deps.append(o)
                    break
        deps += getattr(self, "dma_pending", [])
        self.dma_pending = []
        for e in self.ENGS:
            op = Op()
            op.eng = e
            op.fn = None
            op.inc = False
            op.sem = None
            op.val = 0
            op.dma = None
            op.deps = list(deps)
            self.ops[e].append(op)

    def finalize(self):
        allops = [o for e in self.ENGS for o in self.ops[e]]
        for o in allops:
            for d in o.deps:
                d.inc = True
        fin = [r.w for r in self.final if r.w is not None]
        for o in fin:
            o.inc = True
        MAXV = 30000
        order = {}
        for e in self.ENGS:
            sem = None
            cnt = 0
            for o in self.ops[e]:
                if o.dma is not None:
                    continue
                if o.inc:
                    if sem is None or cnt >= MAXV:
                        sem = self.newsem("c" + e)
                        cnt = 0
                    cnt += 1
                    o.sem, o.val = sem, cnt
        for o in self.dma_order:
            if o.dma not in self.dma_sems:
                self.dma_sems[o.dma] = [self.newsem("d"), 0]
            ent = self.dma_sems[o.dma]
            if ent[1] + 16 > MAXV:
                ent[0] = self.newsem("d")
                ent[1] = 0
            ent[1] += 16
            o.sem, o.val = ent[0], ent[1]
            o.inc = True
        self.fin_ops = fin

    def emit(self):
        nc = self.nc

        def run(ename, eng):
            known = {}
            for o in self.ops[ename]:
                for d in o.deps:
                    k = id(d.sem)
                    if known.get(k, 0) < d.val:
                        eng.wait_ge(d.sem, d.val)
                        known[k] = d.val
                if o.fn is None:
                    continue
                ins = o.fn(eng)
                if o.inc:
                    ins.then_inc(o.sem, 16 if o.dma is not None else 1)
            if ename == "sp":
                for d in self.fin_ops:
                    k = id(d.sem)
                    if known.get(k, 0) < d.val:
                        eng.wait_ge(d.sem, d.val)
                        known[k] = d.val

        with nc.Block() as block:
            @block.tensor
            def _(e):
                run("pe", e)

            @block.scalar
            def _(e):
                run("act", e)

            @block.vector
            def _(e):
                run("dve", e)

            @block.gpsimd
            def _(e):
                run("pool", e)

            @block.sync
            def _(e):
                run("sp", e)


_orig_add = Prog._add


def _add_wrapped(self, eng, fn, reads, writes, dma_key=None):
    op = _orig_add(self, eng, fn, reads, writes, dma_key)
    if dma_key is not None:
        if not hasattr(self, "dma_order"):
            self.dma_order = []
        self.dma_order.append(op)
        if not hasattr(self, "dma_pending"):
            self.dma_pending = []
        self.dma_pending.append(op)
    return op


Prog._add = _add_wrapped


def _split3(a):
    a = np.asarray(a, np.float64)
    hi = a.astype(np.float32).astype(ml_dtypes.bfloat16)
    r1 = a - hi.astype(np.float64)
    mid = r1.astype(np.float32).astype(ml_dtypes.bfloat16)
    r2 = r1 - mid.astype(np.float64)
    lo = r2.astype(np.float32).astype(ml_dtypes.bfloat16)
    return hi, mid, lo


def _common_tables():
    bf = ml_dtypes.bfloat16
    t = {}
    pos = np.arange(S)
    pk = np.zeros((9, S), np.float32)
    pk[0:3] = 1.0
    pk[3:6] = 128.0 * (pos // 128)
    pk[6:9] = pos % 128
    t["posk_s"] = pk.astype(bf)
    n = np.arange(512)
    ce = 16 * n + 31
    pc = np.zeros((9, 512), np.float32)
    pc[0:3] = 1.0
    pc[3:6] = 128.0 * (ce // 128)
    pc[6:9] = ce % 128
    t["posk_c"] = pc.astype(bf)
    ng = (np.arange(4)[None, :, None] * 128 + np.arange(128)[:, None, None])
    jj = np.arange(128)[None, None, :]
    t["ov"] = ((ng >= 4 * jj - 1) & (ng <= 4 * jj + 3)).astype(np.float32).astype(bf)
    t["ind"] = ((np.arange(128)[:, None] % 64) == (np.arange(4096)[None, :] // 64)).astype(np.float32).astype(bf)
    t["identb"] = np.eye(128, dtype=np.float32).astype(bf)
    t["identf"] = np.eye(128, dtype=np.float32)
    t["trilT"] = (np.arange(128)[:, None] <= np.arange(128)[None, :]).astype(np.float32)
    return t


def _core_tables(c):
    bf = ml_dtypes.bfloat16
    t = {}
    h = np.arange(16)
    slopes = np.power(2.0, -8.0 * (h + 1) / 16.0)
    slopes = np.power(np.float32(2.0), (-8.0 * (h + 1).astype(np.float32) / 16)).astype(np.float32).astype(np.float64)
    s_hi, s_mid, s_lo = _split3(slopes)
    qb = np.zeros((NOWN, 9, 16, 128), bf)
    tq = np.arange(128)
    for i in range(NOWN):
        j = 4 * i + c
        tt = 128 * j + tq
        A = -slopes[:, None] * tt[None, :].astype(np.float64)
        a_hi, a_mid, a_lo = _split3(A)
        qb[i, 0], qb[i, 1], qb[i, 2] = a_hi, a_mid, a_lo
        for k, sv in enumerate((s_hi, s_mid, s_lo)):
            qb[i, 3 + k] = np.broadcast_to(sv[:, None], (16, 128))
            qb[i, 6 + k] = np.broadcast_to(sv[:, None], (16, 128))
    t["qb"] = qb.reshape(NOWN, 9, 2048)
    cm = np.zeros((NOWN, 128, 4, 128), np.float32)
    fb2 = np.zeros((NOWN, 128, 128), np.float32)
    caus = np.zeros((NOWN, 128, 128), np.float32)
    blk = np.arange(128)
    for i in range(NOWN):
        j = 4 * i + c
        tt = 128 * j + tq
        ng = np.arange(4)[None, :, None] * 128 + np.arange(128)[:, None, None]
        ok = (16 * ng + 31 <= tt[None, None, :]) & (ng <= 510)
        cm[i] = np.where(ok, 0.0, NEGM)
        cur = tt // 64
        forced = (blk[None, :] == 0) | (blk[None, :] == cur[:, None]) | (blk[None, :] == cur[:, None] - 1)
        cz = blk[None, :] * 64 <= tt[:, None]
        fb2[i] = np.where(cz, np.where(forced, 1000.0, 0.0), -1.0)
        caus[i] = cz.astype(np.float32)
    t["cm"] = cm.astype(bf)
    t["fb2"] = fb2
    t["caus"] = caus
    k = np.arange(128)
    tri = np.zeros((128, 4, 128), np.float32)
    for r in range(4):
        d = 128 * (c - r) + tq[None, :] - k[:, None]
        tri[:, r, :] = np.where(d >= 0, 0.0, NEGM)
    t["tri"] = tri.astype(bf)
    wm = np.zeros((128, 8, 128), np.float32)
    for r in range(8):
        d = 128 * (c + 4 - r) + tq[None, :] - k[:, None]
        wm[:, r, :] = np.where((d >= 0) & (d < 512), 0.0, NEGM)
    t["wm"] = wm.astype(bf)
    return t


def build(debug=()):
    nc = bass.Bass("TRN2", target_bir_lowering=False)
    bf = BF16

    def din(name, shape, dt=F32):
        return nc.dram_tensor(name, list(shape), dt, kind="ExternalInput").ap()

    def dscr(name, shape, dt):
        kind = "ExternalOutput" if name in debug else "Internal"
        return nc.dram_tensor(name, list(shape), dt, kind=kind).ap()

    xb = din("xb", [S, D])
    xo = din("xo", [2048, D])
    po = din("po", [2048, 256])
    norm_g = din("norm_g", [128, D])
    final_g = din("final_g", [128, D])
    w_in = din("w_in", [D, 10800])
    w_own = din("w_own", [D, 9264])
    pe_k = din("pe_k", [64, 32]); w1_k = din("w1_k", [64, 32, 64]); w2_k = din("w2_k", [64, 64])
    pe_v = din("pe_v", [64, 32]); w1_v = din("w1_v", [64, 32, 64]); w2_v = din("w2_v", [64, 64])
    ln_g = din("ln_g", [128, 1024]); ln_b = din("ln_b", [128, 1024])
    sgu_wT = din("sgu_wT", [128, 8, 128])
    sgu_bT = din("sgu_bT", [128, 8])
    w_up_a = din("w_up_a", [1024, D]); w_up_b = din("w_up_b", [1024, D])
    w_out = din("w_out", [D, D]); w_ple = din("w_ple", [256, D]); w_pg = din("w_pg", [D, D])
    t_posk_s = din("posk_s", [9, S], bf); t_posk_c = din("posk_c", [9, 512], bf)
    t_ov = din("ov", [128, 4, 128], bf); t_ind = din("ind", [128, 4096], bf)
    t_identb = din("identb", [128, 128], bf); t_identf = din("identf", [128, 128])
    t_trilT = din("trilT", [128, 128])
    t_qb = din("qb", [NOWN, 9, 2048], bf); t_cm = din("cm", [NOWN, 128, 4, 128], bf)
    t_fb2 = din("fb2", [NOWN, 128, 128]); t_caus = din("caus", [NOWN, 128, 128])
    t_tri = din("tri", [128, 4, 128], bf); t_wm = din("wm", [128, 8, 128], bf)

    out = nc.dram_tensor("out", [2048, D], F32, kind="ExternalOutput").ap()

    KT = dscr("KT", [8, 128, S + 128], bf)
    VS = dscr("VS", [2, 4, 128, NT, 65], bf)
    rKT = Res("KT"); rVS = Res("VS")
    rOUT = Res("OUT")

    with ExitStack() as top:
        P = Prog(nc, top)
        P.final.append(rOUT)
        for nm in debug:
            pass

        def sbt(st, name, shape, dt):
            return st.enter_context(nc.sbuf_tensor("s_" + name, list(shape), dt))

        banks = [top.enter_context(nc.psum_tensor(f"bank{k}", [128, 512], F32)) for k in range(8)]
        rbank = [Res(f"bank{k}") for k in range(8)]
        bstate = {"k": 0}

        def nb():
            k = bstate["k"]
            bstate["k"] = (k + 1) % 5
            return k

        def bankbf(k):
            return banks[k][:].bitcast(bf)

        identb = sbt(top, "identb", [128, 128], bf); r_identb = Res("identb")
        identf = sbt(top, "identf", [128, 128], F32); r_identf = Res("identf")
        P.dma("sp", identb[:], t_identb, writes=[r_identb])
        P.dma("sp", identf[:], t_identf, writes=[r_identf])
        kcT = sbt(top, "kcA", [128, 4, 512], bf); r_kcT = Res("kcA")
        vcA = sbt(top, "vcA", [128, 4, 4, 65], bf); r_vcA = Res("vcA")

        evac_rr = {"k": 0}

        def evac(out_ap, in_ap, reads, writes, eng=None, func=AF.Copy, **kw):
            if eng is None:
                eng = "act" if (evac_rr["k"] % 2 == 0) else "dve"
                evac_rr["k"] += 1
            if eng == "act" or func != AF.Copy or kw:
                return P.act(out_ap, in_ap, func, reads, writes, **kw)
            return P.g("dve", "tensor_copy", reads, writes, out=out_ap, in_=in_ap)

        def make_norm(st, gsrc, tag, nx=2, junk_=None, xn_=None):
            gt = sbt(st, "gt" + tag, [128, D], F32); r_gt = Res("gt" + tag)
            P.dma("sp", gt[:], gsrc, writes=[r_gt])
            xts = [sbt(st, f"xt{tag}{k}", [128, D], F32) for k in range(nx)]
            r_xts = [Res(f"xt{tag}{k}") for k in range(nx)]
            junk, r_junk = junk_ if junk_ else (sbt(st, "junk" + tag, [128, D], bf), Res("junk" + tag))
            xn, r_xn = xn_ if xn_ else (sbt(st, "xn" + tag, [128, D], bf), Res("xn" + tag))
            stat = sbt(st, "stat" + tag, [128, 4], F32); r_stat = Res("stat" + tag)
            return dict(gt=gt, r_gt=r_gt, xts=xts, r_xts=r_xts, junk=junk, r_junk=r_junk, xn=xn,
                        r_xn=r_xn, stat=stat, r_stat=r_stat)

        def norm_load(N, k, src):
            P.dma("sp", N["xts"][k][:], src, writes=[N["r_xts"][k]])

        def norm_compute(N, k, dst_fn, r_dst):
            norm_stats(N, k)
            norm_tr(N, dst_fn, r_dst)

        def norm_stats(N, k):
            xt = N["xts"][k]; rxt = N["r_xts"][k]
            stat = N["stat"]; rs = N["r_stat"]
            P.act(N["junk"][:], xt[:], AF.Square, [rxt], [N["r_junk"], rs], accum_out=stat[:, 0:1])
            P.act(stat[:, 1:2], stat[:, 0:1], AF.Ln, [rs], [rs], scale=1.0 / D, bias=EPS)
            P.act(stat[:, 2:3], stat[:, 1:2], AF.Exp, [rs], [rs], scale=-0.5)
            P.g("dve", "scalar_tensor_tensor", [rxt, rs, N["r_gt"]], [N["r_xn"]], out=N["xn"][:], in0=xt[:],
                scalar=stat[:, 2:3], in1=N["gt"][:], op0=ALU.mult, op1=ALU.mult)

        def norm_tr(N, dst_fn, r_dst):
            for half in range(2):
                b = nb()
                bb = bankbf(b)
                for cc in range(8):
                    ck = half * 8 + cc
                    P.tr(bb[:, cc * 128:(cc + 1) * 128], N["xn"][:, ck * 128:(ck + 1) * 128], identb[:],
                         [N["r_xn"], r_identb], [rbank[b]])
                evac(dst_fn(half), bb.rearrange("p (c t) -> p c t", c=8), [rbank[b]], [r_dst])

        WT = {}
        r_WT = Res("WTscr")
        wlist = [("q", w_own, 0, 1024, 16), ("gl", w_own, 1024, 48, 16), ("za", w_own, 1072, 1024, 16),
                 ("uv", w_own, 2096, 2048, 16), ("zb", w_own, 4144, 1024, 16), ("mg", w_own, 5168, 4096, 16),
                 ("upa", w_up_a, 0, 2048, 8), ("upb", w_up_b, 0, 2048, 8), ("out", w_out, 0, 2048, 16),
                 ("pg", w_pg, 0, 2048, 16), ("ple", w_ple, 0, 2048, 2)]
        def conv_units(st):
            cst = [sbt(st, f"cst{k}", [128, 2048], F32) for k in range(4)]; r_cst = [Res(f"cst{k}") for k in range(4)]
            cbf = [sbt(st, f"cbf{k}", [128, 2048], bf) for k in range(4)]; r_cbf = [Res(f"cbf{k}") for k in range(4)]
            cc_ = 0
            for (nm, src, col0, ncol, KC) in wlist:
                scr = WT[nm]
                for kc in range(KC):
                    for c0 in range(0, ncol, 2048):
                        w_ = min(2048, ncol - c0)
                        k_ = cc_ % 4
                        cc_ += 1
                        P.dma("sp", cst[k_][:, 0:w_], src[kc * 128:(kc + 1) * 128, col0 + c0:col0 + c0 + w_], writes=[r_cst[k_]])
                        ceng = "pool" if cc_ % 3 == 0 else "dve"
                        P.g(ceng, "tensor_copy", [r_cst[k_]], [r_cbf[k_]], out=cbf[k_][:, 0:w_], in_=cst[k_][:, 0:w_])

                        def store(scr=scr, c0=c0, w_=w_, kc=kc, k_=k_):
                            if w_ % CW == 0:
                                P.dma("sp", scr[c0 // CW:(c0 + w_) // CW, :, kc, :].rearrange("c p w -> p c w"),
                                      cbf[k_][:, 0:w_].rearrange("p (c w) -> p c w", w=CW), reads=[r_cbf[k_]], writes=[r_WT], key="WTw")
                            else:
                                P.dma("sp", scr[c0 // CW, :, kc, 0:w_], cbf[k_][:, 0:w_], reads=[r_cbf[k_]], writes=[r_WT], key="WTw")
                        yield store

        for (nm, src, col0, ncol, KC) in wlist:
            WT[nm] = dscr("wb_" + nm, [(ncol + CW - 1) // CW, 128, KC, CW], bf)
        with ExitStack() as st:
            N = make_norm(st, norm_g, "a")
            wkv = sbt(st, "wkv", [128, 16, 1536], bf); r_wkv = Res("wkv")
            wst = [sbt(st, f"wsta{k}", [128, 1536], F32) for k in range(2)]
            r_wst = [Res(f"wsta{k}") for k in range(2)]
            for ck in range(16):
                P.dma("sp", wst[ck % 2][:], w_in[ck * 128:(ck + 1) * 128, 1024:2560], writes=[r_wst[ck % 2]])
                P.g("pool", "tensor_copy", [r_wst[ck % 2]], [r_wkv], out=wkv[:, ck, :], in_=wst[ck % 2][:])
            hT = [sbt(st, f"hTa{k}", [128, 16, 128], bf) for k in range(2)]
            r_hT = [Res(f"hTa{k}") for k in range(2)]
            kvtok = sbt(st, "kvtok", [128, 1024], bf); r_kvtok = Res("kvtok")
            KTst = [sbt(st, f"KTst{k}", [128, 8, 512], bf) for k in range(2)]
            r_KTst = [Res(f"KTst{k}") for k in range(2)]
            Vst = [sbt(st, f"Vst{k}", [128, 2, 4, 4, 65], bf) for k in range(2)]
            r_Vst = [Res(f"Vst{k}") for k in range(2)]
            for k in range(2):
                P.g("pool", "memset", [], [r_Vst[k]], Vst[k][:], 1.0)
            cgen = conv_units(st)
            cpend = []

            def conv_step(n):
                while cpend:
                    cpend.pop(0)()
                for _ in range(n):
                    try:
                        cpend.append(next(cgen))
                    except StopIteration:
                        break
            norm_load(N, 0, xb[0:128, :])
            kvst = []
            norm_load(N, 1, xb[128:256, :])
            norm_compute(N, 0, lambda half: hT[0][:, half * 8:(half + 1) * 8, :], r_hT[0])
            for T in range(NT):
                grp, tt = T // 4, T % 4
                sk = grp % 2
                if T + 2 < NT:
                    norm_load(N, T % 2, xb[(T + 2) * 128:(T + 3) * 128, :])
                while kvst:
                    kvst.pop(0)()
                conv_step(3)
                hk = T % 2
                if T + 1 < NT:
                    norm_stats(N, (T + 1) % 2)
                bks = [nb() for _ in range(3)]
                for cb in range(3):
                    for ck in range(16):
                        if 'm' in SKIP:
                            break
                        P.mm(banks[bks[cb]][:, :], hT[hk][:, ck, :], wkv[:, ck, cb * 512:(cb + 1) * 512],
                             ck == 0, ck == 15, [r_hT[hk], r_wkv], [rbank[bks[cb]]])
                if T + 1 < NT:
                    norm_tr(N, lambda half, hk2=(T + 1) % 2: hT[hk2][:, half * 8:(half + 1) * 8, :], r_hT[(T + 1) % 2])
                EV = os.environ.get('KDBG_EV', '12345')
                if '1' in EV:
                    evac(kvtok[:, 0:512], banks[bks[0]][:, :], [rbank[bks[0]]], [r_kvtok], eng="act")
                if '2' in EV:
                    evac(kvtok[:, 512:768], banks[bks[1]][:, 0:256], [rbank[bks[1]]], [r_kvtok], eng="dve")
                if '3' in EV:
                    evac(Vst[sk][:, 0, tt, :, 0:64], banks[bks[1]][:, 256:512].rearrange("p (g e) -> p g e", g=4),
                         [rbank[bks[1]]], [r_Vst[sk]], eng="dve")
                if '4' in EV:
                    evac(kvtok[:, 768:1024], banks[bks[2]][:, 0:256], [rbank[bks[2]]], [r_kvtok], eng="act")
                if '5' in EV:
                    evac(Vst[sk][:, 1, tt, :, 0:64], banks[bks[2]][:, 256:512].rearrange("p (g e) -> p g e", g=4),
                         [rbank[bks[2]]], [r_Vst[sk]], eng="dve")
                if 't' in SKIP:
                    continue
                b = nb()
                bb = bankbf(b)
                for m in range(8):
                    P.tr(bb[:, m * 128:(m + 1) * 128], kvtok[:, m * 128:(m + 1) * 128], identb[:],
                         [r_kvtok, r_identb], [rbank[b]])
                evac(KTst[sk][:, :, tt * 128:(tt + 1) * 128], bb.rearrange("p (m t) -> p m t", m=8),
                     [rbank[b]], [r_KTst[sk]])
                if tt == 3 and 's' not in SKIP:
                    def kv_store(grp=grp, sk=sk):
                        P.dma("sp", KT.rearrange("m p t -> p m t")[:, :, grp * 512:(grp + 1) * 512], KTst[sk][:],
                              reads=[r_KTst[sk]], writes=[rKT], key="KTw")
                        for y in range(2):
                            for gq in range(4):
                                P.dma("sp", VS[y, gq][:, grp * 4:(grp + 1) * 4, :],
                                      Vst[sk][:, y, :, gq, :], reads=[r_Vst[sk]], writes=[rVS], key="VSw")
                    kvst.append(kv_store)

            while kvst:
                kvst.pop(0)()
            for _ in range(400):
                conv_step(3)
            conv_step(0)

        P.barrier()
        with ExitStack() as st:
          if not os.environ.get('KDBG_NOCMP'):
              w1b = [sbt(st, f"w1b{y}", [128, 32, 128], bf) for y in range(2)]
              w2b = [sbt(st, f"w2b{y}", [128, 128], bf) for y in range(2)]
              peT = [sbt(st, f"peT{y}", [128, 32], bf) for y in range(2)]
              r_cw = Res("cmpw")
              for y, (w1, w2, pe) in enumerate(((w1_k, w2_k, pe_k), (w1_v, w2_v, pe_v))):
                  P.g("pool", "memset", [], [r_cw], w1b[y][:], 0.0)
                  P.g("pool", "memset", [], [r_cw], w2b[y][:], 0.0)
                  for hf in range(2):
                      ps_ = slice(hf * 64, hf * 64 + 64)
                      P.dma("pool", w1b[y][ps_, :, hf * 64:hf * 64 + 64], w1,
                            writes=[r_cw], key="cmpw")
                      P.dma("pool", w2b[y][ps_, hf * 64:hf * 64 + 64], w2, writes=[r_cw], key="cmpw")
                      P.dma("pool", peT[y][ps_, :], pe, writes=[r_cw], key="cmpw")
              cbias = sbt(st, "cbias", [128, 2], F32); r_cb = Res("cbias")
              for y in range(2):
                  b = nb()
                  for l in range(32):
                      P.mm(banks[b][:, 0:1], w1b[y][:, l, :], peT[y][:, l:l + 1], l == 0, l == 31, [r_cw], [rbank[b]])
                  evac(cbias[:, y:y + 1], banks[b][:, 0:1], [rbank[b]], [r_cb], eng="dve")
              P.g("pool", "memset", [], [r_kcT], kcT[:], 0.0)
              for g_ in range(4):
                  pp = slice(64, 73) if g_ % 2 == 0 else slice(0, 9)
                  P.dma("sp", kcT[pp, g_, :], t_posk_c, writes=[r_kcT], key="kcApos")
              P.g("pool", "memset", [], [r_vcA], vcA[:], 0.0)
              P.g("pool", "memset", [], [r_vcA], vcA[:, :, :, 64:65], 1.0)
              kin = [sbt(st, f"kin{k}", [128, S + 128], bf) for k in range(2)]
              r_kin = [Res(f"kin{k}") for k in range(2)]
              hid = sbt(st, "hid", [128, 128], bf); r_hid = Res("hid")
              cnt = 0
              for y in range(2):
                  for pr in range(2):
                      kb = cnt % 2
                      cnt += 1
                      P.dma("sp", kin[kb][:, 0:S], KT[y * 2 + pr][:, 0:S], reads=[rKT], writes=[r_kin[kb]])
                      for nt in range(4):
                          nn = 128 if nt < 3 else 127
                          b = nb()
                          for l in range(32):
                              base = 16 * 128 * nt + l
                              rhs = kin[kb][:, base:base + 16 * (nn - 1) + 1:16]
                              P.mm(banks[b][:, 0:nn], w1b[y][:, l, :], rhs, l == 0, l == 31,
                                   [r_cw, r_kin[kb]], [rbank[b]])
                          if nn < 128:
                              P.g("dve", "memset", [], [r_hid], hid[:], 0.0)
                          P.act(hid[:, 0:nn], banks[b][:, 0:nn], AF.Silu, [rbank[b], r_cb], [r_hid],
                                bias=cbias[:, y:y + 1])
                          b2 = nb()
                          if y == 0:
                              P.mm(banks[b2][:, 0:128], w2b[0][:], hid[:], True, True, [r_cw, r_hid], [rbank[b2]])
                              evac(kcT[0:64, 2 * pr, nt * 128:(nt + 1) * 128], banks[b2][0:64, 0:128], [rbank[b2]], [r_kcT],
                                   eng="dve")
                              evac(kcT[64:128, 2 * pr + 1, nt * 128:(nt + 1) * 128], banks[b2][64:128, 0:128], [rbank[b2]], [r_kcT],
                                   eng="dve")
                          else:
                              P.mm(banks[b2][:, 0:128], hid[:], w2b[1][:], True, True, [r_cw, r_hid], [rbank[b2]])
                              evac(vcA[:, nt, pr * 2:pr * 2 + 2, 0:64],
                                   banks[b2][:, 0:128].rearrange("p (g e) -> p g e", g=2), [rbank[b2]], [r_vcA],
                                   eng="dve")
              if "dbg_kc" in debug:
                  dk = nc.dram_tensor("dbg_kc", [128, 4, 512], bf, kind="ExternalOutput").ap()
                  dv = nc.dram_tensor("dbg_vc", [128, 4, 4, 65], bf, kind="ExternalOutput").ap()
                  rd = Res("dbgkc")
                  P.dma("sp", dk, kcT[:], reads=[r_kcT], writes=[rd], key="dbg")
                  P.dma("sp", dv, vcA[:], reads=[r_vcA], writes=[rd], key="dbg")
                  P.final.append(rd)

        if "stop1" in debug:
            P.final.extend([rKT, rVS])
            P.finalize()
            P.emit()
            return nc


        P.barrier()
        ACTS = dscr("ACTS", [2048, 9216], bf)
        GATES = dscr("GATES", [2048, 48], F32)
        rACTS = Res("ACTS"); rGATES = Res("GATES")
        acol = {"q": 0, "za": 1024, "uv": 2048, "zb": 4096, "mg": 5120}
        nblk = int(os.environ.get("KDBG_NB", NOWN))
        with ExitStack() as st:
            N = make_norm(st, norm_g, "o", nx=2)
            hTo = sbt(st, "hTo", [128, 16, 2048], bf); r_hTo = Res("hTo")
            norm_load(N, 0, xo[0:128, :])
            for i in range(nblk):
                if i + 1 < nblk:
                    norm_load(N, (i + 1) % 2, xo[(i + 1) * 128:(i + 2) * 128, :])
                norm_compute(N, i % 2, lambda half, i=i: hTo[:, half * 8:(half + 1) * 8, i * 128:(i + 1) * 128], r_hTo)
            wbufa = [sbt(st, f"wbufa{k}", [128, 16, CW], bf) for k in range(2)]; r_wbufa = [Res(f"wbufa{k}") for k in range(2)]
            stg = [sbt(st, f"stg{k}", [128, CW], bf) for k in range(4)]; r_stg = [Res(f"stg{k}") for k in range(4)]
            gstg = [sbt(st, f"gstg{k}", [128, 48], F32) for k in range(2)]; r_gstg = [Res(f"gstg{k}") for k in range(2)]
            ca = {"w": 0, "s": 0}
            funcs = {"q": AF.Copy, "gl": AF.Sigmoid, "za": AF.Silu, "uv": AF.Gelu, "zb": AF.Silu, "mg": AF.Sigmoid}
            for (nm, wd) in (("gl", 48), ("q", 1024), ("za", 1024), ("uv", 2048), ("zb", 1024), ("mg", 4096)):
                for cc in range(0, wd, CW):
                    w_ = min(CW, wd - cc)
                    wk = ca["w"] % 2
                    ca["w"] += 1
                    P.dma("sp", wbufa[wk][:, :, 0:w_], WT[nm][cc // CW][:, :, 0:w_], reads=[r_WT], writes=[r_wbufa[wk]])
                    for i in range(nblk):
                        b = nb()
                        for ck in range(16):
                            P.mm(banks[b][:, 0:w_], hTo[:, ck, i * 128:(i + 1) * 128], wbufa[wk][:, ck, 0:w_], ck == 0, ck == 15,
                                 [r_hTo, r_wbufa[wk]], [rbank[b]])
                        if nm == "gl":
                            k_ = i % 2
                            P.act(gstg[k_][:], banks[b][:, 0:48], AF.Sigmoid, [rbank[b]], [r_gstg[k_]])
                            P.dma("sp", GATES[i * 128:(i + 1) * 128, :], gstg[k_][:], reads=[r_gstg[k_]], writes=[rGATES], key="GATESw")
                        else:
                            k_ = ca["s"] % 4
                            ca["s"] += 1
                            if nm == "q":
                                P.act(stg[k_][:, 0:w_], banks[b][:, 0:w_], AF.Copy, [rbank[b]], [r_stg[k_]], scale=0.125)
                            else:
                                P.act(stg[k_][:, 0:w_], banks[b][:, 0:w_], funcs[nm], [rbank[b]], [r_stg[k_]])
                            P.dma("sp", ACTS[i * 128:(i + 1) * 128, acol[nm] + cc:acol[nm] + cc + w_], stg[k_][:, 0:w_],
                                  reads=[r_stg[k_]], writes=[rACTS], key="ACTSw")

        OAB = dscr("OAB", [2048, 2048], bf); rOAB = Res("OAB")
        ssum = sbt(top, "ssum", [128, 64], F32); r_ssum = Res("ssum")
        P.barrier()
        with ExitStack() as st:
            b1 = sbt(st, "b1", [128, 2048], bf); r_b1 = Res("b1")
            oab = sbt(st, "oab", [128, 2048], bf); r_oab = Res("oab")
            N = None
            lng = sbt(st, "lng", [128, 1024], F32); lnb = sbt(st, "lnb", [128, 1024], F32); r_ln = Res("ln")
            P.dma("sp", lng[:], ln_g, writes=[r_ln], key="lnc"); P.dma("sp", lnb[:], ln_b, writes=[r_ln], key="lnc")
            ind = sbt(st, "ind", [128, 4096], bf)
            ov = sbt(st, "ov", [128, 4, 128], bf); tri = sbt(st, "tri", [128, 4, 128], bf); wm = sbt(st, "wm", [128, 8, 128], bf)
            r_tab = Res("tab")
            for dst, src in ((ind, t_ind), (ov, t_ov), (tri, t_tri), (wm, t_wm)):
                P.dma("sp", dst[:], src, writes=[r_tab], key="tab")
            f1 = sbt(st, "f1", [128, 2048], F32); r_f1 = Res("f1")
            wsf = f1[:, 0:1024].rearrange("p (a b) -> p a b", a=8); trl = sbt(st, "trl", [128, 128], F32)
            wsb = sbt(st, "wsb", [128, 8, 128], bf); bsT = sbt(st, "bsT", [128, 8], F32); r_sg = Res("sgw")
            P.dma("sp", wsf, sgu_wT, writes=[r_sg, r_f1], key="sgw"); P.dma("sp", trl[:], t_trilT, writes=[r_sg], key="sgw")
            P.dma("sp", bsT[:], sgu_bT, writes=[r_sg], key="sgw")
            P.g("dve", "tensor_tensor", [r_sg, r_f1], [r_sg], out=wsb[:], in0=wsf,
                in1=trl[:].unsqueeze(1).broadcast_to([128, 8, 128]), op=ALU.mult)

            hTb = sbt(st, "hTb", [128, 16, 128], bf); r_hTb = Res("hTb")
            wbuf = None; r_wbuf = None
            wctr = {"c": 0}

            def dense(lhsT_fn, r_l, KC, wname, c0, width, epi):
                for s0 in range(0, width, CW):
                    dense1(lhsT_fn, r_l, KC, wname, c0 + s0, min(CW, width - s0), lambda b, w, s0=s0: epi(b, w, s0))

            def dense1(lhsT_fn, r_l, KC, wname, c0, width, epi):
                wk = wctr["c"] % 2
                wctr["c"] += 1
                P.dma("sp", wbuf[wk][:, 0:KC, 0:width], WT[wname][c0 // CW][:, :, 0:width], reads=[r_WT], writes=[r_wbuf[wk]])
                b = nb()
                for ck in range(KC):
                    P.mm(banks[b][:, 0:width], lhsT_fn(ck), wbuf[wk][:, ck, 0:width], ck == 0, ck == KC - 1,
                         [r_l, r_wbuf[wk]], [rbank[b]])
                epi(b, width)

            ablk2 = [sbt(st, f"ablk{k}", [128, 5120], bf) for k in range(2)]; r_ablk2 = [Res(f"ablk{k}") for k in range(2)]
            gates2 = [sbt(st, f"gates{k}", [128, 48], F32) for k in range(2)]; r_gates2 = [Res(f"gates{k}") for k in range(2)]
            cur = {}
            QA = sbt(st, "QA", [128, 4, 512], bf); r_QA = Res("QA")
            P.g("pool", "memset", [], [r_QA], QA[:], 0.0)
            cmt = sbt(st, "cmt", [128, 4, 128], bf); r_cmt = Res("cmt")
            fb2 = sbt(st, "fb2", [128, 128], F32); caus = sbt(st, "caus", [128, 128], F32); r_fb = Res("fb")
            PT = [sbt(st, f"PT{k}", [128, 512], bf) for k in range(4)]; r_PT = [Res(f"PT{k}") for k in range(4)]
            PTc = sbt(st, "PTc", [128, 4, 512], bf); r_PTc = [Res(f"PTc{k}") for k in range(4)]
            Osb = [sbt(st, f"Osb{k}", [65, 512], F32) for k in range(2)]; r_Osb = [Res(f"Osb{k}") for k in range(2)]
            oatt = sbt(st, "oatt", [128, 1024], F32); r_oatt = Res("oatt")
            otmp = sbt(st, "otmp", [128, 256], F32); r_otmp = Res("otmp")
            sm = sbt(st, "sm", [128, 32], F32); r_sm = Res("sm")
            imp = sbt(st, "imp", [128, 128], F32); r_imp = Res("imp")
            sc2 = sbt(st, "sc2", [128, 128], F32); r_sc2 = Res("sc2")
            m8 = sbt(st, "m8", [128, 16], F32); r_m8 = Res("m8")
            mbq = sbt(st, "mbq", [128, 128], bf); r_mbq = Res("mbq")
            MBT = sbt(st, "MBT", [128, 2, 128], bf); r_MBT = Res("MBT")
            P.g("pool", "memset", [], [r_MBT], MBT[:], 0.0)
            Kbuf = [sbt(st, f"Kbuf{k}", [128, S], bf) for k in range(2)]; r_Kbuf = [Res(f"Kbuf{k}") for k in range(2)]
            r_Kpos = Res("Kpos")
            for k in range(2):
                P.g("pool", "memset", [], [r_Kpos, r_Kbuf[k]], Kbuf[k][:], 0.0)
            P.dma("sp", Kbuf[0][64:73, :], t_posk_s, writes=[r_Kpos], key="kpos"); P.dma("sp", Kbuf[1][0:9, :], t_posk_s, writes=[r_Kpos], key="kpos")
            Vbuf = [sbt(st, f"Vbuf{k}", [128, NT, 65], bf) for k in range(2)]; r_Vbuf = [Res(f"Vbuf{k}") for k in range(2)]
            r_Kwpos = [Res(f"Kwpos{k}") for k in range(2)]
            HOLD_KW = 1
            Kwb = [sbt(st, f"Kwb{k}", [128, 1024], bf) for k in range(2)]; r_Kwb = [Res(f"Kwb{k}") for k in range(2)]
            for k in range(2):
                P.g("pool", "memset", [], [r_Kwb[k], r_Kwpos[k]], Kwb[k][:], 0.0)
            Vwb = [sbt(st, f"Vwb{k}", [128, 8, 65], bf) for k in range(2)]; r_Vwb = [Res(f"Vwb{k}") for k in range(2)]
            f2 = sbt(st, "f2", [128, 2048], F32); r_f2 = Res("f2")
            oT = hTb; r_oT = r_hTb
            xT = oT; r_xT = r_oT
            ptk = sbt(st, "ptk", [128, 256], F32); pbk = sbt(st, "pbk", [128, 256], bf); r_pt = Res("ptk")
            pT = sbt(st, "pT", [128, 2, 128], bf); r_pT = Res("pT")
            ptc = {"k": 0}

            def transposeN(src, r_src, n, dst, r_dst):
                for h0 in range(0, n, 8):
                    m = min(8, n - h0)
                    b = nb()
                    bb = bankbf(b)
                    for cc in range(m):
                        P.tr(bb[:, cc * 128:(cc + 1) * 128], src[:, (h0 + cc) * 128:(h0 + cc + 1) * 128], identb[:],
                             [r_src, r_identb], [rbank[b]])
                    evac(dst[:, h0:h0 + m, :], bb[:, 0:m * 128].rearrange("p (c t) -> p c t", c=m), [rbank[b]], [r_dst])

            pending = []
            octr = {"k": 0}

            def flush():
                while pending:
                    pending.pop(0)()

            def branch(tiles, Qop, rq, ob, save_pt=False):
                nt_ = len(tiles)
                pts = {}

                def emitS(idx):
                    (kT, rk, extras, vap, rv) = tiles[idx]
                    sbk = nb()
                    Sap = banks[sbk][:, :].rearrange("p (a b) -> p a b", a=4)
                    P.mm(Sap, kT, Qop, True, len(extras) == 0, rk + rq, [rbank[sbk]])
                    for ei, (el, er, rr) in enumerate(extras):
                        P.mm(Sap, el, er, False, ei == len(extras) - 1, [r_tab, r_identb] + rr, [rbank[sbk]])
                    if save_pt:
                        pt_ap, rpt = PTc[:, idx, :], r_PTc[idx]
                    else:
                        k_ = ptc["k"] % 4
                        ptc["k"] += 1
                        pt_ap, rpt = PT[k_][:], r_PT[k_]
                    P.act(pt_ap, banks[sbk][:, :], AF.Exp, [rbank[sbk]], [rpt])
                    pts[idx] = (pt_ap, rpt)

                def emitPV(idx):
                    (kT, rk, extras, vap, rv) = tiles[idx]
                    pt_ap, rpt = pts[idx]
                    P.mm(banks[ob][0:65, :], vap, pt_ap, idx == 0, idx == nt_ - 1, [rv, rpt], [rbank[ob]])

                for idx in range(nt_):
                    emitS(idx)
                    if idx >= 2:
                        emitPV(idx - 2)
                    if idx == min(2, nt_ - 1):
                        flush()
                for idx in range(max(0, nt_ - 2), nt_):
                    emitPV(idx)

            def epi_evac(ob):
                k_ = octr["k"] % 2
                octr["k"] += 1
                evac(Osb[k_][:, :], banks[ob][0:65, :], [rbank[ob]], [r_Osb[k_]], eng="dve")
                return k_

            def epilogue(k_, br, g, first):
                for r in range(4):
                    P.tr(banks[7][:, r * 65:(r + 1) * 65], Osb[k_][0:65, r * 128:(r + 1) * 128], identf[0:65, 0:65],
                         [r_Osb[k_], r_identf], [rbank[7]])
                O3 = banks[7][:, 0:260].rearrange("p (a e) -> p a e", a=4)
                P.g("dve", "tensor_scalar", [rbank[7]], [r_sm], out=sm[:, 0:4], in0=O3[:, :, 64], scalar1=1e-30,
                    scalar2=None, op0=ALU.max)
                P.g("dve", "reciprocal", [r_sm], [r_sm], out=sm[:, 4:8], in_=sm[:, 0:4])
                P.g("dve", "tensor_tensor", [r_sm, cur["r_gates"]], [r_sm], out=sm[:, 8:12], in0=sm[:, 4:8],
                    in1=cur["gates"][:, br * 16 + 4 * g:br * 16 + 4 * g + 4], op=ALU.mult)
                dst = oatt[:, g * 256:(g + 1) * 256].rearrange("p (a e) -> p a e", a=4)
                wb_ = sm[:, 8:12].unsqueeze(2).broadcast_to([128, 4, 64])
                if first:
                    P.g("dve", "tensor_tensor", [rbank[7], r_sm], [r_oatt], out=dst, in0=O3[:, :, 0:64], in1=wb_, op=ALU.mult)
                else:
                    t3 = otmp[:, :].rearrange("p (a e) -> p a e", a=4)
                    P.g("dve", "tensor_tensor", [rbank[7], r_sm], [r_otmp], out=t3, in0=O3[:, :, 0:64], in1=wb_, op=ALU.mult)
                    P.g("dve", "tensor_tensor", [r_otmp, r_oatt], [r_oatt], out=dst, in0=dst, in1=t3, op=ALU.add)

            segs = [(0, 1024, "q"), (0, 48, "gl"), (0, 1024, "za"), (0, 2048, "uv"), (0, 1024, "zb"), (0, 4096, "mg")]
            nblk = int(os.environ.get("KDBG_NB", NOWN))
            for i in range(nblk):
                tok = slice(i * 128, (i + 1) * 128)
                xk = 0
                def load_acts(j):
                    P.dma("sp", ablk2[j % 2][:], ACTS[j * 128:(j + 1) * 128, 0:5120], reads=[rACTS], writes=[r_ablk2[j % 2]])
                    P.dma("sp", gates2[j % 2][:], GATES[j * 128:(j + 1) * 128, :], reads=[rGATES], writes=[r_gates2[j % 2]])
                if i == 0:
                    load_acts(0)
                ablk = ablk2[i % 2]; r_ablk = r_ablk2[i % 2]
                r_qtok = r_za = r_uvg = r_zb = r_ablk
                za = ablk[:, 1024:2048]; zb = ablk[:, 4096:5120]
                cur["gates"] = gates2[i % 2]; cur["r_gates"] = r_gates2[i % 2]
                qb3 = t_qb[i].rearrange("p (g c) -> p g c", g=4)
                for g_ in range(4):
                    pp = slice(64, 73) if g_ % 2 == 0 else slice(0, 9)
                    P.dma("sp", QA[pp, g_, :], qb3[:, g_, :], writes=[r_QA], key="QAqb")
                P.dma("sp", cmt[:], t_cm[i], writes=[r_cmt])
                P.dma("sp", fb2[:], t_fb2[i], writes=[r_fb], key="fb"); P.dma("sp", caus[:], t_caus[i], writes=[r_fb], key="fb")
                bq = nb()
                bbq = bankbf(bq)
                for cc in range(8):
                    P.tr(bbq[:, cc * 128:(cc + 1) * 128], ablk[:, cc * 128:(cc + 1) * 128], identb[:],
                         [r_qtok, r_identb], [rbank[bq]])
                for P2 in range(2):
                    for hf2 in range(2):
                        hs2 = slice(hf2 * 64, hf2 * 64 + 64)
                        evac(QA[hs2, 2 * P2 + hf2, :], bbq[hs2, P2 * 512:(P2 + 1) * 512], [rbank[bq]], [r_QA])
                if i + 1 < nblk:
                    load_acts(i + 1)
                ntc = (32 * i + 31) // 128 + 1
                L = (4 * i + 4) * 128
                T0 = max(0, 4 * i - 4)
                for g in range(4):
                    P_, hf = g // 2, g % 2
                    hs = slice(hf * 64, hf * 64 + 64)
                    kb = g % 2
                    po_ = slice(64, 73) if hf == 0 else slice(0, 9)
                    Qop = QA[:, g, :].rearrange("p (a b) -> p a b", a=4)
                    rq = [r_QA]
                    po_ = slice(64, 73) if hf == 0 else slice(0, 9)
                    P.dma("sp", Kbuf[kb][hs, 0:L], KT[4 + P_][hs, 0:L], reads=[rKT], writes=[r_Kbuf[kb]])
                    P.dma("sp", Vbuf[kb][:, 0:L // 128, :], VS[0, g][:, 0:L // 128, :], reads=[rVS], writes=[r_Vbuf[kb]])
                    P.dma("sp", Kwb[kb][hs, 0:L - T0 * 128], KT[6 + P_][hs, T0 * 128:L], reads=[rKT], writes=[r_Kwb[kb]])
                    P.dma("sp", Kwb[kb][po_, 0:L - T0 * 128], t_posk_s[:, T0 * 128:L], writes=[r_Kwpos[kb]])
                    P.dma("sp", Vwb[kb][:, 0:L // 128 - T0, :], VS[1, g][:, T0:L // 128, :], reads=[rVS], writes=[r_Vwb[kb]])
                    bc4 = lambda ap: ap.unsqueeze(1).broadcast_to([128, 4, 128])
                    tiles = []
                    for nt in range(ntc):
                        tiles.append((kcT[:, g, nt * 128:(nt + 1) * 128], [r_kcT],
                                      [(identb[:], bc4(cmt[:, nt, :]), [r_cmt])], vcA[:, nt, g, :], r_vcA))
                    obC = 5 + (octr["k"] % 2)
                    branch(tiles, Qop, rq, obC, save_pt=True)
                    kC = epi_evac(obC)
                    ub = nb()
                    for r in range(4):
                        for nt in range(ntc):
                            P.mm(banks[ub][:, r * 128:(r + 1) * 128], PTc[:, nt, r * 128:(r + 1) * 128], ov[:, nt, :],
                                 nt == 0, nt == ntc - 1, [r_PTc[nt], r_tab], [rbank[ub]])

                    def after_c(kC=kC, g=g, ub=ub):
                        epilogue(kC, 0, g, True)
                        for r in range(4):
                            if r == 0:
                                P.g("dve", "tensor_scalar", [rbank[ub], r_sm], [r_imp], out=imp[:], in0=banks[ub][:, 0:128],
                                    scalar1=sm[:, 4:5], scalar2=None, op0=ALU.mult)
                            else:
                                P.g("dve", "scalar_tensor_tensor", [rbank[ub], r_sm, r_imp], [r_imp], out=imp[:],
                                    in0=banks[ub][:, r * 128:(r + 1) * 128], scalar=sm[:, 4 + r:5 + r], in1=imp[:],
                                    op0=ALU.mult, op1=ALU.add)
                        P.g("dve", "tensor_tensor", [r_imp, r_fb], [r_imp], out=imp[:], in0=imp[:], in1=caus[:], op=ALU.mult)
                        P.g("dve", "tensor_tensor", [r_imp, r_fb], [r_imp], out=imp[:], in0=imp[:], in1=fb2[:], op=ALU.add)
                        P.g("dve", "max", [r_imp], [r_m8], out=m8[:, 0:8], in_=imp[:])
                        P.g("dve", "match_replace", [r_imp, r_m8], [r_sc2], out=sc2[:], in_to_replace=m8[:, 0:8],
                            in_values=imp[:], imm_value=-2.0)
                        P.g("dve", "max", [r_sc2], [r_m8], out=m8[:, 8:16], in_=sc2[:])
                        P.g("dve", "tensor_scalar", [r_m8], [r_m8], out=m8[:, 0:1], in0=m8[:, 15:16], scalar1=-0.5,
                            scalar2=None, op0=ALU.max)
                        P.g("dve", "tensor_scalar", [r_imp, r_m8], [r_sc2], out=sc2[:], in0=imp[:], scalar1=m8[:, 0:1],
                            scalar2=-NEGM, op0=ALU.is_ge, op1=ALU.mult)
                        P.g("dve", "tensor_scalar", [r_sc2], [r_mbq], out=mbq[:], in0=sc2[:], scalar1=NEGM, scalar2=None,
                            op0=ALU.add)
                    pending.append(after_c)
                    tiles = []
                    for rr_ in range(8):
                        T = 4 * i - 4 + rr_
                        if T < 0:
                            continue
                        tiles.append((Kwb[kb][:, (T - T0) * 128:(T - T0 + 1) * 128], [r_Kwb[kb], r_Kwpos[kb]],
                                      [(identb[:], bc4(wm[:, rr_, :]), [])],
                                      Vwb[kb][:, T - T0, :], r_Vwb[kb]))
                    obW = 5 + (octr["k"] % 2)
                    branch(tiles, Qop, rq, obW)
                    flush()
                    kW = epi_evac(obW)
                    mb_ = nb()
                    P.tr(bankbf(mb_)[:, 0:128], mbq[:], identb[:], [r_mbq, r_identb], [rbank[mb_]])
                    evac(MBT[0:64, 0, :], bankbf(mb_)[0:64, 0:128], [rbank[mb_]], [r_MBT], eng="dve")
                    evac(MBT[64:128, 1, :], bankbf(mb_)[64:128, 0:128], [rbank[mb_]], [r_MBT], eng="dve")
                    pending.append(lambda kW=kW, g=g: epilogue(kW, 2, g, False))
                    tiles = []
                    for kt in range(4 * i + 4):
                        ex = [(ind[:, (kt % 32) * 128:(kt % 32 + 1) * 128],
                               MBT[:, kt // 32, :].unsqueeze(1).broadcast_to([128, 4, 128]), [r_MBT])]
                        if kt >= 4 * i:
                            ex.append((identb[:], bc4(tri[:, kt - 4 * i, :]), []))
                        tiles.append((Kbuf[kb][:, kt * 128:(kt + 1) * 128], [r_Kbuf[kb], r_Kpos],
                                      ex, Vbuf[kb][:, kt, :], r_Vbuf[kb]))
                    obS = 5 + (octr["k"] % 2)
                    branch(tiles, Qop, rq, obS)
                    flush()
                    kS = epi_evac(obS)
                    pending.append(lambda kS=kS, g=g: epilogue(kS, 1, g, False))
                flush()
                P.g("dve", "tensor_tensor", [r_oatt, r_za], [r_oab], out=oab[:, 0:1024], in0=oatt[:], in1=za, op=ALU.mult)
                v_ = ablk[:, 3072:4096]
                P.act(f1[:, 0:1024], v_, AF.Copy, [r_uvg], [r_f1, r_sm], accum_out=sm[:, 16:17])
                P.act(f1[:, 1024:2048], v_, AF.Square, [r_uvg], [r_f1, r_sm], accum_out=sm[:, 17:18])
                P.g("dve", "tensor_scalar", [r_sm], [r_sm], out=sm[:, 18:19], in0=sm[:, 16:17], scalar1=1.0 / 1024,
                    scalar2=None, op0=ALU.mult)
                P.g("dve", "tensor_tensor", [r_sm], [r_sm], out=sm[:, 19:20], in0=sm[:, 18:19], in1=sm[:, 18:19], op=ALU.mult)
                P.g("dve", "scalar_tensor_tensor", [r_sm], [r_sm], out=sm[:, 20:21], in0=sm[:, 17:18], scalar=1.0 / 1024,
                    in1=sm[:, 19:20], op0=ALU.mult, op1=ALU.subtract)
                P.act(sm[:, 21:22], sm[:, 20:21], AF.Ln, [r_sm], [r_sm], bias=EPS)
                P.act(sm[:, 22:23], sm[:, 21:22], AF.Exp, [r_sm], [r_sm], scale=-0.5)
                P.g("dve", "tensor_scalar", [r_uvg, r_sm], [r_f1], out=f1[:, 0:1024], in0=v_, scalar1=sm[:, 18:19],
                    scalar2=sm[:, 22:23], op0=ALU.subtract, op1=ALU.mult)
                P.g("dve", "tensor_tensor", [r_f1, r_ln], [r_f1], out=f1[:, 0:1024], in0=f1[:, 0:1024], in1=lng[:], op=ALU.mult)
                P.g("dve", "tensor_tensor", [r_f1, r_ln], [r_b1], out=b1[:, 0:1024], in0=f1[:, 0:1024], in1=lnb[:], op=ALU.add)
                for half in range(2):
                    b = nb()
                    for gq in range(4):
                        G_ = half * 4 + gq
                        P.mm(banks[b][:, gq * 128:(gq + 1) * 128], wsb[:, G_, :], b1[:, G_ * 128:(G_ + 1) * 128], True, True,
                             [r_sg, r_b1], [rbank[b]])
                    P.g("dve", "tensor_tensor", [rbank[b], r_sg], [r_f2], out=f2[:, half * 512:(half + 1) * 512].rearrange("p (a e) -> p a e", a=4),
                        in0=banks[b][:, :].rearrange("p (a e) -> p a e", a=4),
                        in1=bsT[:, half * 4:(half + 1) * 4].unsqueeze(2).broadcast_to([128, 4, 128]), op=ALU.add)
                P.g("dve", "tensor_tensor", [r_f2, r_uvg], [r_f2], out=f2[:, 0:1024], in0=f2[:, 0:1024], in1=ablk[:, 2048:3072], op=ALU.mult)
                P.g("dve", "tensor_tensor", [r_f2, r_zb], [r_oab], out=oab[:, 1024:2048], in0=f2[:, 0:1024], in1=zb, op=ALU.mult)
                P.dma("sp", OAB[tok, :], oab[:], reads=[r_oab], writes=[rOAB], key="OABw")

        P.barrier()
        X1 = dscr("X1", [2048, D], F32); X2 = dscr("X2", [2048, D], F32)
        rX1 = Res("X1"); rX2 = Res("X2")
        with ExitStack() as st:
            TA = sbt(st, "TA", [128, 16, 2048], bf); r_TA = Res("TA")
            MA = sbt(st, "MA", [128, 16, 2048], bf); r_MA = Res("MA")
            big = [sbt(st, f"big{k}", [128, 16, 512], bf) for k in range(2)]; r_big = [Res(f"big{k}") for k in range(2)]
            wsm = [sbt(st, f"wsm{k}", [128, 8, 512], bf) for k in range(2)]; r_wsm = [Res(f"wsm{k}") for k in range(2)]
            plew = sbt(st, "plew", [128, 2, 512], bf); r_plew = Res("plew")
            tA = [sbt(st, f"tA{k}", [128, 512], F32) for k in range(2)]; r_tA = [Res(f"tA{k}") for k in range(2)]
            tB = [sbt(st, f"tB{k}", [128, 512], F32) for k in range(2)]; r_tB = [Res(f"tB{k}") for k in range(2)]
            tX = [sbt(st, f"tX{k}", [128, 512], F32) for k in range(2)]; r_tX = [Res(f"tX{k}") for k in range(2)]
            tY = [sbt(st, f"tY{k}", [128, 512], F32) for k in range(2)]; r_tY = [Res(f"tY{k}") for k in range(2)]
            pT3 = wsm[0][:].rearrange("p a b -> p (a b)").rearrange("p (c t) -> p c t", c=2); r_pT3 = r_wsm[0]
            ptk3 = sbt(st, "ptk3", [128, 256], F32); pbk3 = sbt(st, "pbk3", [128, 256], bf); r_p3 = Res("p3")
            jk3 = sbt(st, "jk3", [128, 512], bf); r_jk3 = Res("jk3")

            def tr_tiles(src_fn, r_src, i):
                for half in range(2):
                    b = nb()
                    bb = bankbf(b)
                    for c8 in range(8):
                        P.tr(bb[:, c8 * 128:(c8 + 1) * 128], src_fn(half * 8 + c8), identb[:], [r_src, r_identb], [rbank[b]])
                    evac(TA[:, half * 8:(half + 1) * 8, i * 128:(i + 1) * 128], bb.rearrange("p (c t) -> p c t", c=8),
                         [rbank[b]], [r_TA])

            for i in range(nblk):
                k = i % 2
                ldv = big[k][:, 0:4, :]
                P.dma("sp", ldv, OAB[i * 128:(i + 1) * 128, :].rearrange("p (a b) -> p a b", a=4), reads=[rOAB], writes=[r_big[k]])
                tr_tiles(lambda c, k=k: big[k][:, c // 4, (c % 4) * 128:(c % 4 + 1) * 128], r_big[k], i)
            acts3 = ACTS.rearrange("(t p) c -> p t c", p=128)
            for cc in range(4):
                P.dma("sp", big[0][:, 0:nblk, :], acts3[:, 0:nblk, 5120 + cc * 512:5120 + (cc + 1) * 512], reads=[rACTS], writes=[r_big[0]])
                P.dma("sp", big[1][:, 0:nblk, :], acts3[:, 0:nblk, 7168 + cc * 512:7168 + (cc + 1) * 512], reads=[rACTS], writes=[r_big[1]])
                P.dma("sp", wsm[0][:], WT["upa"][cc], reads=[r_WT], writes=[r_wsm[0]])
                P.dma("sp", wsm[1][:], WT["upb"][cc], reads=[r_WT], writes=[r_wsm[1]])
                for i in range(nblk):
                    k = i % 2
                    bA = nb()
                    for ck in range(8):
                        P.mm(banks[bA][:, :], TA[:, ck, i * 128:(i + 1) * 128], wsm[0][:, ck, :], ck == 0, ck == 7, [r_TA, r_wsm[0]], [rbank[bA]])
                    bB = nb()
                    for ck in range(8):
                        P.mm(banks[bB][:, :], TA[:, 8 + ck, i * 128:(i + 1) * 128], wsm[1][:, ck, :], ck == 0, ck == 7, [r_TA, r_wsm[1]], [rbank[bB]])
                    P.g("dve", "tensor_tensor", [rbank[bA], r_big[0]], [r_tA[k]], out=tA[k][:], in0=banks[bA][:, :], in1=big[0][:, i, :], op=ALU.mult)
                    P.g("dve", "tensor_tensor", [rbank[bB], r_big[1]], [r_tB[k]], out=tB[k][:], in0=banks[bB][:, :], in1=big[1][:, i, :], op=ALU.mult)
                    P.g("dve", "tensor_tensor", [r_tA[k], r_tB[k]], [r_MA], out=MA[:, i, cc * 512:(cc + 1) * 512], in0=tA[k][:], in1=tB[k][:], op=ALU.add)
            for i in range(nblk):
                tr_tiles(lambda c, i=i: MA[:, i, c * 128:(c + 1) * 128], r_MA, i)
            for cc in range(4):
                wk = cc % 2
                P.dma("sp", big[wk][:], WT["out"][cc], reads=[r_WT], writes=[r_big[wk]])
                for i in range(nblk):
                    k = i % 2
                    b = nb()
                    for ck in range(16):
                        P.mm(banks[b][:, :], TA[:, ck, i * 128:(i + 1) * 128], big[wk][:, ck, :], ck == 0, ck == 15, [r_TA, r_big[wk]], [rbank[b]])
                    P.dma("sp", tX[k][:], xo[i * 128:(i + 1) * 128, cc * 512:(cc + 1) * 512], writes=[r_tX[k]])
                    P.g("dve", "tensor_tensor", [rbank[b], r_tX[k]], [r_tY[k]], out=tY[k][:], in0=banks[b][:, :], in1=tX[k][:], op=ALU.add)
                    P.dma("sp", X1[i * 128:(i + 1) * 128, cc * 512:(cc + 1) * 512], tY[k][:], reads=[r_tY[k]], writes=[rX1], key="X1w")
                    P.act(MA[:, i, cc * 512:(cc + 1) * 512], tY[k][:], AF.Copy, [r_tY[k]], [r_MA])
            for i in range(nblk):
                tr_tiles(lambda c, i=i: MA[:, i, c * 128:(c + 1) * 128], r_MA, i)
            for i in range(nblk):
                P.dma("sp", ptk3[:], po[i * 128:(i + 1) * 128, :], writes=[r_p3])
                P.act(pbk3[:], ptk3[:], AF.Copy, [r_p3], [r_p3])
                b = nb()
                bb = bankbf(b)
                for c2 in range(2):
                    P.tr(bb[:, c2 * 128:(c2 + 1) * 128], pbk3[:, c2 * 128:(c2 + 1) * 128], identb[:], [r_p3, r_identb], [rbank[b]])
                evac(pT3[:, :, i * 128:(i + 1) * 128], bb[:, 0:256].rearrange("p (c t) -> p c t", c=2), [rbank[b]], [r_pT3])
            for cc in range(4):
                wk = cc % 2
                P.dma("sp", big[wk][:], WT["pg"][cc], reads=[r_WT], writes=[r_big[wk]])
                P.dma("sp", plew[:], WT["ple"][cc], reads=[r_WT], writes=[r_plew])
                for i in range(nblk):
                    k = i % 2
                    bG = nb()
                    for ck in range(16):
                        P.mm(banks[bG][:, :], TA[:, ck, i * 128:(i + 1) * 128], big[wk][:, ck, :], ck == 0, ck == 15, [r_TA, r_big[wk]], [rbank[bG]])
                    P.act(tA[k][:], banks[bG][:, :], AF.Sigmoid, [rbank[bG]], [r_tA[k]])
                    bP = nb()
                    for ck in range(2):
                        P.mm(banks[bP][:, :], pT3[:, ck, i * 128:(i + 1) * 128], plew[:, ck, :], ck == 0, ck == 1, [r_pT3, r_plew], [rbank[bP]])
                    P.dma("sp", tX[k][:], X1[i * 128:(i + 1) * 128, cc * 512:(cc + 1) * 512], reads=[rX1], writes=[r_tX[k]])
                    P.g("dve", "tensor_tensor", [rbank[bP], r_tA[k]], [r_tB[k]], out=tB[k][:], in0=banks[bP][:, :], in1=tA[k][:], op=ALU.mult)
                    P.g("dve", "tensor_tensor", [r_tB[k], r_tX[k]], [r_tY[k]], out=tY[k][:], in0=tB[k][:], in1=tX[k][:], op=ALU.add)
                    P.act(jk3[:], tY[k][:], AF.Square, [r_tY[k]], [r_jk3, r_ssum], accum_out=ssum[:, i * 4 + cc:i * 4 + cc + 1])
                    P.dma("sp", X2[i * 128:(i + 1) * 128, cc * 512:(cc + 1) * 512], tY[k][:], reads=[r_tY[k]], writes=[rX2], key="X2w")
        P.barrier()
        with ExitStack() as st:
            fgt = sbt(st, "fgt", [128, D], F32); r_fgt = Res("fgt")
            P.dma("sp", fgt[:], final_g, writes=[r_fgt])
            xt3 = [sbt(st, f"xt3{k}", [128, D], F32) for k in range(2)]; r_xt3 = [Res(f"xt3{k}") for k in range(2)]
            ot3 = [sbt(st, f"ot3{k}", [128, D], F32) for k in range(2)]; r_ot3 = [Res(f"ot3{k}") for k in range(2)]
            st3 = sbt(st, "st3", [128, 8], F32); r_st3 = Res("st3")
            for i in range(nblk):
                k = i % 2
                P.dma("sp", xt3[k][:], X2[i * 128:(i + 1) * 128, :], reads=[rX2], writes=[r_xt3[k]])
                P.g("dve", "tensor_tensor", [r_ssum], [r_st3], out=st3[:, 0:2], in0=ssum[:, i * 4:i * 4 + 2], in1=ssum[:, i * 4 + 2:i * 4 + 4], op=ALU.add)
                P.g("dve", "tensor_tensor", [r_st3], [r_st3], out=st3[:, 2:3], in0=st3[:, 0:1], in1=st3[:, 1:2], op=ALU.add)
                P.act(st3[:, 3:4], st3[:, 2:3], AF.Ln, [r_st3], [r_st3], scale=1.0 / D, bias=EPS)
                P.act(st3[:, 4:5], st3[:, 3:4], AF.Exp, [r_st3], [r_st3], scale=-0.5)
                P.g("dve", "scalar_tensor_tensor", [r_xt3[k], r_st3, r_fgt], [r_ot3[k]], out=ot3[k][:], in0=xt3[k][:], scalar=st3[:, 4:5],
                    in1=fgt[:], op0=ALU.mult, op1=ALU.mult)
                P.dma("sp", out[i * 128:(i + 1) * 128, :], ot3[k][:], reads=[r_ot3[k]], writes=[rOUT], key="outw")
        P.finalize()
        P.emit()
    return nc


def _make_w_own(w):
    qcols = []
    for P_ in range(2):
        for r in range(4):
            for hd in (8 * P_ + r, 8 * P_ + 4 + r):
                qcols.extend(range(hd * 64, hd * 64 + 64))
    return np.ascontiguousarray(np.concatenate([w[:, qcols], w[:, 2560:]], axis=1))


def _prep_inputs(inputs):
    f = lambda a: np.ascontiguousarray(np.asarray(a, dtype=np.float32))
    x = f(inputs["x"]); p = f(inputs["p"])[0]
    com = _common_tables()
    shared = {
        "norm_g": np.ascontiguousarray(np.broadcast_to(f(inputs["norm_g"])[0][None, :], (128, D))),
        "final_g": np.ascontiguousarray(np.broadcast_to(f(inputs["final_g"])[None, :], (128, D))),
        "w_in": f(inputs["w_in"])[0],
        "w_own": _make_w_own(f(inputs["w_in"])[0]),
        "pe_k": np.ascontiguousarray(f(inputs["cmp_pe_k"])[0].T), "w1_k": np.ascontiguousarray(f(inputs["cmp_w1_k"])[0].transpose(1, 0, 2)), "w2_k": f(inputs["cmp_w2_k"])[0],
        "pe_v": np.ascontiguousarray(f(inputs["cmp_pe_v"])[0].T), "w1_v": np.ascontiguousarray(f(inputs["cmp_w1_v"])[0].transpose(1, 0, 2)), "w2_v": f(inputs["cmp_w2_v"])[0],
        "ln_g": np.ascontiguousarray(np.broadcast_to(f(inputs["ln_v_g"])[0][None, :], (128, 1024))),
        "ln_b": np.ascontiguousarray(np.broadcast_to(f(inputs["ln_v_b"])[0][None, :], (128, 1024))),
        "sgu_wT": np.ascontiguousarray(f(inputs["sgu_w"])[0].transpose(2, 0, 1)),
        "sgu_bT": np.ascontiguousarray(f(inputs["sgu_b"])[0].T),
        "w_up_a": f(inputs["w_up_a"])[0], "w_up_b": f(inputs["w_up_b"])[0],
        "w_out": f(inputs["w_out"])[0], "w_ple": f(inputs["w_ple"])[0], "w_pg": f(inputs["w_ple_gate"])[0],
    }
    shared.update(com)
    in_maps = []
    for core in range(8):
        b, c = core // 4, core % 4
        m = dict(shared)
        m["xb"] = x[b]
        xr = x[b].reshape(16, 4, 128, D)[:, c].reshape(2048, D)
        m["xo"] = np.ascontiguousarray(xr)
        m["po"] = np.ascontiguousarray(p[b].reshape(16, 4, 128, 256)[:, c].reshape(2048, 256))
        m.update(_core_tables(c))
        in_maps.append(m)
    return in_maps


def kernel(**inputs):
    in_maps = _prep_inputs(inputs)
    nc = build()
    res = run_bass_kernel_spmd(nc, in_maps, core_ids=list(range(8)))
    outp = np.zeros((2, S, D), np.float32)
    o = outp.reshape(2, 16, 4, 128, D)
    for core in range(8):
        b, c = core // 4, core % 4
        o[b, :, c] = res.results[core]["out"].reshape(16, 128, D)
    return outp
```

```python
import os
import numpy as np
from contextlib import ExitStack
import ml_dtypes
import concourse.bass as bass
import concourse.mybir as mybir
from concourse.bass_utils import run_bass_kernel_spmd

F32 = mybir.dt.float32
BF16 = mybir.dt.bfloat16
AF = mybir.ActivationFunctionType
ALU = mybir.AluOpType

S = 8192
D = 2048
NT = 64
NOWN = 16
NEGM = -30000.0
EPS = 1e-6
CW = 512
SKIP = os.environ.get('KDBG_SKIP', '')


class Res:
    __slots__ = ("name", "w", "r")

    def __init__(self, name):
        self.name = name
        self.w = None
        self.r = []


class Op:
    __slots__ = ("eng", "fn", "deps", "inc", "sem", "val", "dma")


class Prog:
    ENGS = ["pe", "act", "dve", "pool", "sp"]

    def __init__(self, nc, stack):
        self.nc = nc
        self.stack = stack
        self.ops = {e: [] for e in self.ENGS}
        self.dma_sems = {}
        self.nsem = 0
        self.final = []

    def newsem(self, name):
        self.nsem += 1
        return self.stack.enter_context(self.nc.semaphore(f"{name}{self.nsem}"))

    def _add(self, eng, fn, reads, writes, dma_key=None):
        op = Op()
        op.eng = eng
        op.fn = fn
        op.inc = False
        op.sem = None
        op.val = 0
        op.dma = dma_key
        writes = list(writes) + [r for r in reads if r.name.startswith("bank") and r not in writes]
        deps = []
        for r in reads:
            if r.w is not None:
                deps.append(r.w)
        for w in writes:
            if w.w is not None:
                deps.append(w.w)
            deps.extend(w.r)
        op.deps = [d for d in deps
                   if not (d.eng == "pe" and eng == "pe" and d.dma is None and dma_key is None)]
        for r in reads:
            r.r.append(op)
        for w in writes:
            w.w = op
            w.r = []
        self.ops[eng].append(op)
        return op

    def g(self, eng, name, reads, writes, *args, **kw):
        return self._add(eng, lambda e: getattr(e, name)(*args, **kw), reads, writes)

    def mm(self, out, lhsT, rhs, start, stop, reads, writes):
        return self._add("pe", lambda e: e.matmul(out, lhsT=lhsT, rhs=rhs, start=start, stop=stop),
                         reads, writes)

    def tr(self, out, in_, ident, reads, writes):
        return self._add("pe", lambda e: e.transpose(out=out, in_=in_, identity=ident), reads, writes)

    def act(self, out, in_, func, reads, writes, eng="act", **kw):
        return self._add(eng, lambda e: e.activation(out=out, in_=in_, func=func, **kw), reads, writes)

    def dma(self, eng, out, in_, reads=(), writes=(), key=None):
        if key is None:
            key = (writes[0].name if writes else reads[0].name)
        return self._add(eng, lambda e: e.dma_start(out=out, in_=in_), reads, writes, dma_key=key)

    def barrier(self):
        deps = []
        for e in self.ENGS:
            for o in reversed(self.ops[e]):
                if o.fn is not None and o.dma is None:
                    deps.append(o)
                    break
        deps += getattr(self, "dma_pending", [])
        self.dma_pending = []
        for e in self.ENGS:
            op = Op()
            op.eng = e
            op.fn = None
            op.inc = False
            op.sem = None
            op.val = 0
            op.dma = None
            op.deps = list(deps)
            self.ops[e].append(op)

    def finalize(self):
        allops = [o for e in self.ENGS for o in self.ops[e]]
        for o in allops:
            for d in o.deps:
                d.inc = True
        fin = [r.w for r in self.final if r.w is not None]
        for o in fin:
            o.inc = True
        MAXV = 30000
        order = {}
        for e in self.ENGS:
            sem = None
            cnt = 0
            for o in self.ops[e]:
                if o.dma is not None:
                    continue
                if o.inc:
                    if sem is None or cnt >= MAXV:
                        sem = self.newsem("c" + e)
                        cnt = 0
                    cnt += 1
                    o.sem, o.val = sem, cnt
        for o in self.dma_order:
            if o.dma not in self.dma_sems:
                self.dma_sems[o.dma] = [self.newsem("d"), 0]
            ent = self.dma_sems[o.dma]
            if ent[1] + 16 > MAXV:
                ent[0] = self.newsem("d")
                ent[1] = 0
            ent[1] += 16
            o.sem, o.val = ent[0], ent[1]
            o.inc = True
        self.fin_ops = fin

    def emit(self):
        nc = self.nc

        def run(ename, eng):
            known = {}
            for o in self.ops[ename]:
                for d in o.deps:
                    k = id(d.sem)
                    if known.get(k, 0) < d.val:
                        eng.wait_ge(d.sem, d.val)
                        known[k] = d.val
                if o.fn is None:
                    continue
                ins = o.fn(eng)
                if o.inc:
                    ins.then_inc(o.sem, 16 if o.dma is not None else 1)
            if ename == "sp":
                for d in self.fin_ops:
                    k = id(d.sem)
                    if known.get(k, 0) < d.val:
                        eng.wait_ge(d.sem, d.val)
                        known[k] = d.val

        with nc.Block() as block:
            @block.tensor
            def _(e):
                run("pe", e)

            @block.scalar
            def _(e):
                run("act", e)

            @block.vector
            def _(e):
                run("dve", e)

            @block.gpsimd
            def _(e):
                run("pool", e)

            @block.sync
            def _(e):
                run("sp", e)


_orig_add = Prog._add


def _add_wrapped(self, eng, fn, reads, writes, dma_key=None):
    op = _orig_add(self, eng, fn, reads, writes, dma_key)
    if dma_key is not None:
        if not hasattr(self, "dma_order"):
            self.dma_order = []
        self.dma_order.append(op)
        if not hasattr(self, "dma_pending"):
            self.dma_pending = []
        self.dma_pending.append(op)
    return op


Prog._add = _add_wrapped


def _split3(a):
    a = np.asarray(a, np.float64)
    hi = a.astype(np.float32).astype(ml_dtypes.bfloat16)
    r1 = a - hi.astype(np.float64)
    mid = r1.astype(np.float32).astype(ml_dtypes.bfloat16)
    r2 = r1 - mid.astype(np.float64)
    lo = r2.astype(np.float32).astype(ml_dtypes.bfloat16)
    return hi, mid, lo


def _common_tables():
    bf = ml_dtypes.bfloat16
    t = {}
    pos = np.arange(S)
    pk = np.zeros((9, S), np.float32)
    pk[0:3] = 1.0
    pk[3:6] = 128.0 * (pos // 128)
    pk[6:9] = pos % 128
    t["posk_s"] = pk.astype(bf)
    n = np.arange(512)
    ce = 16 * n + 31
    pc = np.zeros((9, 512), np.float32)
    pc[0:3] = 1.0
    pc[3:6] = 128.0 * (ce // 128)
    pc[6:9] = ce % 128
    t["posk_c"] = pc.astype(bf)
    ng = (np.arange(4)[None, :, None] * 128 + np.arange(128)[:, None, None])
    jj = np.arange(128)[None, None, :]
    t["ov"] = ((ng >= 4 * jj - 1) & (ng <= 4 * jj + 3)).astype(np.float32).astype(bf)
    t["ind"] = ((np.arange(128)[:, None] % 64) == (np.arange(4096)[None, :] // 64)).astype(np.float32).astype(bf)
    t["identb"] = np.eye(128, dtype=np.float32).astype(bf)
    t["identf"] = np.eye(128, dtype=np.float32)
    t["trilT"] = (np.arange(128)[:, None] <= np.arange(128)[None, :]).astype(np.float32)
    return t


def _core_tables(c):
    bf = ml_dtypes.bfloat16
    t = {}
    h = np.arange(16)
    slopes = np.power(2.0, -8.0 * (h + 1) / 16.0)
    slopes = np.power(np.float32(2.0), (-8.0 * (h + 1).astype(np.float32) / 16)).astype(np.float32).astype(np.float64)
    s_hi, s_mid, s_lo = _split3(slopes)
    qb = np.zeros((NOWN, 9, 16, 128), bf)
    tq = np.arange(128)
    for i in range(NOWN):
        j = 4 * i + c
        tt = 128 * j + tq
        A = -slopes[:, None] * tt[None, :].astype(np.float64)
        a_hi, a_mid, a_lo = _split3(A)
        qb[i, 0], qb[i, 1], qb[i, 2] = a_hi, a_mid, a_lo
        for k, sv in enumerate((s_hi, s_mid, s_lo)):
            qb[i, 3 + k] = np.broadcast_to(sv[:, None], (16, 128))
            qb[i, 6 + k] = np.broadcast_to(sv[:, None], (16, 128))
    t["qb"] = qb.reshape(NOWN, 9, 2048)
    cm = np.zeros((NOWN, 128, 4, 128), np.float32)
    fb2 = np.zeros((NOWN, 128, 128), np.float32)
    caus = np.zeros((NOWN, 128, 128), np.float32)
    blk = np.arange(128)
    for i in range(NOWN):
        j = 4 * i + c
        tt = 128 * j + tq
        ng = np.arange(4)[None, :, None] * 128 + np.arange(128)[:, None, None]
        ok = (16 * ng + 31 <= tt[None, None, :]) & (ng <= 510)
        cm[i] = np.where(ok, 0.0, NEGM)
        cur = tt // 64
        forced = (blk[None, :] == 0) | (blk[None, :] == cur[:, None]) | (blk[None, :] == cur[:, None] - 1)
        cz = blk[None, :] * 64 <= tt[:, None]
        fb2[i] = np.where(cz, np.where(forced, 1000.0, 0.0), -1.0)
        caus[i] = cz.astype(np.float32)
    t["cm"] = cm.astype(bf)
    t["fb2"] = fb2
    t["caus"] = caus
    k = np.arange(128)
    tri = np.zeros((128, 4, 128), np.float32)
    for r in range(4):
        d = 128 * (c - r) + tq[None, :] - k[:, None]
        tri[:, r, :] = np.where(d >= 0, 0.0, NEGM)
    t["tri"] = tri.astype(bf)
    wm = np.zeros((128, 8, 128), np.float32)
    for r in range(8):
        d = 128 * (c + 4 - r) + tq[None, :] - k[:, None]
        wm[:, r, :] = np.where((d >= 0) & (d < 512), 0.0, NEGM)
    t["wm"] = wm.astype(bf)
    return t


def build(debug=()):
    nc = bass.Bass("TRN2", target_bir_lowering=False)
    bf = BF16

    def din(name, shape, dt=F32):
        return nc.dram_tensor(name, list(shape), dt, kind="ExternalInput").ap()

    def dscr(name, shape, dt):
        kind = "ExternalOutput" if name in debug else "Internal"
        return nc.dram_tensor(name, list(shape), dt, kind=kind).ap()

    xb = din("xb", [S, D])
    xo = din("xo", [2048, D])
    po = din("po", [2048, 256])
    norm_g = din("norm_g", [128, D])
    final_g = din("final_g", [128, D])
    w_in = din("w_in", [D, 10800])
    w_own = din("w_own", [D, 9264])
    pe_k = din("pe_k", [64, 32]); w1_k = din("w1_k", [64, 32, 64]); w2_k = din("w2_k", [64, 64])
    pe_v = din("pe_v", [64, 32]); w1_v = din("w1_v", [64, 32, 64]); w2_v = din("w2_v", [64, 64])
    ln_g = din("ln_g", [128, 1024]); ln_b = din("ln_b", [128, 1024])
    sgu_wT = din("sgu_wT", [128, 8, 128])
    sgu_bT = din("sgu_bT", [128, 8])
    w_up_a = din("w_up_a", [1024, D]); w_up_b = din("w_up_b", [1024, D])
    w_out = din("w_out", [D, D]); w_ple = din("w_ple", [256, D]); w_pg = din("w_pg", [D, D])
    t_posk_s = din("posk_s", [9, S], bf); t_posk_c = din("posk_c", [9, 512], bf)
    t_ov = din("ov", [128, 4, 128], bf); t_ind = din("ind", [128, 4096], bf)
    t_identb = din("identb", [128, 128], bf); t_identf = din("identf", [128, 128])
    t_trilT = din("trilT", [128, 128])
    t_qb = din("qb", [NOWN, 9, 2048], bf); t_cm = din("cm", [NOWN, 128, 4, 128], bf)
    t_fb2 = din("fb2", [NOWN, 128, 128]); t_caus = din("caus", [NOWN, 128, 128])
    t_tri = din("tri", [128, 4, 128], bf); t_wm = din("wm", [128, 8, 128], bf)

    out = nc.dram_tensor("out", [2048, D], F32, kind="ExternalOutput").ap()

    KT = dscr("KT", [8, 128, S + 128], bf)
    VS = dscr("VS", [2, 4, 128, NT, 65], bf)
    rKT = Res("KT"); rVS = Res("VS")
    rOUT = Res("OUT")

    with ExitStack() as top:
        P = Prog(nc, top)
        P.final.append(rOUT)
        for nm in debug:
            pass

        def sbt(st, name, shape, dt):
            return st.enter_context(nc.sbuf_tensor("s_" + name, list(shape), dt))

        banks = [top.enter_context(nc.psum_tensor(f"bank{k}", [128, 512], F32)) for k in range(8)]
        rbank = [Res(f"bank{k}") for k in range(8)]
        bstate = {"k": 0}

        def nb():
            k = bstate["k"]
            bstate["k"] = (k + 1) % 5
            return k

        def bankbf(k):
            return banks[k][:].bitcast(bf)

        identb = sbt(top, "identb", [128, 128], bf); r_identb = Res("identb")
        identf = sbt(top, "identf", [128, 128], F32); r_identf = Res("identf")
        P.dma("sp", identb[:], t_identb, writes=[r_identb])
        P.dma("sp", identf[:], t_identf, writes=[r_identf])
        kcT = sbt(top, "kcA", [128, 4, 512], bf); r_kcT = Res("kcA")
        vcA = sbt(top, "vcA", [128, 4, 4, 65], bf); r_vcA = Res("vcA")

        evac_rr = {"k": 0}

        def evac(out_ap, in_ap, reads, writes, eng=None, func=AF.Copy, **kw):
            if eng is None:
                eng = "act" if (evac_rr["k"] % 2 == 0) else "dve"
                evac_rr["k"] += 1
            if eng == "act" or func != AF.Copy or kw:
                return P.act(out_ap, in_ap, func, reads, writes, **kw)
            return P.g("dve", "tensor_copy", reads, writes, out=out_ap, in_=in_ap)

        def make_norm(st, gsrc, tag, nx=2, junk_=None, xn_=None):
            gt = sbt(st, "gt" + tag, [128, D], F32); r_gt = Res("gt" + tag)
            P.dma("sp", gt[:], gsrc, writes=[r_gt])
            xts = [sbt(st, f"xt{tag}{k}", [128, D], F32) for k in range(nx)]
            r_xts = [Res(f"xt{tag}{k}") for k in range(nx)]
            junk, r_junk = junk_ if junk_ else (sbt(st, "junk" + tag, [128, D], bf), Res("junk" + tag))
            xn, r_xn = xn_ if xn_ else (sbt(st, "xn" + tag, [128, D], bf), Res("xn" + tag))
            stat = sbt(st, "stat" + tag, [128, 4], F32); r_stat = Res("stat" + tag)
            return dict(gt=gt, r_gt=r_gt, xts=xts, r_xts=r_xts, junk=junk, r_junk=r_junk, xn=xn,
                        r_xn=r_xn, stat=stat, r_stat=r_stat)

        def norm_load(N, k, src):
            P.dma("sp", N["xts"][k][:], src, writes=[N["r_xts"][k]])

        def norm_compute(N, k, dst_fn, r_dst):
            norm_stats(N, k)
            norm_tr(N, dst_fn, r_dst)

        def norm_stats(N, k):
            xt = N["xts"][k]; rxt = N["r_xts"][k]
            stat = N["stat"]; rs = N["r_stat"]
            P.act(N["junk"][:], xt[:], AF.Square, [rxt], [N["r_junk"], rs], accum_out=stat[:, 0:1])
            P.act(stat[:, 1:2], stat[:, 0:1], AF.Ln, [rs], [rs], scale=1.0 / D, bias=EPS)
            P.act(stat[:, 2:3], stat[:, 1:2], AF.Exp, [rs], [rs], scale=-0.5)
            P.g("dve", "scalar_tensor_tensor", [rxt, rs, N["r_gt"]], [N["r_xn"]], out=N["xn"][:], in0=xt[:],
                scalar=stat[:, 2:3], in1=N["gt"][:], op0=ALU.mult, op1=ALU.mult)

        def norm_tr(N, dst_fn, r_dst):
            for half in range(2):
                b = nb()
                bb = bankbf(b)
                for cc in range(8):
                    ck = half * 8 + cc
                    P.tr(bb[:, cc * 128:(cc + 1) * 128], N["xn"][:, ck * 128:(ck + 1) * 128], identb[:],
                         [N["r_xn"], r_identb], [rbank[b]])
                evac(dst_fn(half), bb.rearrange("p (c t) -> p c t", c=8), [rbank[b]], [r_dst])

        WT = {}
        r_WT = Res("WTscr")
        wlist = [("q", w_own, 0, 1024, 16), ("gl", w_own, 1024, 48, 16), ("za", w_own, 1072, 1024, 16),
                 ("uv", w_own, 2096, 2048, 16), ("zb", w_own, 4144, 1024, 16), ("mg", w_own, 5168, 4096, 16),
                 ("upa", w_up_a, 0, 2048, 8), ("upb", w_up_b, 0, 2048, 8), ("out", w_out, 0, 2048, 16),
                 ("pg", w_pg, 0, 2048, 16), ("ple", w_ple, 0, 2048, 2)]
        def conv_units(st):
            cst = [sbt(st, f"cst{k}", [128, 2048], F32) for k in range(4)]; r_cst = [Res(f"cst{k}") for k in range(4)]
            cbf = [sbt(st, f"cbf{k}", [128, 2048], bf) for k in range(4)]; r_cbf = [Res(f"cbf{k}") for k in range(4)]
            cc_ = 0
            for (nm, src, col0, ncol, KC) in wlist:
                scr = WT[nm]
                for kc in range(KC):
                    for c0 in range(0, ncol, 2048):
                        w_ = min(2048, ncol - c0)
                        k_ = cc_ % 4
                        cc_ += 1
                        P.dma("sp", cst[k_][:, 0:w_], src[kc * 128:(kc + 1) * 128, col0 + c0:col0 + c0 + w_], writes=[r_cst[k_]])
                        ceng = "pool" if cc_ % 3 == 0 else "dve"
                        P.g(ceng, "tensor_copy", [r_cst[k_]], [r_cbf[k_]], out=cbf[k_][:, 0:w_], in_=cst[k_][:, 0:w_])

                        def store(scr=scr, c0=c0, w_=w_, kc=kc, k_=k_):
                            if w_ % CW == 0:
                                P.dma("sp", scr[c0 // CW:(c0 + w_) // CW, :, kc, :].rearrange("c p w -> p c w"),
                                      cbf[k_][:, 0:w_].rearrange("p (c w) -> p c w", w=CW), reads=[r_cbf[k_]], writes=[r_WT], key="WTw")
                            else:
                                P.dma("sp", scr[c0 // CW, :, kc, 0:w_], cbf[k_][:, 0:w_], reads=[r_cbf[k_]], writes=[r_WT], key="WTw")
                        yield store

        for (nm, src, col0, ncol, KC) in wlist:
            WT[nm] = dscr("wb_" + nm, [(ncol + CW - 1) // CW, 128, KC, CW], bf)
        with ExitStack() as st:
            N = make_norm(st, norm_g, "a")
            wkv = sbt(st, "wkv", [128, 16, 1536], bf); r_wkv = Res("wkv")
            wst = [sbt(st, f"wsta{k}", [128, 1536], F32) for k in range(2)]
            r_wst = [Res(f"wsta{k}") for k in range(2)]
            for ck in range(16):
                P.dma("sp", wst[ck % 2][:], w_in[ck * 128:(ck + 1) * 128, 1024:2560], writes=[r_wst[ck % 2]])
                P.g("pool", "tensor_copy", [r_wst[ck % 2]], [r_wkv], out=wkv[:, ck, :], in_=wst[ck % 2][:])
            hT = [sbt(st, f"hTa{k}", [128, 16, 128], bf) for k in range(2)]
            r_hT = [Res(f"hTa{k}") for k in range(2)]
            kvtok = sbt(st, "kvtok", [128, 1024], bf); r_kvtok = Res("kvtok")
            KTst = [sbt(st, f"KTst{k}", [128, 8, 512], bf) for k in range(2)]
            r_KTst = [Res(f"KTst{k}") for k in range(2)]
            Vst = [sbt(st, f"Vst{k}", [128, 2, 4, 4, 65], bf) for k in range(2)]
            r_Vst = [Res(f"Vst{k}") for k in range(2)]
            for k in range(2):
                P.g("pool", "memset", [], [r_Vst[k]], Vst[k][:], 1.0)
            cgen = conv_units(st)
            cpend = []

            def conv_step(n):
                while cpend:
                    cpend.pop(0)()
                for _ in range(n):
                    try:
                        cpend.append(next(cgen))
                    except StopIteration:
                        break
            norm_load(N, 0, xb[0:128, :])
            kvst = []
            norm_load(N, 1, xb[128:256, :])
            norm_compute(N, 0, lambda half: hT[0][:, half * 8:(half + 1) * 8, :], r_hT[0])
            for T in range(NT):
                grp, tt = T // 4, T % 4
                sk = grp % 2
                if T + 2 < NT:
                    norm_load(N, T % 2, xb[(T + 2) * 128:(T + 3) * 128, :])
                while kvst:
                    kvst.pop(0)()
                hk = T % 2
                if T + 1 < NT:
                    norm_stats(N, (T + 1) % 2)
                conv_step(3)
                bks = [nb() for _ in range(3)]
                for cb in range(3):
                    for ck in range(16):
                        if 'm' in SKIP:
                            break
                        P.mm(banks[bks[cb]][:, :], hT[hk][:, ck, :], wkv[:, ck, cb * 512:(cb + 1) * 512],
                             ck == 0, ck == 15, [r_hT[hk], r_wkv], [rbank[bks[cb]]])
                if T + 1 < NT:
                    norm_tr(N, lambda half, hk2=(T + 1) % 2: hT[hk2][:, half * 8:(half + 1) * 8, :], r_hT[(T + 1) % 2])
                EV = os.environ.get('KDBG_EV', '12345')
                if '1' in EV:
                    evac(kvtok[:, 0:512], banks[bks[0]][:, :], [rbank[bks[0]]], [r_kvtok], eng="act")
                if '2' in EV:
                    evac(kvtok[:, 512:768], banks[bks[1]][:, 0:256], [rbank[bks[1]]], [r_kvtok], eng="dve")
                if '3' in EV:
                    evac(Vst[sk][:, 0, tt, :, 0:64], banks[bks[1]][:, 256:512].rearrange("p (g e) -> p g e", g=4),
                         [rbank[bks[1]]], [r_Vst[sk]], eng="dve")
                if '4' in EV:
                    evac(kvtok[:, 768:1024], banks[bks[2]][:, 0:256], [rbank[bks[2]]], [r_kvtok], eng="act")
                if '5' in EV:
                    evac(Vst[sk][:, 1, tt, :, 0:64], banks[bks[2]][:, 256:512].rearrange("p (g e) -> p g e", g=4),
                         [rbank[bks[2]]], [r_Vst[sk]], eng="dve")
                if 't' in SKIP:
                    continue
                b = nb()
                bb = bankbf(b)
                for m in range(8):
                    P.tr(bb[:, m * 128:(m + 1) * 128], kvtok[:, m * 128:(m + 1) * 128], identb[:],
                         [r_kvtok, r_identb], [rbank[b]])
                evac(KTst[sk][:, :, tt * 128:(tt + 1) * 128], bb.rearrange("p (m t) -> p m t", m=8),
                     [rbank[b]], [r_KTst[sk]])
                if tt == 3 and 's' not in SKIP:
                    def kv_store(grp=grp, sk=sk):
                        P.dma("sp", KT.rearrange("m p t -> p m t")[:, :, grp * 512:(grp + 1) * 512], KTst[sk][:],
                              reads=[r_KTst[sk]], writes=[rKT], key="KTw")
                        for y in range(2):
                            for gq in range(4):
                                P.dma("sp", VS[y, gq][:, grp * 4:(grp + 1) * 4, :],
                                      Vst[sk][:, y, :, gq, :], reads=[r_Vst[sk]], writes=[rVS], key="VSw")
                    kvst.append(kv_store)

            while kvst:
                kvst.pop(0)()
            for _ in range(400):
                conv_step(3)
            conv_step(0)

        P.barrier()
        with ExitStack() as st:
          if not os.environ.get('KDBG_NOCMP'):
              w1b = [sbt(st, f"w1b{y}", [128, 32, 128], bf) for y in range(2)]
              w2b = [sbt(st, f"w2b{y}", [128, 128], bf) for y in range(2)]
              peT = [sbt(st, f"peT{y}", [128, 32], bf) for y in range(2)]
              r_cw = Res("cmpw")
              for y, (w1, w2, pe) in enumerate(((w1_k, w2_k, pe_k), (w1_v, w2_v, pe_v))):
                  P.g("pool", "memset", [], [r_cw], w1b[y][:], 0.0)
                  P.g("pool", "memset", [], [r_cw], w2b[y][:], 0.0)
                  for hf in range(2):
                      ps_ = slice(hf * 64, hf * 64 + 64)
                      P.dma("pool", w1b[y][ps_, :, hf * 64:hf * 64 + 64], w1,
                            writes=[r_cw], key="cmpw")
                      P.dma("pool", w2b[y][ps_, hf * 64:hf * 64 + 64], w2, writes=[r_cw], key="cmpw")
                      P.dma("pool", peT[y][ps_, :], pe, writes=[r_cw], key="cmpw")
              cbias = sbt(st, "cbias", [128, 2], F32); r_cb = Res("cbias")
              for y in range(2):
                  b = nb()
                  for l in range(32):
                      P.mm(banks[b][:, 0:1], w1b[y][:, l, :], peT[y][:, l:l + 1], l == 0, l == 31, [r_cw], [rbank[b]])
                  evac(cbias[:, y:y + 1], banks[b][:, 0:1], [rbank[b]], [r_cb], eng="dve")
              P.g("pool", "memset", [], [r_kcT], kcT[:], 0.0)
              for g_ in range(4):
                  pp = slice(64, 73) if g_ % 2 == 0 else slice(0, 9)
                  P.dma("sp", kcT[pp, g_, :], t_posk_c, writes=[r_kcT], key="kcApos")
              P.g("pool", "memset", [], [r_vcA], vcA[:], 0.0)
              P.g("pool", "memset", [], [r_vcA], vcA[:, :, :, 64:65], 1.0)
              kin = [sbt(st, f"kin{k}", [128, S + 128], bf) for k in range(2)]
              r_kin = [Res(f"kin{k}") for k in range(2)]
              hid = sbt(st, "hid", [128, 128], bf); r_hid = Res("hid")
              cnt = 0
              for y in range(2):
                  for pr in range(2):
                      kb = cnt % 2
                      cnt += 1
                      P.dma("sp", kin[kb][:, 0:S], KT[y * 2 + pr][:, 0:S], reads=[rKT], writes=[r_kin[kb]])
                      for nt in range(4):
                          nn = 128 if nt < 3 else 127
                          b = nb()
                          for l in range(32):
                              base = 16 * 128 * nt + l
                              rhs = kin[kb][:, base:base + 16 * (nn - 1) + 1:16]
                              P.mm(banks[b][:, 0:nn], w1b[y][:, l, :], rhs, l == 0, l == 31,
                                   [r_cw, r_kin[kb]], [rbank[b]])
                          if nn < 128:
                              P.g("dve", "memset", [], [r_hid], hid[:], 0.0)
                          P.act(hid[:, 0:nn], banks[b][:, 0:nn], AF.Silu, [rbank[b], r_cb], [r_hid],
                                bias=cbias[:, y:y + 1])
                          b2 = nb()
                          if y == 0:
                              P.mm(banks[b2][:, 0:128], w2b[0][:], hid[:], True, True, [r_cw, r_hid], [rbank[b2]])
                              evac(kcT[0:64, 2 * pr, nt * 128:(nt + 1) * 128], banks[b2][0:64, 0:128], [rbank[b2]], [r_kcT],
                                   eng="dve")
                              evac(kcT[64:128, 2 * pr + 1, nt * 128:(nt + 1) * 128], banks[b2][64:128, 0:128], [rbank[b2]], [r_kcT],
                                   eng="dve")
                          else:
                              P.mm(banks[b2][:, 0:128], hid[:], w2b[1][:], True, True, [r_cw, r_hid], [rbank[b2]])
                              evac(vcA[:, nt, pr * 2:pr * 2 + 2, 0:64],
                                   banks[b2][:, 0:128].rearrange("p (g e) -> p g e", g=2), [rbank[b2]], [r_vcA],
                                   eng="dve")
              if "dbg_kc" in debug:
                  dk = nc.dram_tensor("dbg_kc", [128, 4, 512], bf, kind="ExternalOutput").ap()
                  dv = nc.dram_tensor("dbg_vc", [128, 4, 4, 65], bf, kind="ExternalOutput").ap()
                  rd = Res("dbgkc")
                  P.dma("sp", dk, kcT[:], reads=[r_kcT], writes=[rd], key="dbg")
                  P.dma("sp", dv, vcA[:], reads=[r_vcA], writes=[rd], key="dbg")
                  P.final.append(rd)

        if "stop1" in debug:
            P.final.extend([rKT, rVS])
            P.finalize()
            P.emit()
            return nc


        P.barrier()
        ACTS = dscr("ACTS", [2048, 9216], bf)
        GATES = dscr("GATES", [2048, 48], F32)
        rACTS = Res("ACTS"); rGATES = Res("GATES")
        acol = {"q": 0, "za": 1024, "uv": 2048, "zb": 4096, "mg": 5120}
        nblk = int(os.environ.get("KDBG_NB", NOWN))
        with ExitStack() as st:
            N = make_norm(st, norm_g, "o", nx=2)
            hTo = sbt(st, "hTo", [128, 16, 2048], bf); r_hTo = Res("hTo")
            norm_load(N, 0, xo[0:128, :])
            for i in range(nblk):
                if i + 1 < nblk:
                    norm_load(N, (i + 1) % 2, xo[(i + 1) * 128:(i + 2) * 128, :])
                norm_compute(N, i % 2, lambda half, i=i: hTo[:, half * 8:(half + 1) * 8, i * 128:(i + 1) * 128], r_hTo)
            wbufa = [sbt(st, f"wbufa{k}", [128, 16, CW], bf) for k in range(2)]; r_wbufa = [Res(f"wbufa{k}") for k in range(2)]
            stg = [sbt(st, f"stg{k}", [128, CW], bf) for k in range(4)]; r_stg = [Res(f"stg{k}") for k in range(4)]
            gstg = [sbt(st, f"gstg{k}", [128, 48], F32) for k in range(2)]; r_gstg = [Res(f"gstg{k}") for k in range(2)]
            ca = {"w": 0, "s": 0}
            funcs = {"q": AF.Copy, "gl": AF.Sigmoid, "za": AF.Silu, "uv": AF.Gelu, "zb": AF.Silu, "mg": AF.Sigmoid}
            for (nm, wd) in (("gl", 48), ("q", 1024), ("za", 1024), ("uv", 2048), ("zb", 1024), ("mg", 4096)):
                for cc in range(0, wd, CW):
                    w_ = min(CW, wd - cc)
                    wk = ca["w"] % 2
                    ca["w"] += 1
                    P.dma("sp", wbufa[wk][:, :, 0:w_], WT[nm][cc // CW][:, :, 0:w_], reads=[r_WT], writes=[r_wbufa[wk]])
                    for i in range(nblk):
                        b = nb()
                        for ck in range(16):
                            P.mm(banks[b][:, 0:w_], hTo[:, ck, i * 128:(i + 1) * 128], wbufa[wk][:, ck, 0:w_], ck == 0, ck == 15,
                                 [r_hTo, r_wbufa[wk]], [rbank[b]])
                        if nm == "gl":
                            k_ = i % 2
                            P.act(gstg[k_][:], banks[b][:, 0:48], AF.Sigmoid, [rbank[b]], [r_gstg[k_]])
                            P.dma("sp", GATES[i * 128:(i + 1) * 128, :], gstg[k_][:], reads=[r_gstg[k_]], writes=[rGATES], key="GATESw")
                        else:
                            k_ = ca["s"] % 4
                            ca["s"] += 1
                            if nm == "q":
                                P.act(stg[k_][:, 0:w_], banks[b][:, 0:w_], AF.Copy, [rbank[b]], [r_stg[k_]], scale=0.125)
                            else:
                                P.act(stg[k_][:, 0:w_], banks[b][:, 0:w_], funcs[nm], [rbank[b]], [r_stg[k_]])
                            P.dma("sp", ACTS[i * 128:(i + 1) * 128, acol[nm] + cc:acol[nm] + cc + w_], stg[k_][:, 0:w_],
                                  reads=[r_stg[k_]], writes=[rACTS], key="ACTSw")

        OAB = dscr("OAB", [2048, 2048], bf); rOAB = Res("OAB")
        ssum = sbt(top, "ssum", [128, 64], F32); r_ssum = Res("ssum")
        P.barrier()
        with ExitStack() as st:
            b1 = sbt(st, "b1", [128, 2048], bf); r_b1 = Res("b1")
            oab = sbt(st, "oab", [128, 2048], bf); r_oab = Res("oab")
            N = None
            lng = sbt(st, "lng", [128, 1024], F32); lnb = sbt(st, "lnb", [128, 1024], F32); r_ln = Res("ln")
            P.dma("sp", lng[:], ln_g, writes=[r_ln], key="lnc"); P.dma("sp", lnb[:], ln_b, writes=[r_ln], key="lnc")
            ind = sbt(st, "ind", [128, 4096], bf)
            ov = sbt(st, "ov", [128, 4, 128], bf); tri = sbt(st, "tri", [128, 4, 128], bf); wm = sbt(st, "wm", [128, 8, 128], bf)
            r_tab = Res("tab")
            for dst, src in ((ind, t_ind), (ov, t_ov), (tri, t_tri), (wm, t_wm)):
                P.dma("sp", dst[:], src, writes=[r_tab], key="tab")
            f1 = sbt(st, "f1", [128, 2048], F32); r_f1 = Res("f1")
            wsf = f1[:, 0:1024].rearrange("p (a b) -> p a b", a=8); trl = sbt(st, "trl", [128, 128], F32)
            wsb = sbt(st, "wsb", [128, 8, 128], bf); bsT = sbt(st, "bsT", [128, 8], F32); r_sg = Res("sgw")
            P.dma("sp", wsf, sgu_wT, writes=[r_sg, r_f1], key="sgw"); P.dma("sp", trl[:], t_trilT, writes=[r_sg], key="sgw")
            P.dma("sp", bsT[:], sgu_bT, writes=[r_sg], key="sgw")
            P.g("dve", "tensor_tensor", [r_sg, r_f1], [r_sg], out=wsb[:], in0=wsf,
                in1=trl[:].unsqueeze(1).broadcast_to([128, 8, 128]), op=ALU.mult)

            hTb = sbt(st, "hTb", [128, 16, 128], bf); r_hTb = Res("hTb")
            wbuf = None; r_wbuf = None
            wctr = {"c": 0}

            def dense(lhsT_fn, r_l, KC, wname, c0, width, epi):
                for s0 in range(0, width, CW):
                    dense1(lhsT_fn, r_l, KC, wname, c0 + s0, min(CW, width - s0), lambda b, w, s0=s0: epi(b, w, s0))

            def dense1(lhsT_fn, r_l, KC, wname, c0, width, epi):
                wk = wctr["c"] % 2
                wctr["c"] += 1
                P.dma("sp", wbuf[wk][:, 0:KC, 0:width], WT[wname][c0 // CW][:, :, 0:width], reads=[r_WT], writes=[r_wbuf[wk]])
                b = nb()
                for ck in range(KC):
                    P.mm(banks[b][:, 0:width], lhsT_fn(ck), wbuf[wk][:, ck, 0:width], ck == 0, ck == KC - 1,
                         [r_l, r_wbuf[wk]], [rbank[b]])
                epi(b, width)

            ablk2 = [sbt(st, f"ablk{k}", [128, 5120], bf) for k in range(2)]; r_ablk2 = [Res(f"ablk{k}") for k in range(2)]
            gates2 = [sbt(st, f"gates{k}", [128, 48], F32) for k in range(2)]; r_gates2 = [Res(f"gates{k}") for k in range(2)]
            cur = {}
            QA = sbt(st, "QA", [128, 4, 512], bf); r_QA = Res("QA")
            P.g("pool", "memset", [], [r_QA], QA[:], 0.0)
            cmt = sbt(st, "cmt", [128, 4, 128], bf); r_cmt = Res("cmt")
            fb2 = sbt(st, "fb2", [128, 128], F32); caus = sbt(st, "caus", [128, 128], F32); r_fb = Res("fb")
            PT = [sbt(st, f"PT{k}", [128, 512], bf) for k in range(4)]; r_PT = [Res(f"PT{k}") for k in range(4)]
            PTc = sbt(st, "PTc", [128, 4, 512], bf); r_PTc = [Res(f"PTc{k}") for k in range(4)]
            Osb = [sbt(st, f"Osb{k}", [65, 512], F32) for k in range(2)]; r_Osb = [Res(f"Osb{k}") for k in range(2)]
            oatt = sbt(st, "oatt", [128, 1024], F32); r_oatt = Res("oatt")
            otmp = sbt(st, "otmp", [128, 256], F32); r_otmp = Res("otmp")
            sm = sbt(st, "sm", [128, 32], F32); r_sm = Res("sm")
            imp = sbt(st, "imp", [128, 128], F32); r_imp = Res("imp")
            sc2 = sbt(st, "sc2", [128, 128], F32); r_sc2 = Res("sc2")
            m8 = sbt(st, "m8", [128, 16], F32); r_m8 = Res("m8")
            mbq = sbt(st, "mbq", [128, 128], bf); r_mbq = Res("mbq")
            MBT = sbt(st, "MBT", [128, 2, 128], bf); r_MBT = Res("MBT")
            P.g("pool", "memset", [], [r_MBT], MBT[:], 0.0)
            Kbuf = [sbt(st, f"Kbuf{k}", [128, S], bf) for k in range(2)]; r_Kbuf = [Res(f"Kbuf{k}") for k in range(2)]
            r_Kpos = Res("Kpos")
            for k in range(2):
                P.g("pool", "memset", [], [r_Kpos, r_Kbuf[k]], Kbuf[k][:], 0.0)
            P.dma("sp", Kbuf[0][64:73, :], t_posk_s, writes=[r_Kpos], key="kpos"); P.dma("sp", Kbuf[1][0:9, :], t_posk_s, writes=[r_Kpos], key="kpos")
            Vbuf = [sbt(st, f"Vbuf{k}", [128, NT, 65], bf) for k in range(2)]; r_Vbuf = [Res(f"Vbuf{k}") for k in range(2)]
            r_Kwpos = [Res(f"Kwpos{k}") for k in range(2)]
            HOLD_KW = 1
            Kwb = [sbt(st, f"Kwb{k}", [128, 1024], bf) for k in range(2)]; r_Kwb = [Res(f"Kwb{k}") for k in range(2)]
            for k in range(2):
                P.g("pool", "memset", [], [r_Kwb[k], r_Kwpos[k]], Kwb[k][:], 0.0)
            Vwb = [sbt(st, f"Vwb{k}", [128, 8, 65], bf) for k in range(2)]; r_Vwb = [Res(f"Vwb{k}") for k in range(2)]
            f2 = sbt(st, "f2", [128, 2048], F32); r_f2 = Res("f2")
            oT = hTb; r_oT = r_hTb
            xT = oT; r_xT = r_oT
            ptk = sbt(st, "ptk", [128, 256], F32); pbk = sbt(st, "pbk", [128, 256], bf); r_pt = Res("ptk")
            pT = sbt(st, "pT", [128, 2, 128], bf); r_pT = Res("pT")
            ptc = {"k": 0}

            def transposeN(src, r_src, n, dst, r_dst):
                for h0 in range(0, n, 8):
                    m = min(8, n - h0)
                    b = nb()
                    bb = bankbf(b)
                    for cc in range(m):
                        P.tr(bb[:, cc * 128:(cc + 1) * 128], src[:, (h0 + cc) * 128:(h0 + cc + 1) * 128], identb[:],
                             [r_src, r_identb], [rbank[b]])
                    evac(dst[:, h0:h0 + m, :], bb[:, 0:m * 128].rearrange("p (c t) -> p c t", c=m), [rbank[b]], [r_dst])

            pending = []
            octr = {"k": 0}

            def flush():
                while pending:
                    pending.pop(0)()

            def branch(tiles, Qop, rq, ob, save_pt=False):
                nt_ = len(tiles)
                pts = {}

                def emitS(idx):
                    (kT, rk, extras, vap, rv) = tiles[idx]
                    sbk = nb()
                    Sap = banks[sbk][:, :].rearrange("p (a b) -> p a b", a=4)
                    P.mm(Sap, kT, Qop, True, len(extras) == 0, rk + rq, [rbank[sbk]])
                    for ei, (el, er, rr) in enumerate(extras):
                        P.mm(Sap, el, er, False, ei == len(extras) - 1, [r_tab, r_identb] + rr, [rbank[sbk]])
                    if save_pt:
                        pt_ap, rpt = PTc[:, idx, :], r_PTc[idx]
                    else:
                        k_ = ptc["k"] % 4
                        ptc["k"] += 1
                        pt_ap, rpt = PT[k_][:], r_PT[k_]
                    P.act(pt_ap, banks[sbk][:, :], AF.Exp, [rbank[sbk]], [rpt])
                    pts[idx] = (pt_ap, rpt)

                def emitPV(idx):
                    (kT, rk, extras, vap, rv) = tiles[idx]
                    pt_ap, rpt = pts[idx]
                    P.mm(banks[ob][0:65, :], vap, pt_ap, idx == 0, idx == nt_ - 1, [rv, rpt], [rbank[ob]])

                for idx in range(nt_):
                    emitS(idx)
                    if idx >= 2:
                        emitPV(idx - 2)
                    if idx == min(2, nt_ - 1):
                        flush()
                for idx in range(max(0, nt_ - 2), nt_):
                    emitPV(idx)

            def epi_evac(ob):
                k_ = octr["k"] % 2
                octr["k"] += 1
                evac(Osb[k_][:, :], banks[ob][0:65, :], [rbank[ob]], [r_Osb[k_]], eng="dve")
                return k_

            def epilogue(k_, br, g, first):
                for r in range(4):
                    P.tr(banks[7][:, r * 65:(r + 1) * 65], Osb[k_][0:65, r * 128:(r + 1) * 128], identf[0:65, 0:65],
                         [r_Osb[k_], r_identf], [rbank[7]])
                O3 = banks[7][:, 0:260].rearrange("p (a e) -> p a e", a=4)
                P.g("dve", "tensor_scalar", [rbank[7]], [r_sm], out=sm[:, 0:4], in0=O3[:, :, 64], scalar1=1e-30,
                    scalar2=None, op0=ALU.max)
                P.g("dve", "reciprocal", [r_sm], [r_sm], out=sm[:, 4:8], in_=sm[:, 0:4])
                P.g("dve", "tensor_tensor", [r_sm, cur["r_gates"]], [r_sm], out=sm[:, 8:12], in0=sm[:, 4:8],
                    in1=cur["gates"][:, br * 16 + 4 * g:br * 16 + 4 * g + 4], op=ALU.mult)
                dst = oatt[:, g * 256:(g + 1) * 256].rearrange("p (a e) -> p a e", a=4)
                wb_ = sm[:, 8:12].unsqueeze(2).broadcast_to([128, 4, 64])
                if first:
                    P.g("dve", "tensor_tensor", [rbank[7], r_sm], [r_oatt], out=dst, in0=O3[:, :, 0:64], in1=wb_, op=ALU.mult)
                else:
                    t3 = otmp[:, :].rearrange("p (a e) -> p a e", a=4)
                    P.g("dve", "tensor_tensor", [rbank[7], r_sm], [r_otmp], out=t3, in0=O3[:, :, 0:64], in1=wb_, op=ALU.mult)
                    P.g("dve", "tensor_tensor", [r_otmp, r_oatt], [r_oatt], out=dst, in0=dst, in1=t3, op=ALU.add)

            segs = [(0, 1024, "q"), (0, 48, "gl"), (0, 1024, "za"), (0, 2048, "uv"), (0, 1024, "zb"), (0, 4096, "mg")]
            nblk = int(os.environ.get("KDBG_NB", NOWN))
            for i in range(nblk):
                tok = slice(i * 128, (i + 1) * 128)
                xk = 0
                def load_acts(j):
                    P.dma("sp", ablk2[j % 2][:], ACTS[j * 128:(j + 1) * 128, 0:5120], reads=[rACTS], writes=[r_ablk2[j % 2]])
                    P.dma("sp", gates2[j % 2][:], GATES[j * 128:(j + 1) * 128, :], reads=[rGATES], writes=[r_gates2[j % 2]])
                if i == 0:
                    load_acts(0)
                ablk = ablk2[i % 2]; r_ablk = r_ablk2[i % 2]
                r_qtok = r_za = r_uvg = r_zb = r_ablk
                za = ablk[:, 1024:2048]; zb = ablk[:, 4096:5120]
                cur["gates"] = gates2[i % 2]; cur["r_gates"] = r_gates2[i % 2]
                qb3 = t_qb[i].rearrange("p (g c) -> p g c", g=4)
                for g_ in range(4):
                    pp = slice(64, 73) if g_ % 2 == 0 else slice(0, 9)
                    P.dma("sp", QA[pp, g_, :], qb3[:, g_, :], writes=[r_QA], key="QAqb")
                P.dma("sp", cmt[:], t_cm[i], writes=[r_cmt])
                P.dma("sp", fb2[:], t_fb2[i], writes=[r_fb], key="fb"); P.dma("sp", caus[:], t_caus[i], writes=[r_fb], key="fb")
                bq = nb()
                bbq = bankbf(bq)
                for cc in range(8):
                    P.tr(bbq[:, cc * 128:(cc + 1) * 128], ablk[:, cc * 128:(cc + 1) * 128], identb[:],
                         [r_qtok, r_identb], [rbank[bq]])
                for P2 in range(2):
                    for hf2 in range(2):
                        hs2 = slice(hf2 * 64, hf2 * 64 + 64)
                        evac(QA[hs2, 2 * P2 + hf2, :], bbq[hs2, P2 * 512:(P2 + 1) * 512], [rbank[bq]], [r_QA])
                if i + 1 < nblk:
                    load_acts(i + 1)
                ntc = (32 * i + 31) // 128 + 1
                L = (4 * i + 4) * 128
                T0 = max(0, 4 * i - 4)
                for g in range(4):
                    P_, hf = g // 2, g % 2
                    hs = slice(hf * 64, hf * 64 + 64)
                    kb = g % 2
                    po_ = slice(64, 73) if hf == 0 else slice(0, 9)
                    Qop = QA[:, g, :].rearrange("p (a b) -> p a b", a=4)
                    rq = [r_QA]
                    po_ = slice(64, 73) if hf == 0 else slice(0, 9)
                    P.dma("sp", Kbuf[kb][hs, 0:L], KT[4 + P_][hs, 0:L], reads=[rKT], writes=[r_Kbuf[kb]])
                    P.dma("sp", Vbuf[kb][:, 0:L // 128, :], VS[0, g][:, 0:L // 128, :], reads=[rVS], writes=[r_Vbuf[kb]])
                    P.dma("sp", Kwb[kb][hs, 0:L - T0 * 128], KT[6 + P_][hs, T0 * 128:L], reads=[rKT], writes=[r_Kwb[kb]])
                    P.dma("sp", Kwb[kb][po_, 0:L - T0 * 128], t_posk_s[:, T0 * 128:L], writes=[r_Kwpos[kb]])
                    P.dma("sp", Vwb[kb][:, 0:L // 128 - T0, :], VS[1, g][:, T0:L // 128, :], reads=[rVS], writes=[r_Vwb[kb]])
                    bc4 = lambda ap: ap.unsqueeze(1).broadcast_to([128, 4, 128])
                    tiles = []
                    for nt in range(ntc):
                        tiles.append((kcT[:, g, nt * 128:(nt + 1) * 128], [r_kcT],
                                      [(identb[:], bc4(cmt[:, nt, :]), [r_cmt])], vcA[:, nt, g, :], r_vcA))
                    obC = 5 + (octr["k"] % 2)
                    branch(tiles, Qop, rq, obC, save_pt=True)
                    kC = epi_evac(obC)
                    ub = nb()
                    for r in range(4):
                        for nt in range(ntc):
                            P.mm(banks[ub][:, r * 128:(r + 1) * 128], PTc[:, nt, r * 128:(r + 1) * 128], ov[:, nt, :],
                                 nt == 0, nt == ntc - 1, [r_PTc[nt], r_tab], [rbank[ub]])

                    def after_c(kC=kC, g=g, ub=ub):
                        epilogue(kC, 0, g, True)
                        for r in range(4):
                            if r == 0:
                                P.g("dve", "tensor_scalar", [rbank[ub], r_sm], [r_imp], out=imp[:], in0=banks[ub][:, 0:128],
                                    scalar1=sm[:, 4:5], scalar2=None, op0=ALU.mult)
                            else:
                                P.g("dve", "scalar_tensor_tensor", [rbank[ub], r_sm, r_imp], [r_imp], out=imp[:],
                                    in0=banks[ub][:, r * 128:(r + 1) * 128], scalar=sm[:, 4 + r:5 + r], in1=imp[:],
                                    op0=ALU.mult, op1=ALU.add)
                        P.g("dve", "tensor_tensor", [r_imp, r_fb], [r_imp], out=imp[:], in0=imp[:], in1=caus[:], op=ALU.mult)
                        P.g("dve", "tensor_tensor", [r_imp, r_fb], [r_imp], out=imp[:], in0=imp[:], in1=fb2[:], op=ALU.add)
                        P.g("dve", "max", [r_imp], [r_m8], out=m8[:, 0:8], in_=imp[:])
                        P.g("dve", "match_replace", [r_imp, r_m8], [r_sc2], out=sc2[:], in_to_replace=m8[:, 0:8],
                            in_values=imp[:], imm_value=-2.0)
                        P.g("dve", "max", [r_sc2], [r_m8], out=m8[:, 8:16], in_=sc2[:])
                        P.g("dve", "tensor_scalar", [r_m8], [r_m8], out=m8[:, 0:1], in0=m8[:, 15:16], scalar1=-0.5,
                            scalar2=None, op0=ALU.max)
                        P.g("dve", "tensor_scalar", [r_imp, r_m8], [r_sc2], out=sc2[:], in0=imp[:], scalar1=m8[:, 0:1],
                            scalar2=-NEGM, op0=ALU.is_ge, op1=ALU.mult)
                        P.g("dve", "tensor_scalar", [r_sc2], [r_mbq], out=mbq[:], in0=sc2[:], scalar1=NEGM, scalar2=None,
                            op0=ALU.add)
                    pending.append(after_c)
                    tiles = []
                    for rr_ in range(8):
                        T = 4 * i - 4 + rr_
                        if T < 0:
                            continue
                        tiles.append((Kwb[kb][:, (T - T0) * 128:(T - T0 + 1) * 128], [r_Kwb[kb], r_Kwpos[kb]],
                                      [(identb[:], bc4(wm[:, rr_, :]), [])],
                                      Vwb[kb][:, T - T0, :], r_Vwb[kb]))
                    obW = 5 + (octr["k"] % 2)
                    branch(tiles, Qop, rq, obW)
                    flush()
                    kW = epi_evac(obW)
                    mb_ = nb()
                    P.tr(bankbf(mb_)[:, 0:128], mbq[:], identb[:], [r_mbq, r_identb], [rbank[mb_]])
                    evac(MBT[0:64, 0, :], bankbf(mb_)[0:64, 0:128], [rbank[mb_]], [r_MBT], eng="dve")
                    evac(MBT[64:128, 1, :], bankbf(mb_)[64:128, 0:128], [rbank[mb_]], [r_MBT], eng="dve")
                    pending.append(lambda kW=kW, g=g: epilogue(kW, 2, g, False))
                    tiles = []
                    for kt in range(4 * i + 4):
                        ex = [(ind[:, (kt % 32) * 128:(kt % 32 + 1) * 128],
                               MBT[:, kt // 32, :].unsqueeze(1).broadcast_to([128, 4, 128]), [r_MBT])]
                        if kt >= 4 * i:
                            ex.append((identb[:], bc4(tri[:, kt - 4 * i, :]), []))
                        tiles.append((Kbuf[kb][:, kt * 128:(kt + 1) * 128], [r_Kbuf[kb], r_Kpos],
                                      ex, Vbuf[kb][:, kt, :], r_Vbuf[kb]))
                    obS = 5 + (octr["k"] % 2)
                    branch(tiles, Qop, rq, obS)
                    flush()
                    kS = epi_evac(obS)
                    pending.append(lambda kS=kS, g=g: epilogue(kS, 1, g, False))
                flush()
                P.g("dve", "tensor_tensor", [r_oatt, r_za], [r_oab], out=oab[:, 0:1024], in0=oatt[:], in1=za, op=ALU.mult)
                v_ = ablk[:, 3072:4096]
                P.act(f1[:, 0:1024], v_, AF.Copy, [r_uvg], [r_f1, r_sm], accum_out=sm[:, 16:17])
                P.act(f1[:, 1024:2048], v_, AF.Square, [r_uvg], [r_f1, r_sm], accum_out=sm[:, 17:18])
                P.g("dve", "tensor_scalar", [r_sm], [r_sm], out=sm[:, 18:19], in0=sm[:, 16:17], scalar1=1.0 / 1024,
                    scalar2=None, op0=ALU.mult)
                P.g("dve", "tensor_tensor", [r_sm], [r_sm], out=sm[:, 19:20], in0=sm[:, 18:19], in1=sm[:, 18:19], op=ALU.mult)
                P.g("dve", "scalar_tensor_tensor", [r_sm], [r_sm], out=sm[:, 20:21], in0=sm[:, 17:18], scalar=1.0 / 1024,
                    in1=sm[:, 19:20], op0=ALU.mult, op1=ALU.subtract)
                P.act(sm[:, 21:22], sm[:, 20:21], AF.Ln, [r_sm], [r_sm], bias=EPS)
                P.act(sm[:, 22:23], sm[:, 21:22], AF.Exp, [r_sm], [r_sm], scale=-0.5)
                P.g("dve", "tensor_scalar", [r_uvg, r_sm], [r_f1], out=f1[:, 0:1024], in0=v_, scalar1=sm[:, 18:19],
                    scalar2=sm[:, 22:23], op0=ALU.subtract, op1=ALU.mult)
                P.g("dve", "tensor_tensor", [r_f1, r_ln], [r_f1], out=f1[:, 0:1024], in0=f1[:, 0:1024], in1=lng[:], op=ALU.mult)
                P.g("dve", "tensor_tensor", [r_f1, r_ln], [r_b1], out=b1[:, 0:1024], in0=f1[:, 0:1024], in1=lnb[:], op=ALU.add)
                for half in range(2):
                    b = nb()
                    for gq in range(4):
                        G_ = half * 4 + gq
                        P.mm(banks[b][:, gq * 128:(gq + 1) * 128], wsb[:, G_, :], b1[:, G_ * 128:(G_ + 1) * 128], True, True,
                             [r_sg, r_b1], [rbank[b]])
                    P.g("dve", "tensor_tensor", [rbank[b], r_sg], [r_f2], out=f2[:, half * 512:(half + 1) * 512].rearrange("p (a e) -> p a e", a=4),
                        in0=banks[b][:, :].rearrange("p (a e) -> p a e", a=4),
                        in1=bsT[:, half * 4:(half + 1) * 4].unsqueeze(2).broadcast_to([128, 4, 128]), op=ALU.add)
                P.g("dve", "tensor_tensor", [r_f2, r_uvg], [r_f2], out=f2[:, 0:1024], in0=f2[:, 0:1024], in1=ablk[:, 2048:3072], op=ALU.mult)
                P.g("dve", "tensor_tensor", [r_f2, r_zb], [r_oab], out=oab[:, 1024:2048], in0=f2[:, 0:1024], in1=zb, op=ALU.mult)
                P.dma("sp", OAB[tok, :], oab[:], reads=[r_oab], writes=[rOAB], key="OABw")

        P.barrier()
        X1 = dscr("X1", [2048, D], F32); X2 = dscr("X2", [2048, D], F32)
        rX1 = Res("X1"); rX2 = Res("X2")
        with ExitStack() as st:
            TA = sbt(st, "TA", [128, 16, 2048], bf); r_TA = Res("TA")
            MA = sbt(st, "MA", [128, 16, 2048], bf); r_MA = Res("MA")
            big = [sbt(st, f"big{k}", [128, 16, 512], bf) for k in range(2)]; r_big = [Res(f"big{k}") for k in range(2)]
            wsm = [sbt(st, f"wsm{k}", [128, 8, 512], bf) for k in range(2)]; r_wsm = [Res(f"wsm{k}") for k in range(2)]
            plew = sbt(st, "plew", [128, 2, 512], bf); r_plew = Res("plew")
            tA = [sbt(st, f"tA{k}", [128, 512], F32) for k in range(2)]; r_tA = [Res(f"tA{k}") for k in range(2)]
            tB = [sbt(st, f"tB{k}", [128, 512], F32) for k in range(2)]; r_tB = [Res(f"tB{k}") for k in range(2)]
            tX = [sbt(st, f"tX{k}", [128, 512], F32) for k in range(2)]; r_tX = [Res(f"tX{k}") for k in range(2)]
            tY = [sbt(st, f"tY{k}", [128, 512], F32) for k in range(2)]; r_tY = [Res(f"tY{k}") for k in range(2)]
            pT3 = wsm[0][:].rearrange("p a b -> p (a b)").rearrange("p (c t) -> p c t", c=2); r_pT3 = r_wsm[0]
            ptk3 = sbt(st, "ptk3", [128, 256], F32); pbk3 = sbt(st, "pbk3", [128, 256], bf); r_p3 = Res("p3")
            jk3 = sbt(st, "jk3", [128, 512], bf); r_jk3 = Res("jk3")

            def tr_tiles(src_fn, r_src, i):
                for half in range(2):
                    b = nb()
                    bb = bankbf(b)
                    for c8 in range(8):
                        P.tr(bb[:, c8 * 128:(c8 + 1) * 128], src_fn(half * 8 + c8), identb[:], [r_src, r_identb], [rbank[b]])
                    evac(TA[:, half * 8:(half + 1) * 8, i * 128:(i + 1) * 128], bb.rearrange("p (c t) -> p c t", c=8),
                         [rbank[b]], [r_TA])

            for i in range(nblk):
                k = i % 2
                ldv = big[k][:, 0:4, :]
                P.dma("sp", ldv, OAB[i * 128:(i + 1) * 128, :].rearrange("p (a b) -> p a b", a=4), reads=[rOAB], writes=[r_big[k]])
                tr_tiles(lambda c, k=k: big[k][:, c // 4, (c % 4) * 128:(c % 4 + 1) * 128], r_big[k], i)
            acts3 = ACTS.rearrange("(t p) c -> p t c", p=128)
            for cc in range(4):
                P.dma("sp", big[0][:, 0:nblk, :], acts3[:, 0:nblk, 5120 + cc * 512:5120 + (cc + 1) * 512], reads=[rACTS], writes=[r_big[0]])
                P.dma("sp", big[1][:, 0:nblk, :], acts3[:, 0:nblk, 7168 + cc * 512:7168 + (cc + 1) * 512], reads=[rACTS], writes=[r_big[1]])
                P.dma("sp", wsm[0][:], WT["upa"][cc], reads=[r_WT], writes=[r_wsm[0]])
                P.dma("sp", wsm[1][:], WT["upb"][cc], reads=[r_WT], writes=[r_wsm[1]])
                for i in range(nblk):
                    k = i % 2
                    bA = nb()
                    for ck in range(8):
                        P.mm(banks[bA][:, :], TA[:, ck, i * 128:(i + 1) * 128], wsm[0][:, ck, :], ck == 0, ck == 7, [r_TA, r_wsm[0]], [rbank[bA]])
                    bB = nb()
                    for ck in range(8):
                        P.mm(banks[bB][:, :], TA[:, 8 + ck, i * 128:(i + 1) * 128], wsm[1][:, ck, :], ck == 0, ck == 7, [r_TA, r_wsm[1]], [rbank[bB]])
                    P.g("dve", "tensor_tensor", [rbank[bA], r_big[0]], [r_tA[k]], out=tA[k][:], in0=banks[bA][:, :], in1=big[0][:, i, :], op=ALU.mult)
                    P.g("dve", "tensor_tensor", [rbank[bB], r_big[1]], [r_tB[k]], out=tB[k][:], in0=banks[bB][:, :], in1=big[1][:, i, :], op=ALU.mult)
                    P.g("dve", "tensor_tensor", [r_tA[k], r_tB[k]], [r_MA], out=MA[:, i, cc * 512:(cc + 1) * 512], in0=tA[k][:], in1=tB[k][:], op=ALU.add)
            for i in range(nblk):
                tr_tiles(lambda c, i=i: MA[:, i, c * 128:(c + 1) * 128], r_MA, i)
            for cc in range(4):
                wk = cc % 2
                P.dma("sp", big[wk][:], WT["out"][cc], reads=[r_WT], writes=[r_big[wk]])
                for i in range(nblk):
                    k = i % 2
                    b = nb()
                    for ck in range(16):
                        P.mm(banks[b][:, :], TA[:, ck, i * 128:(i + 1) * 128], big[wk][:, ck, :], ck == 0, ck == 15, [r_TA, r_big[wk]], [rbank[b]])
                    P.dma("sp", tX[k][:], xo[i * 128:(i + 1) * 128, cc * 512:(cc + 1) * 512], writes=[r_tX[k]])
                    P.g("dve", "tensor_tensor", [rbank[b], r_tX[k]], [r_tY[k]], out=tY[k][:], in0=banks[b][:, :], in1=tX[k][:], op=ALU.add)
                    P.dma("sp", X1[i * 128:(i + 1) * 128, cc * 512:(cc + 1) * 512], tY[k][:], reads=[r_tY[k]], writes=[rX1], key="X1w")
                    P.act(MA[:, i, cc * 512:(cc + 1) * 512], tY[k][:], AF.Copy, [r_tY[k]], [r_MA])
            for i in range(nblk):
                tr_tiles(lambda c, i=i: MA[:, i, c * 128:(c + 1) * 128], r_MA, i)
            for i in range(nblk):
                P.dma("sp", ptk3[:], po[i * 128:(i + 1) * 128, :], writes=[r_p3])
                P.act(pbk3[:], ptk3[:], AF.Copy, [r_p3], [r_p3])
                b = nb()
                bb = bankbf(b)
                for c2 in range(2):
                    P.tr(bb[:, c2 * 128:(c2 + 1) * 128], pbk3[:, c2 * 128:(c2 + 1) * 128], identb[:], [r_p3, r_identb], [rbank[b]])
                evac(pT3[:, :, i * 128:(i + 1) * 128], bb[:, 0:256].rearrange("p (c t) -> p c t", c=2), [rbank[b]], [r_pT3])
            for cc in range(4):
                wk = cc % 2
                P.dma("sp", big[wk][:], WT["pg"][cc], reads=[r_WT], writes=[r_big[wk]])
                P.dma("sp", plew[:], WT["ple"][cc], reads=[r_WT], writes=[r_plew])
                for i in range(nblk):
                    k = i % 2
                    bG = nb()
                    for ck in range(16):
                        P.mm(banks[bG][:, :], TA[:, ck, i * 128:(i + 1) * 128], big[wk][:, ck, :], ck == 0, ck == 15, [r_TA, r_big[wk]], [rbank[bG]])
                    P.act(tA[k][:], banks[bG][:, :], AF.Sigmoid, [rbank[bG]], [r_tA[k]])
                    bP = nb()
                    for ck in range(2):
                        P.mm(banks[bP][:, :], pT3[:, ck, i * 128:(i + 1) * 128], plew[:, ck, :], ck == 0, ck == 1, [r_pT3, r_plew], [rbank[bP]])
                    P.dma("sp", tX[k][:], X1[i * 128:(i + 1) * 128, cc * 512:(cc + 1) * 512], reads=[rX1], writes=[r_tX[k]])
                    P.g("dve", "tensor_tensor", [rbank[bP], r_tA[k]], [r_tB[k]], out=tB[k][:], in0=banks[bP][:, :], in1=tA[k][:], op=ALU.mult)
                    P.g("dve", "tensor_tensor", [r_tB[k], r_tX[k]], [r_tY[k]], out=tY[k][:], in0=tB[k][:], in1=tX[k][:], op=ALU.add)
                    P.act(jk3[:], tY[k][:], AF.Square, [r_tY[k]], [r_jk3, r_ssum], accum_out=ssum[:, i * 4 + cc:i * 4 + cc + 1])
                    P.dma("sp", X2[i * 128:(i + 1) * 128, cc * 512:(cc + 1) * 512], tY[k][:], reads=[r_tY[k]], writes=[rX2], key="X2w")
        P.barrier()
        with ExitStack() as st:
            fgt = sbt(st, "fgt", [128, D], F32); r_fgt = Res("fgt")
            P.dma("sp", fgt[:], final_g, writes=[r_fgt])
            xt3 = [sbt(st, f"xt3{k}", [128, D], F32) for k in range(2)]; r_xt3 = [Res(f"xt3{k}") for k in range(2)]
            ot3 = [sbt(st, f"ot3{k}", [128, D], F32) for k in range(2)]; r_ot3 = [Res(f"ot3{k}") for k in range(2)]
            st3 = sbt(st, "st3", [128, 8], F32); r_st3 = Res("st3")
            for i in range(nblk):
                k = i % 2
                P.dma("sp", xt3[k][:], X2[i * 128:(i + 1) * 128, :], reads=[rX2], writes=[r_xt3[k]])
                P.g("dve", "tensor_tensor", [r_ssum], [r_st3], out=st3[:, 0:2], in0=ssum[:, i * 4:i * 4 + 2], in1=ssum[:, i * 4 + 2:i * 4 + 4], op=ALU.add)
                P.g("dve", "tensor_tensor", [r_st3], [r_st3], out=st3[:, 2:3], in0=st3[:, 0:1], in1=st3[:, 1:2], op=ALU.add)
                P.act(st3[:, 3:4], st3[:, 2:3], AF.Ln, [r_st3], [r_st3], scale=1.0 / D, bias=EPS)
                P.act(st3[:, 4:5], st3[:, 3:4], AF.Exp, [r_st3], [r_st3], scale=-0.5)
                P.g("dve", "scalar_tensor_tensor", [r_xt3[k], r_st3, r_fgt], [r_ot3[k]], out=ot3[k][:], in0=xt3[k][:], scalar=st3[:, 4:5],
                    in1=fgt[:], op0=ALU.mult, op1=ALU.mult)
                P.dma("sp", out[i * 128:(i + 1) * 128, :], ot3[k][:], reads=[r_ot3[k]], writes=[rOUT], key="outw")
        P.finalize()
        P.emit()
    return nc


def _make_w_own(w):
    qcols = []
    for P_ in range(2):
        for r in range(4):
            for hd in (8 * P_ + r, 8 * P_ + 4 + r):
                qcols.extend(range(hd * 64, hd * 64 + 64))
    return np.ascontiguousarray(np.concatenate([w[:, qcols], w[:, 2560:]], axis=1))


def _prep_inputs(inputs):
    f = lambda a: np.ascontiguousarray(np.asarray(a, dtype=np.float32))
    x = f(inputs["x"]); p = f(inputs["p"])[0]
    com = _common_tables()
    shared = {
        "norm_g": np.ascontiguousarray(np.broadcast_to(f(inputs["norm_g"])[0][None, :], (128, D))),
        "final_g": np.ascontiguousarray(np.broadcast_to(f(inputs["final_g"])[None, :], (128, D))),
        "w_in": f(inputs["w_in"])[0],
        "w_own": _make_w_own(f(inputs["w_in"])[0]),
        "pe_k": np.ascontiguousarray(f(inputs["cmp_pe_k"])[0].T), "w1_k": np.ascontiguousarray(f(inputs["cmp_w1_k"])[0].transpose(1, 0, 2)), "w2_k": f(inputs["cmp_w2_k"])[0],
        "pe_v": np.ascontiguousarray(f(inputs["cmp_pe_v"])[0].T), "w1_v": np.ascontiguousarray(f(inputs["cmp_w1_v"])[0].transpose(1, 0, 2)), "w2_v": f(inputs["cmp_w2_v"])[0],
        "ln_g": np.ascontiguousarray(np.broadcast_to(f(inputs["ln_v_g"])[0][None, :], (128, 1024))),
        "ln_b": np.ascontiguousarray(np.broadcast_to(f(inputs["ln_v_b"])[0][None, :], (128, 1024))),
        "sgu_wT": np.ascontiguousarray(f(inputs["sgu_w"])[0].transpose(2, 0, 1)),
        "sgu_bT": np.ascontiguousarray(f(inputs["sgu_b"])[0].T),
        "w_up_a": f(inputs["w_up_a"])[0], "w_up_b": f(inputs["w_up_b"])[0],
        "w_out": f(inputs["w_out"])[0], "w_ple": f(inputs["w_ple"])[0], "w_pg": f(inputs["w_ple_gate"])[0],
    }
    shared.update(com)
    in_maps = []
    for core in range(8):
        b, c = core // 4, core % 4
        m = dict(shared)
        m["xb"] = x[b]
        xr = x[b].reshape(16, 4, 128, D)[:, c].reshape(2048, D)
        m["xo"] = np.ascontiguousarray(xr)
        m["po"] = np.ascontiguousarray(p[b].reshape(16, 4, 128, 256)[:, c].reshape(2048, 256))
        m.update(_core_tables(c))
        in_maps.append(m)
    return in_maps


def kernel(**inputs):
    in_maps = _prep_inputs(inputs)
    nc = build()
    res = run_bass_kernel_spmd(nc, in_maps, core_ids=list(range(8)))
    outp = np.zeros((2, S, D), np.float32)
    o = outp.reshape(2, 16, 4, 128, D)
    for core in range(8):
        b, c = core // 4, core % 4
        o[b, :, c] = res.results[core]["out"].reshape(16, 128, D)
    return outp
```
